# Optimizing a Trainium2 kernel written in Bass

```python
import math
import jax, jax.numpy as jnp
from jax import lax
import numpy as np

D_MODEL = 1024
BATCH = 4
SEQ = 8192
DEPTH = 4
DEC_BATCH = 16
DEC_SEQ = 16
PAST_LEN = 1024

CHUNK = 64
D_S5 = 512
S5_GROUP = 16
S5_GROUPS = D_S5 // S5_GROUP
S5_STATE = 64
RET_HEADS = 4
RET_DK = 128
RET_DV = 128
D_RET = RET_HEADS * RET_DV
D_MIX = D_S5 + D_RET
D_IN = D_S5 + 2 * RET_HEADS * RET_DK + 2 * D_RET
N_MEM = 256
X_HEADS = 4
X_DH = D_MODEL // X_HEADS
D_FF = 2816
ROPE_BASE = 10000.0
EPS = 1e-6

kernel_name = "hybrid_s5_retention_streaming_step"

F32 = jnp.float32


def rms_norm(x, g):
    xf = x.astype(F32)
    y = xf * lax.rsqrt(jnp.mean(xf * xf, axis=-1, keepdims=True) + EPS)
    return y * g.astype(F32)


def rope(x, pos):
    half = x.shape[-1] // 2
    inv = ROPE_BASE ** (-jnp.arange(half, dtype=F32) / half)
    ang = pos.astype(F32)[:, None] * inv[None, :]
    cos = jnp.cos(ang)[None, :, None, :]
    sin = jnp.sin(ang)[None, :, None, :]
    x1, x2 = x[..., :half], x[..., half:]
    return jnp.concatenate([x1 * cos - x2 * sin, x1 * sin + x2 * cos], axis=-1)


def retention_log_decay():
    return jnp.log(1.0 - 2.0 ** (-5.0 - jnp.arange(RET_HEADS, dtype=F32)))


def retention(q, k, v, s0):
    B, L, H, dk = q.shape
    dv = v.shape[-1]
    C = CHUNK if L >= CHUNK else L
    nc = L // C
    lg = retention_log_decay()
    pos = jnp.arange(C, dtype=F32)
    q = q.reshape(B, nc, C, H, dk) * (dk ** -0.5)
    k = k.reshape(B, nc, C, H, dk)
    v = v.reshape(B, nc, C, H, dv)
    dmat = jnp.exp(lg[:, None, None] * jnp.abs(pos[:, None] - pos[None, :]))
    scores = jnp.einsum('bnihd,bnjhd->bnhij', q, k) * dmat
    inner = jnp.einsum('bnhij,bnjhe->bnihe', scores, v)
    k_dec = k * jnp.exp(lg[None, :] * (C - 1.0 - pos)[:, None])[:, :, None]
    kv = jnp.einsum('bnjhd,bnjhe->nbhde', k_dec, v)
    g_chunk = jnp.exp(lg * C)[None, :, None, None]
    if s0 is None:
        s0 = jnp.zeros((B, H, dk, dv), F32)

    def step(S, kv_c):
        return g_chunk * S + kv_c, S

    s_final, s_before = lax.scan(step, s0.astype(F32), kv)
    q_dec = q * jnp.exp(lg[None, :] * (pos + 1.0)[:, None])[:, :, None]
    cross = jnp.einsum('bnihd,nbhde->bnihe', q_dec, s_before)
    return (inner + cross).reshape(B, L, H, dv), s_final


def _complex_linear_combine(e1, e2):
    a1r, a1i, b1r, b1i = e1
    a2r, a2i, b2r, b2i = e2
    return (a2r * a1r - a2i * a1i,
            a2r * a1i + a2i * a1r,
            a2r * b1r - a2i * b1i + b2r,
            a2r * b1i + a2i * b1r + b2i)


def s5_ssm(u, lam_re, lam_im, log_step, b_re, b_im, c_re, c_im, d_skip, x0_re, x0_im):
    B, L, _ = u.shape
    uf = u.astype(F32).reshape(B, L, S5_GROUPS, S5_GROUP)
    lam_re = lam_re.astype(F32)
    lam_im = lam_im.astype(F32)
    dt = jnp.exp(log_step.astype(F32))[:, None]
    ar = lam_re * dt
    ai = lam_im * dt
    mag = jnp.exp(ar)
    abar_re = mag * jnp.cos(ai)
    abar_im = mag * jnp.sin(ai)
    den = lam_re * lam_re + lam_im * lam_im
    nr = abar_re - 1.0
    ni = abar_im
    f_re = (nr * lam_re + ni * lam_im) / den
    f_im = (ni * lam_re - nr * lam_im) / den
    b_re = b_re.astype(F32)
    b_im = b_im.astype(F32)
    bb_re = f_re[..., None] * b_re - f_im[..., None] * b_im
    bb_im = f_re[..., None] * b_im + f_im[..., None] * b_re
    bu_re = jnp.einsum('blgh,gph->blgp', uf, bb_re)
    bu_im = jnp.einsum('blgh,gph->blgp', uf, bb_im)
    a_re = jnp.broadcast_to(abar_re[None, None], (1, L, S5_GROUPS, S5_STATE))
    a_im = jnp.broadcast_to(abar_im[None, None], (1, L, S5_GROUPS, S5_STATE))
    _, _, xr, xi = lax.associative_scan(_complex_linear_combine, (a_re, a_im, bu_re, bu_im), axis=1)
    if x0_re is not None:
        t = jnp.arange(1, L + 1, dtype=F32)[:, None, None]
        pm = jnp.exp(t * ar)
        pr = pm * jnp.cos(t * ai)
        pi = pm * jnp.sin(t * ai)
        x0r = x0_re.astype(F32)[:, None]
        x0i = x0_im.astype(F32)[:, None]
        xr = xr + pr * x0r - pi * x0i
        xi = xi + pr * x0i + pi * x0r
    y = (jnp.einsum('blgp,ghp->blgh', xr, c_re.astype(F32))
         - jnp.einsum('blgp,ghp->blgh', xi, c_im.astype(F32))
         + d_skip.astype(F32) * uf)
    return y.reshape(B, L, D_S5), xr[:, -1], xi[:, -1]


def mixer_sublayer(x, pos0, s5_x0_re, s5_x0_im, ret_s0, p, i):
    B, L, _ = x.shape
    h = rms_norm(x, p['mix_norm_pre'][i]).astype(x.dtype)
    proj = (h @ p['w_in'][i]).astype(F32)
    o1 = D_S5
    o2 = o1 + RET_HEADS * RET_DK
    o3 = o2 + RET_HEADS * RET_DK
    o4 = o3 + D_RET
    u = proj[..., :o1]
    q = proj[..., o1:o2].reshape(B, L, RET_HEADS, RET_DK)
    k = proj[..., o2:o3].reshape(B, L, RET_HEADS, RET_DK)
    v = proj[..., o3:o4].reshape(B, L, RET_HEADS, RET_DV)
    g = proj[..., o4:]
    y, s5r, s5i = s5_ssm(u, p['s5_lambda_re'][i], p['s5_lambda_im'][i], p['s5_log_step'][i],
                         p['s5_b_re'][i], p['s5_b_im'][i], p['s5_c_re'][i], p['s5_c_im'][i],
                         p['s5_d'][i], s5_x0_re, s5_x0_im)
    z = jax.nn.gelu(y)
    z = z * jax.nn.sigmoid(z @ p['s5_w_glu'][i].astype(F32) + p['s5_b_glu'][i].astype(F32))
    s5_out = rms_norm(z, p['s5_out_norm'][i])
    pos = pos0 + jnp.arange(L)
    o, s_ret = retention(rope(q, pos), rope(k, pos), v, ret_s0)
    o = rms_norm(o, p['ret_out_norm'][i]).reshape(B, L, D_RET)
    ret_out = jax.nn.silu(g) * o
    mix = jnp.concatenate([s5_out, ret_out], axis=-1).astype(x.dtype) @ p['w_out'][i]
    x = x + rms_norm(mix, p['mix_norm_post'][i]).astype(x.dtype)
    return x, s5r, s5i, s_ret


def memory_kv(mem, g_mem, w_ck, w_cv):
    B = mem.shape[0]
    m = rms_norm(mem, g_mem).astype(mem.dtype)
    k = (m @ w_ck).reshape(B, N_MEM, X_HEADS, X_DH)
    v = (m @ w_cv).reshape(B, N_MEM, X_HEADS, X_DH)
    return k, v


def cross_sublayer(x, mem_k, mem_v, p, i):
    B, L, _ = x.shape
    h = rms_norm(x, p['xattn_norm_pre'][i]).astype(x.dtype)
    q = (h @ p['w_cq'][i]).reshape(B, L, X_HEADS, X_DH).astype(F32)
    s = jnp.einsum('blhd,bmhd->bhlm', q, mem_k.astype(F32)) * (X_DH ** -0.5)
    pr = jax.nn.softmax(s, axis=-1)
    o = jnp.einsum('bhlm,bmhd->blhd', pr, mem_v.astype(F32)).reshape(B, L, D_MODEL)
    out = o.astype(x.dtype) @ p['w_co'][i]
    return x + rms_norm(out, p['xattn_norm_post'][i]).astype(x.dtype)


def ffn_sublayer(x, p, i):
    h = rms_norm(x, p['ffn_norm_pre'][i]).astype(x.dtype)
    f = (jax.nn.silu(h @ p['w_gate'][i]) * (h @ p['w_up'][i])) @ p['w_down'][i]
    return x + rms_norm(f, p['ffn_norm_post'][i]).astype(x.dtype)


def trunk(x, pos0, mem_k, mem_v, s5_re0, s5_im0, ret0, p):
    s5r_all, s5i_all, ret_all = [], [], []
    for i in range(DEPTH):
        if ret0 is None:
            x, s5r, s5i, s_ret = mixer_sublayer(x, pos0, None, None, None, p, i)
        else:
            x, s5r, s5i, s_ret = mixer_sublayer(x, pos0, s5_re0[i], s5_im0[i], ret0[i], p, i)
        x = cross_sublayer(x, mem_k[i], mem_v[i], p, i)
        x = ffn_sublayer(x, p, i)
        s5r_all.append(s5r)
        s5i_all.append(s5i)
        ret_all.append(s_ret)
    return x, jnp.stack(s5r_all), jnp.stack(s5i_all), jnp.stack(ret_all)


def setup_inputs(seed: int = 0) -> dict:
    key = jax.random.key(seed)
    ks = iter(jax.random.split(key, 48))

    def nrm(shape, scale):
        return jax.random.normal(next(ks), shape, F32) * scale

    def gain(shape):
        return 1.0 + nrm(shape, 0.01)

    G, P = S5_GROUPS, S5_STATE
    return {
        'x_prompt': nrm((BATCH, SEQ, D_MODEL), 1.0),
        'x_sample': nrm((DEC_BATCH, DEC_SEQ, D_MODEL), 1.0),
        'mem_prompt': nrm((BATCH, N_MEM, D_MODEL), 1.0),
        'state_s5_re': nrm((DEPTH, DEC_BATCH, G, P), 0.1),
        'state_s5_im': nrm((DEPTH, DEC_BATCH, G, P), 0.1),
        'state_ret': nrm((DEPTH, DEC_BATCH, RET_HEADS, RET_DK, RET_DV), 1.0),
        'cache_mem_k': nrm((DEPTH, DEC_BATCH, N_MEM, X_HEADS, X_DH), 1.0),
        'cache_mem_v': nrm((DEPTH, DEC_BATCH, N_MEM, X_HEADS, X_DH), 1.0),
        'mix_norm_pre': gain((DEPTH, D_MODEL)),
        'mix_norm_post': gain((DEPTH, D_MODEL)),
        'w_in': nrm((DEPTH, D_MODEL, D_IN), D_MODEL ** -0.5),
        's5_lambda_re': -0.5 + nrm((DEPTH, G, P), 0.01),
        's5_lambda_im': jnp.pi * jnp.arange(P, dtype=F32) + nrm((DEPTH, G, P), 0.01),
        's5_log_step': jax.random.uniform(next(ks), (DEPTH, G), F32, math.log(1e-3), math.log(1e-1)),
        's5_b_re': nrm((DEPTH, G, P, S5_GROUP), (2 * S5_GROUP) ** -0.5),
        's5_b_im': nrm((DEPTH, G, P, S5_GROUP), (2 * S5_GROUP) ** -0.5),
        's5_c_re': nrm((DEPTH, G, S5_GROUP, P), (2 * P) ** -0.5),
        's5_c_im': nrm((DEPTH, G, S5_GROUP, P), (2 * P) ** -0.5),
        's5_d': nrm((DEPTH, G, S5_GROUP), 1.0),
        's5_w_glu': nrm((DEPTH, D_S5, D_S5), D_S5 ** -0.5),
        's5_b_glu': nrm((DEPTH, D_S5), 0.01),
        's5_out_norm': gain((DEPTH, D_S5)),
        'ret_out_norm': gain((DEPTH, RET_HEADS, RET_DV)),
        'w_out': nrm((DEPTH, D_MIX, D_MODEL), D_MIX ** -0.5),
        'xattn_norm_pre': gain((DEPTH, D_MODEL)),
        'xattn_norm_post': gain((DEPTH, D_MODEL)),
        'mem_norm': gain((DEPTH, D_MODEL)),
        'w_cq': nrm((DEPTH, D_MODEL, D_MODEL), D_MODEL ** -0.5),
        'w_ck': nrm((DEPTH, D_MODEL, D_MODEL), D_MODEL ** -0.5),
        'w_cv': nrm((DEPTH, D_MODEL, D_MODEL), D_MODEL ** -0.5),
        'w_co': nrm((DEPTH, D_MODEL, D_MODEL), D_MODEL ** -0.5),
        'ffn_norm_pre': gain((DEPTH, D_MODEL)),
        'ffn_norm_post': gain((DEPTH, D_MODEL)),
        'w_gate': nrm((DEPTH, D_MODEL, D_FF), D_MODEL ** -0.5),
        'w_up': nrm((DEPTH, D_MODEL, D_FF), D_MODEL ** -0.5),
        'w_down': nrm((DEPTH, D_FF, D_MODEL), D_FF ** -0.5),
    }


def reference(x_prompt, x_sample, mem_prompt, state_s5_re, state_s5_im, state_ret, cache_mem_k, cache_mem_v,
              mix_norm_pre, mix_norm_post, w_in, s5_lambda_re, s5_lambda_im, s5_log_step,
              s5_b_re, s5_b_im, s5_c_re, s5_c_im, s5_d, s5_w_glu, s5_b_glu, s5_out_norm, ret_out_norm,
              w_out, xattn_norm_pre, xattn_norm_post, mem_norm, w_cq, w_ck, w_cv, w_co,
              ffn_norm_pre, ffn_norm_post, w_gate, w_up, w_down):
    p = {
        'mix_norm_pre': mix_norm_pre, 'mix_norm_post': mix_norm_post, 'w_in': w_in,
        's5_lambda_re': s5_lambda_re, 's5_lambda_im': s5_lambda_im, 's5_log_step': s5_log_step,
        's5_b_re': s5_b_re, 's5_b_im': s5_b_im, 's5_c_re': s5_c_re, 's5_c_im': s5_c_im, 's5_d': s5_d,
        's5_w_glu': s5_w_glu, 's5_b_glu': s5_b_glu, 's5_out_norm': s5_out_norm,
        'ret_out_norm': ret_out_norm, 'w_out': w_out,
        'xattn_norm_pre': xattn_norm_pre, 'xattn_norm_post': xattn_norm_post,
        'w_cq': w_cq, 'w_co': w_co,
        'ffn_norm_pre': ffn_norm_pre, 'ffn_norm_post': ffn_norm_post,
        'w_gate': w_gate, 'w_up': w_up, 'w_down': w_down,
    }
    mk_list, mv_list = [], []
    for i in range(DEPTH):
        mk, mv = memory_kv(mem_prompt, mem_norm[i], w_ck[i], w_cv[i])
        mk_list.append(mk)
        mv_list.append(mv)
    new_mem_k_prompt = jnp.stack(mk_list)
    new_mem_v_prompt = jnp.stack(mv_list)
    y_prompt, new_s5_re_prompt, new_s5_im_prompt, new_ret_prompt = trunk(
        x_prompt, 0, new_mem_k_prompt, new_mem_v_prompt, None, None, None, p)
    y_sample, new_s5_re_sample, new_s5_im_sample, new_ret_sample = trunk(
        x_sample, PAST_LEN, cache_mem_k, cache_mem_v, state_s5_re, state_s5_im, state_ret, p)
    return (y_prompt, y_sample, new_s5_re_prompt, new_s5_im_prompt, new_ret_prompt,
            new_mem_k_prompt, new_mem_v_prompt, new_s5_re_sample, new_s5_im_sample, new_ret_sample)
```

```python
import math
import numpy as np
import ml_dtypes
import concourse.bass as bass
import concourse.mybir as mybir
from concourse.bass_utils import run_bass_kernel_spmd

F32 = mybir.dt.float32
BF16 = mybir.dt.bfloat16
AF = mybir.ActivationFunctionType
ALU = mybir.AluOpType
AX = mybir.AxisListType
DSZ = {F32: 4, BF16: 2}

D = 1024
DIN = 2560
DFF = 2816
NMEM = 256
NL = 4
EPS = 1e-6
SB_BASE = 16640
SB_END = 229376 - 64
ATOM = 256
NDMA = 32
NDMA_SP = 24
SAME_ENGINE_SMALL_ONLY = False
EMBED_WAIT = True


class Sched:
    def __init__(self, nc, sems):
        self.nc = nc
        self.names = ["PE", "ACT", "DVE", "POOL", "SP"] + [f"D{i}" for i in range(NDMA)]
        self.idx = {n: i for i, n in enumerate(self.names)}
        self.eng = {"PE": nc.tensor, "ACT": nc.scalar, "DVE": nc.vector, "POOL": nc.gpsimd, "SP": nc.sync}
        self.sem = sems
        ne = len(self.names)
        self.cnt = np.zeros(ne, np.int64)
        self.mult = np.array([1] * 5 + [16] * NDMA, np.int64)
        self.seen = np.zeros((5, ne), np.int64)
        self.snap = [[] for _ in range(ne)]
        self.n_sb = (SB_END + ATOM - 1) // ATOM
        self.n_ps = 8 * 2048 // ATOM
        self.n_atoms = self.n_sb + self.n_ps
        self.dram = {}
        cap = self.n_atoms + 256
        self.lw_eng = -np.ones(cap, np.int64)
        self.lw_cnt = np.zeros(cap, np.int64)
        self.rd = np.zeros((cap, ne), np.int64)
        self.dma_rr = 0
        self.dma_rr_sw = 0
        self.ninst = 0

    def dres(self, name):
        if name not in self.dram:
            self.dram[name] = self.n_atoms + len(self.dram)
        a = self.dram[name]
        return (a, a + 1)

    def rng(self, ap):
        if isinstance(ap, tuple):
            return ap
        t = ap.tensor
        cls = type(t).__name__
        if cls.startswith("DRam"):
            return None
        dsz = DSZ[ap.dtype]
        row = 1
        for s in list(t.shape)[1:]:
            row *= int(s)
        col0 = int(ap.offset) % row
        ext = 1
        for st, c in list(ap.ap)[1:]:
            ext += (int(c) - 1) * abs(int(st))
        lo = col0 * dsz
        hi = (col0 + ext) * dsz
        if cls.startswith("SB"):
            base = int(t.manual_sbuf_range[0])
            return ((base + lo) // ATOM, (base + hi + ATOM - 1) // ATOM)
        return (self.n_sb + (lo // 2048) * (2048 // ATOM), self.n_sb + ((hi + 2047) // 2048) * (2048 // ATOM))

    def _deps(self, reads, writes):
        deps = np.zeros(len(self.names), np.int64)
        for r in reads:
            a, b = r
            le = self.lw_eng[a:b]
            m = le >= 0
            if m.any():
                np.maximum.at(deps, le[m], self.lw_cnt[a:b][m])
        for w in writes:
            a, b = w
            le = self.lw_eng[a:b]
            m = le >= 0
            if m.any():
                np.maximum.at(deps, le[m], self.lw_cnt[a:b][m])
            np.maximum(deps, self.rd[a:b].max(axis=0), out=deps)
        return deps

    def _waits(self, e, deps, embed=False):
        ei = self.idx[e]
        know = self.seen[ei]
        need = [o for o in np.nonzero(deps > know)[0] if not (o == ei and e == "PE")]
        need.sort(key=lambda o: -int(deps[o]))
        todo = []
        for o in need:
            d = int(deps[o])
            if d <= know[o]:
                continue
            todo.append((o, d))
            sn = self.snap[o]
            if d - 1 < len(sn):
                np.maximum(know, sn[d - 1], out=know)
            know[o] = max(know[o], d)
        last = None
        if embed and EMBED_WAIT and todo:
            last = todo.pop()
        for o, d in todo:
            self.eng[e].wait_ge(self.sem[o], int(d * self.mult[o]))
        if last is not None:
            return (self.sem[last[0]], int(last[1] * self.mult[last[0]]))
        return None

    def _snapshot(self, issuer, ei_done):
        k = self.seen[self.idx[issuer]].copy()
        k[ei_done] = self.cnt[ei_done]
        self.snap[ei_done].append(k)

    def _small(self, ap):
        if isinstance(ap, tuple):
            return False
        n = 1
        for st, c in list(ap.ap)[1:]:
            n *= int(c)
        return n * DSZ[ap.dtype] <= 64

    def op(self, e, fn, reads, writes):
        sm_r = [self.rng(r) for r in reads if self._small(r)]
        sm_w = [self.rng(w) for w in writes if self._small(w)]
        reads = [x for x in (self.rng(r) for r in reads) if x is not None]
        writes = [x for x in (self.rng(w) for w in writes) if x is not None]
        deps = self._deps(reads, writes)
        ei = self.idx[e]
        if SAME_ENGINE_SMALL_ONLY and e != "PE":
            dsm = self._deps([x for x in sm_r if x is not None], [x for x in sm_w if x is not None])
            deps[ei] = dsm[ei]
        for a, b in reads:
            if a >= self.n_sb and a < self.n_atoms:
                extra = self.rd[a:b].max(axis=0).copy()
                extra[ei] = 0
                np.maximum(deps, extra, out=deps)
        emb = self._waits(e, deps, embed=True)
        inst = fn()
        if emb is not None:
            inst._wait_ge(emb[0], emb[1])
        self.cnt[ei] += 1
        inst.then_inc(self.sem[ei], 1)
        self._snapshot(e, ei)
        c = self.cnt[ei]
        for a, b in writes:
            self.lw_eng[a:b] = ei
            self.lw_cnt[a:b] = c
            self.rd[a:b, :] = 0
        for a, b in reads:
            self.rd[a:b, ei] = c
        self.ninst += 1

    def dma(self, q, out, in_, reads=(), writes=(), **kw):
        if q == "POOL":
            di = 5 + NDMA_SP + self.dma_rr_sw
            self.dma_rr_sw = (self.dma_rr_sw + 1) % (NDMA - NDMA_SP)
        else:
            di = 5 + self.dma_rr
            self.dma_rr = (self.dma_rr + 1) % NDMA_SP
        rs = [x for x in (self.rng(r) for r in [in_] + list(reads)) if x is not None]
        ws = [x for x in (self.rng(w) for w in [out] + list(writes)) if x is not None]
        deps = self._deps(rs, ws)
        deps[di] = max(deps[di], self.cnt[di])
        emb = self._waits(q, deps, embed=True)
        inst = self.eng[q].dma_start(out=out, in_=in_, **kw)
        if emb is not None:
            inst._wait_ge(emb[0], emb[1])
        self.cnt[di] += 1
        inst.then_inc(self.sem[di], 16)
        self._snapshot(q, di)
        c = self.cnt[di]
        for a, b in ws:
            self.lw_eng[a:b] = di
            self.lw_cnt[a:b] = c
            self.rd[a:b, :] = 0
        for a, b in rs:
            self.rd[a:b, di] = c
        self.ninst += 1

    def finish(self):
        deps = self.cnt.copy()
        self._waits("SP", deps)


class Arena:
    def __init__(self, nc):
        self.nc = nc
        self.off = SB_BASE
        self.n = 0
        self.peak = SB_BASE

    def alloc(self, shape, dtype, name="t"):
        nbytes = DSZ[dtype]
        for s in shape[1:]:
            nbytes *= s
        off = (self.off + 63) // 64 * 64
        assert off + nbytes <= SB_END, f"SBUF overflow allocating {name} {shape}: {off + nbytes}"
        self.n += 1
        t = self.nc.alloc_sbuf_tensor_at(f"{name}_{self.n}", list(shape), dtype, offset=off)
        self.off = off + nbytes
        self.peak = max(self.peak, self.off)
        return t

    def mark(self):
        return self.off

    def reset(self, m):
        self.off = m


class Ring:
    def __init__(self, bufs):
        self.bufs = bufs
        self.i = 0

    def next(self):
        b = self.bufs[self.i]
        self.i = (self.i + 1) % len(self.bufs)
        return b


class _Stop(Exception):
    pass


def build(T, with_sample=True, nl=NL, stop=None):
    def ck(name):
        if stop == name:
            raise _Stop()
    nc = bass.Bass("TRN2", target_bir_lowering=False)
    NT = 512
    assert T % NT == 0
    n_tiles = T // NT

    def din(name, shape, dt=F32):
        return nc.dram_tensor(name, list(shape), dt, kind="ExternalInput").ap()

    def dout(name, shape, dt=F32):
        return nc.dram_tensor(name, list(shape), dt, kind="ExternalOutput").ap()

    def dscr(name, shape, dt=BF16):
        return nc.dram_tensor(name, list(shape), dt).ap()

    xp = din("xp", [T, D]); xs = din("xs", [32, D]); mem = din("mem", [NMEM, D])
    s5re0 = din("s5re0", [NL, 2, 32, 64]); s5im0 = din("s5im0", [NL, 2, 32, 64])
    ret0 = din("ret0", [NL, 2, 4, 128, 128])
    cmk = din("cmk", [NL, 2, NMEM, D]); cmv = din("cmv", [NL, 2, NMEM, D])
    W = {}
    wshapes = dict(w_in=[NL, D, DIN], s5_w_glu=[NL, 512, 512], w_out=[NL, D, D], w_cq=[NL, D, D], w_ck=[NL, D, D],
                   w_cv=[NL, D, D], w_co=[NL, D, D], w_gate=[NL, D, DFF], w_up=[NL, D, DFF], w_down=[NL, DFF, D])
    for k, s in wshapes.items():
        W[k] = din(k, s)
    vshapes = dict(mix_norm_pre=[NL, D], mix_norm_post=[NL, D], s5_lambda_re=[NL, 32, 64], s5_lambda_im=[NL, 32, 64],
                   s5_log_step=[NL, 32], s5_b_re=[NL, 32, 64, 16], s5_b_im=[NL, 32, 64, 16], s5_c_re=[NL, 32, 16, 64],
                   s5_c_im=[NL, 32, 16, 64], s5_d=[NL, 32, 16], s5_b_glu=[NL, 512], s5_out_norm=[NL, 512],
                   ret_out_norm=[NL, 512], xattn_norm_pre=[NL, D], xattn_norm_post=[NL, D], mem_norm=[NL, D],
                   ffn_norm_pre=[NL, D], ffn_norm_post=[NL, D])
    for k, s in vshapes.items():
        W[k] = din(k, s)
    c_idb = din("c_idb", [128, 128], BF16); c_idf = din("c_idf", [128, 128]); c_maskM = din("c_maskM", [128, 128])
    c_misc = din("c_misc", [128, 8])
    c_ropeP = din("c_ropeP", [T, 256]); c_ropeS = din("c_ropeS", [32, 256])
    c_dmP = din("c_dmP", [128, 512]); c_kdP = din("c_kdP", [128, 8]); c_qdP = din("c_qdP", [1, 1024])
    c_dmS = din("c_dmS", [32, 128]); c_kdS = din("c_kdS", [32, 8]); c_qdS = din("c_qdS", [1, 256])
    c_gP = din("c_gP", [1, 512]); c_gS = din("c_gS", [1, 512])

    yp = dout("yp", [T, D]); ys = dout("ys", [32, D])
    o_s5re_p = dout("o_s5re_p", [NL, 32 * 64]); o_s5im_p = dout("o_s5im_p", [NL, 32 * 64])
    o_ret_p = dout("o_ret_p", [NL, 4, 128, 128])
    o_memk = dout("o_memk", [NL, NMEM, D]); o_memv = dout("o_memv", [NL, NMEM, D])
    o_s5re_s = dout("o_s5re_s", [NL, 2, 32 * 64]); o_s5im_s = dout("o_s5im_s", [NL, 2, 32 * 64])
    o_ret_s = dout("o_ret_s", [NL, 2, 4, 128, 128])

    WB = {k: dscr(k + "_b", s) for k, s in wshapes.items()}
    s5wV = dscr("s5wV", [NL, 128, 32 * 2 * 128]); s5wY = dscr("s5wY", [NL, 128, 32 * 3 * 128])
    rotw = dscr("rotw", [NL, 128, 16 * 2 * 64], F32)
    kvs = dscr("kvs", [NL, 128, 4096])

    ps = nc.alloc_psum_tensor("ps", [128, 8, 512], F32)
    sem_cm = [nc.semaphore(f"s{i}") for i in range(5 + NDMA)]
    sems = [s.__enter__() for s in sem_cm]
    S = Sched(nc, sems)
    A = Arena(nc)

    def mm(out, lhsT, rhs, start=True, stop=True):
        S.op("PE", lambda: nc.tensor.matmul(out, lhsT=lhsT, rhs=rhs, start=start, stop=stop),
             [lhsT, rhs] + ([] if start else [out]), [out])

    def tr(out, in_, ident):
        S.op("PE", lambda: nc.tensor.transpose(out, in_, ident), [in_, ident], [out])

    def act(out, in_, func, bias=None, scale=None, accum=None):
        kw = {}
        rd = [in_]
        wr = [out]
        if bias is not None:
            kw["bias"] = bias; rd.append(bias)
        if scale is not None:
            kw["scale"] = scale
            if not isinstance(scale, float):
                rd.append(scale)
        if accum is not None:
            kw["accum_out"] = accum; wr.append(accum)
        S.op("ACT", lambda: nc.scalar.activation(out=out, in_=in_, func=func, **kw), rd, wr)

    def E(e):
        return {"DVE": nc.vector, "POOL": nc.gpsimd, "ACT": nc.scalar}[e]

    def tt(e, out, in0, in1, op):
        S.op(e, lambda: E(e).tensor_tensor(out=out, in0=in0, in1=in1, op=op), [in0, in1], [out])

    def ts(e, out, in0, s1, s2, op0, op1=None):
        rd = [in0] + [s for s in (s1, s2) if s is not None and not isinstance(s, float)]
        if op1 is None:
            S.op(e, lambda: E(e).tensor_scalar(out=out, in0=in0, scalar1=s1, scalar2=None, op0=op0), rd, [out])
        else:
            S.op(e, lambda: E(e).tensor_scalar(out=out, in0=in0, scalar1=s1, scalar2=s2, op0=op0, op1=op1), rd, [out])

    def stt(out, in0, scalar, in1, op0, op1, accum=None):
        rd = [in0, in1] + ([] if isinstance(scalar, float) else [scalar])
        wr = [out] + ([accum] if accum is not None else [])
        kw = {"accum_out": accum} if accum is not None else {}
        S.op("DVE", lambda: nc.vector.scalar_tensor_tensor(out=out, in0=in0, scalar=scalar, in1=in1, op0=op0, op1=op1, **kw),
             rd, wr)

    def cp(e, out, in_):
        if e == "ACT":
            S.op(e, lambda: nc.scalar.copy(out=out, in_=in_), [in_], [out])
        else:
            S.op(e, lambda: E(e).tensor_copy(out=out, in_=in_), [in_], [out])

    def recip(out, in_):
        S.op("DVE", lambda: nc.vector.reciprocal(out=out, in_=in_), [in_], [out])

    def memset(e, ap, v):
        S.op(e, lambda: E(e).memset(ap, v), [], [ap])

    def dma(out, in_, q="SP", reads=(), writes=(), slow=False):
        kw = {"allow_slow_non_contiguous": True} if slow else {}
        S.dma(q, out, in_, reads=reads, writes=writes, **kw)

    bank_rr = [0]

    def bank(n=1):
        b = (bank_rr[0] + n - 1) // n * n
        if b + n > 8:
            b = 0
        bank_rr[0] = (b + n) % 8
        return b

    def psf(b, P=128, n=512, nb=1):
        if nb == 1:
            return ps[0:P, b, 0:n]
        return ps[0:P, b:b + nb, :].rearrange("p b n -> p (b n)")[:, 0:n]

    def psb(b, P=128):
        return ps[0:P, b, :].bitcast(BF16)

    evac_rr = [0]

    def evac(out, in_):
        evac_rr[0] ^= 1
        cp("ACT" if evac_rr[0] else "DVE", out, in_)

    idb = A.alloc([128, 128], BF16, "idb"); idf = A.alloc([128, 128], F32, "idf")
    maskM = A.alloc([128, 128], F32, "maskM"); misc = A.alloc([128, 8], F32, "misc")
    onesb = A.alloc([128, 128], BF16, "onesb")
    dmP = A.alloc([128, 512], F32, "dmP"); kdP = A.alloc([128, 8], F32, "kdP"); qdP = A.alloc([128, 1024], F32, "qdP")
    dmS = A.alloc([128, 128], F32, "dmS"); kdS = A.alloc([128, 8], F32, "kdS"); qdS = A.alloc([128, 256], F32, "qdS")
    gP = A.alloc([128, 512], F32, "gP"); gS = A.alloc([128, 512], F32, "gS")
    rope = A.alloc([128, 4, 256], F32, "rope")
    x_sb = A.alloc([128, 4, D], F32, "x")
    hT = A.alloc([128, 8, NT], BF16, "hT")
    featA = A.alloc([128, 8, NT], BF16, "featA")
    featB = A.alloc([128, 8, NT], BF16, "featB")
    Sst = A.alloc([128, NL, 512], F32, "Sst")
    Ssm = A.alloc([128, 2, 512], F32, "Ssm")
    Sbf = Ring([A.alloc([128, 512], BF16, "Sbf") for _ in range(3)])
    car = A.alloc([128, NL, 2, 16], F32, "car")
    cars = A.alloc([128, 2, 2, 16], F32, "cars")
    Rall = A.alloc([128, NL, 16], F32, "Rall")
    gring = Ring([A.alloc([128, D], F32, "g") for _ in range(2)])
    tmpf = Ring([A.alloc([128, D], F32, "tmpf") for _ in range(2)])
    junk = A.alloc([128, D], BF16, "junk")
    hbr = Ring([A.alloc([128, D], BF16, "hb") for _ in range(4)])
    ssr = Ring([A.alloc([128, 8], F32, "ss") for _ in range(4)])
    ring_mark = A.mark()
    NSLOT = 4
    wring = Ring([A.alloc([128, 8192], BF16, "wr") for _ in range(NSLOT)])
    phase_mark = A.mark()

    pm = lambda g2: misc[:, g2:g2 + 1]
    npm = lambda g2: misc[:, 2 + g2:3 + g2]
    eps_c = misc[:, 4:5]
    hpi_c = misc[:, 5:6]

    dma(idb[:], c_idb); dma(idf[:], c_idf); dma(maskM[:], c_maskM); dma(misc[:], c_misc)
    dma(dmP[:], c_dmP); dma(kdP[:], c_kdP); dma(qdP[:], c_qdP.partition_broadcast(128))
    dma(dmS[0:32, :], c_dmS); dma(kdS[0:32, :], c_kdS); dma(qdS[:], c_qdS.partition_broadcast(128))
    dma(gP[:], c_gP.partition_broadcast(128)); dma(gS[:], c_gS.partition_broadcast(128))
    memset("DVE", onesb[:], 1.0)
    memset("DVE", Sst[:], 0.0)
    memset("DVE", car[:], 0.0)

    for l in range(nl):
        for k in wshapes:
            S.dma("POOL", WB[k][l], W[k][l], writes=[S.dres(f"{k}{l}")])

    def wload(src, shape, res, q="SP"):
        slot = wring.next()
        n = 1
        for s in shape[1:]:
            n *= s
        v = slot[:, 0:n]
        if len(shape) == 3:
            v = v.rearrange("p (a b) -> p a b", b=shape[2])
        elif len(shape) == 4:
            v = v.rearrange("p (a b c) -> p a b c", b=shape[2], c=shape[3])
        dma(v, src, q=q, reads=[S.dres(r) for r in res])
        return v

    def gload(row):
        g = gring.next()
        n = row.shape[-1]
        dma(g[:, 0:n], row.partition_broadcast(128))
        return g

    def s5_setup(l):
        A.reset(ring_mark)
        al = lambda shape, dt=F32, nm="s": A.alloc(shape, dt, nm)
        LR = al([128, 16]); LI = al([128, 16]); LS = al([128, 16])
        lam_v = lambda a: a[l].rearrange("(j g2) p -> (g2 p) j", g2=2)
        dma(LR[:], lam_v(W["s5_lambda_re"]), slow=True)
        dma(LI[:], lam_v(W["s5_lambda_im"]), slow=True)
        lsv = W["s5_log_step"][l].rearrange("(j g2) -> g2 j", g2=2)
        dma(LS[0:64, :], lsv[0:1, :].partition_broadcast(64), slow=True)
        dma(LS[64:128, :], lsv[1:2, :].partition_broadcast(64), slow=True)
        dt_ = al([128, 16]); ar = al([128, 16]); ai = al([128, 16]); mag = al([128, 16])
        act(dt_[:], LS[:], AF.Exp)
        tt("DVE", ar[:], LR[:], dt_[:], ALU.mult)
        tt("DVE", ai[:], LI[:], dt_[:], ALU.mult)
        act(mag[:], ar[:], AF.Exp)
        cs = al([128, 16]); sn = al([128, 16]); t1 = al([128, 16]); t2 = al([128, 16])
        act(sn[:], ai[:], AF.Sin, scale=1.0 / 16)
        act(cs[:], ai[:], AF.Sin, bias=hpi_c, scale=1.0 / 16)
        for _ in range(4):
            tt("DVE", t1[:], cs[:], cs[:], ALU.mult)
            tt("DVE", t2[:], sn[:], sn[:], ALU.mult)
            stt(sn[:], cs[:], 2.0, sn[:], ALU.mult, ALU.mult)
            tt("DVE", cs[:], t1[:], t2[:], ALU.subtract)
        apw = al([128, 9, 2, 16]); aiv = al([128, 9, 2, 16])
        memset("DVE", apw[:, 0, 0, :], 1.0); memset("DVE", apw[:, 0, 1, :], 0.0)
        tt("DVE", apw[:, 1, 0, :], mag[:], cs[:], ALU.mult)
        tt("DVE", apw[:, 1, 1, :], mag[:], sn[:], ALU.mult)

        def cmul(o_re, o_im, a_re, a_im, b_re, b_im, tA, tB):
            tt("DVE", tA, a_re, b_re, ALU.mult)
            tt("DVE", tB, a_im, b_im, ALU.mult)
            tt("DVE", o_re, tA, tB, ALU.subtract)
            tt("DVE", tA, a_re, b_im, ALU.mult)
            tt("DVE", tB, a_im, b_re, ALU.mult)
            tt("DVE", o_im, tA, tB, ALU.add)

        for n in range(2, 9):
            cmul(apw[:, n, 0, :], apw[:, n, 1, :], apw[:, n - 1, 0, :], apw[:, n - 1, 1, :],
                 apw[:, 1, 0, :], apw[:, 1, 1, :], t1[:], t2[:])
        m2 = al([128, 16]); rm2 = al([128, 16])
        tt("DVE", m2[:], mag[:], mag[:], ALU.mult)
        recip(rm2[:], m2[:])
        tt("DVE", aiv[:, 1, 0, :], apw[:, 1, 0, :], rm2[:], ALU.mult)
        stt(aiv[:, 1, 1, :], apw[:, 1, 1, :], -1.0, rm2[:], ALU.mult, ALU.mult)
        for n in range(2, 9):
            cmul(aiv[:, n, 0, :], aiv[:, n, 1, :], aiv[:, n - 1, 0, :], aiv[:, n - 1, 1, :],
                 aiv[:, 1, 0, :], aiv[:, 1, 1, :], t1[:], t2[:])
        nr = al([128, 16]); den = al([128, 16]); fre = al([128, 16]); fim = al([128, 16])
        ts("DVE", nr[:], apw[:, 1, 0, :], -1.0, None, ALU.add)
        tt("DVE", t1[:], LR[:], LR[:], ALU.mult)
        tt("DVE", t2[:], LI[:], LI[:], ALU.mult)
        tt("DVE", den[:], t1[:], t2[:], ALU.add)
        recip(den[:], den[:])
        tt("DVE", t1[:], nr[:], LR[:], ALU.mult)
        tt("DVE", t2[:], apw[:, 1, 1, :], LI[:], ALU.mult)
        tt("DVE", t1[:], t1[:], t2[:], ALU.add)
        tt("DVE", fre[:], t1[:], den[:], ALU.mult)
        tt("DVE", t1[:], apw[:, 1, 1, :], LR[:], ALU.mult)
        tt("DVE", t2[:], nr[:], LI[:], ALU.mult)
        tt("DVE", t1[:], t1[:], t2[:], ALU.subtract)
        tt("DVE", fim[:], t1[:], den[:], ALU.mult)
        R = Rall[:, l, :]
        tt("DVE", t1[:], m2[:], m2[:], ALU.mult)
        tt("DVE", R, t1[:], t1[:], ALU.mult)
        rR = al([128, 16])
        recip(rR[:], R)
        rot = al([128, 16, 2, 64])
        tt("DVE", rot[:, :, 0, 0], apw[:, 8, 0, :], rR[:], ALU.mult)
        tt("DVE", rot[:, :, 1, 0], apw[:, 8, 1, :], rR[:], ALU.mult)
        tr1 = al([128, 16, 32]); tr2 = al([128, 16, 32])
        for k in range(6):
            w = 1 << k
            bre = rot[:, :, 0, w - 1:w].to_broadcast([128, 16, w])
            bim = rot[:, :, 1, w - 1:w].to_broadcast([128, 16, w])
            cmul(rot[:, :, 0, w:2 * w], rot[:, :, 1, w:2 * w], rot[:, :, 0, 0:w], rot[:, :, 1, 0:w], bre, bim,
                 tr1[:, :, 0:w], tr2[:, :, 0:w])
        dma(rotw[l].rearrange("p (j r n) -> p j r n", r=2, n=64), rot[:], writes=[S.dres(f"rot{l}")])
        Bre = al([128, 16, 16]); Bim = al([128, 16, 16])
        bv = lambda a: a[l].rearrange("(j g2) p h -> (g2 p) j h", g2=2)
        dma(Bre[:], bv(W["s5_b_re"]), slow=True)
        dma(Bim[:], bv(W["s5_b_im"]), slow=True)
        Cre = al([128, 16, 16]); Cim = al([128, 16, 16])
        cin = al([128, 128])
        for (src, dst) in ((W["s5_c_re"], Cre), (W["s5_c_im"], Cim)):
            for half in range(2):
                cv = src[l].rearrange("(j g2) h p -> j h g2 p", g2=2)[half * 8:(half + 1) * 8]
                for jl in range(8):
                    dma(cin[jl * 16:(jl + 1) * 16, :].rearrange("h (g2 p) -> h g2 p", p=64), cv[jl], slow=True)
                b = bank()
                tr(psf(b, 128, 128), cin[:], idf[:])
                cp("DVE", dst[:, half * 8:(half + 1) * 8, :], psf(b, 128, 128).rearrange("p (j h) -> p j h", h=16))
        Bbr = al([128, 16, 16]); Bbi = al([128, 16, 16]); u1 = al([128, 16, 16]); u2 = al([128, 16, 16])
        bc = lambda a: a.unsqueeze(2).to_broadcast([128, 16, 16])
        cmul(Bbr[:], Bbi[:], bc(fre[:]), bc(fim[:]), Bre[:], Bim[:], u1[:], u2[:])
        CAr = al([128, 16, 8, 16]); CAi = al([128, 16, 8, 16])
        ABr = al([128, 16, 8, 16]); ABi = al([128, 16, 8, 16])
        Vr = al([128, 16, 8, 16]); Vi = al([128, 16, 8, 16])
        for t in range(8):
            cmul(CAr[:, :, t, :], CAi[:, :, t, :], bc(apw[:, t + 1, 0, :]), bc(apw[:, t + 1, 1, :]), Cre[:], Cim[:], u1[:], u2[:])
            cmul(ABr[:, :, t, :], ABi[:, :, t, :], bc(aiv[:, t + 1, 0, :]), bc(aiv[:, t + 1, 1, :]), Bbr[:], Bbi[:], u1[:], u2[:])
            cmul(Vr[:, :, t, :], Vi[:, :, t, :], bc(apw[:, 7 - t, 0, :]), bc(apw[:, 7 - t, 1, :]), Bbr[:], Bbi[:], u1[:], u2[:])
        dcol = al([128, 32])
        for s in range(8):
            dma(dcol[s * 16:(s + 1) * 16, :], W["s5_d"][l].rearrange("g h -> h g"), slow=True)
        fl = lambda a: a[:].rearrange("p j t h -> p j (t h)")
        SV = Ring([al([128, 8, 2, 128], BF16) for _ in range(2)])
        SY = Ring([al([128, 8, 3, 128], BF16) for _ in range(2)])
        mtmp = Ring([al([128, 128]) for _ in range(20)])
        for g2 in range(2):
            for q4 in range(4):
                sv = SV.next(); sy = SY.next()
                js = [q4 * 4 + jl for jl in range(4)]
                m0 = [mtmp.next() for _ in range(4)]; m1 = [mtmp.next() for _ in range(4)]
                mt = [mtmp.next() for _ in range(4)]; m2_ = [mtmp.next() for _ in range(4)]; m3_ = [mtmp.next() for _ in range(4)]
                for jl, j in enumerate(js):
                    act(m0[jl][:], fl(ABr)[:, j, :], AF.Copy, scale=pm(g2))
                    act(m1[jl][:], fl(ABi)[:, j, :], AF.Copy, scale=npm(g2))
                for jl, j in enumerate(js):
                    ts("DVE", m2_[jl][:], fl(Vr)[:, j, :], pm(g2), None, ALU.mult)
                    ts("DVE", m3_[jl][:], fl(Vi)[:, j, :], pm(g2), None, ALU.mult)
                bM = [bank() for _ in range(4)]
                for jl, j in enumerate(js):
                    mm(psf(bM[jl], 128, 128), m0[jl][:], fl(CAr)[:, j, :], True, False)
                    mm(psf(bM[jl], 128, 128), m1[jl][:], fl(CAi)[:, j, :], False, True)
                bV = [bank() for _ in range(4)]
                for jl, j in enumerate(js):
                    tr(psf(bV[jl], 128, 128), m2_[jl][:], idf[:])
                    tr(psf(bV[jl], 128, 256)[:, 128:256], m3_[jl][:], idf[:])
                for jl, j in enumerate(js):
                    g = 2 * j + g2
                    tt("DVE", mt[jl][:], psf(bM[jl], 128, 128), maskM[:], ALU.mult)
                    stt(sy[:, jl, 0, :], idf[:], dcol[:, g:g + 1], mt[jl][:], ALU.mult, ALU.add)
                for jl, j in enumerate(js):
                    act(sy[:, jl, 1, :], fl(CAr)[:, j, :], AF.Copy, scale=pm(g2))
                    act(sy[:, jl, 2, :], fl(CAi)[:, j, :], AF.Copy, scale=npm(g2))
                for jl, j in enumerate(js):
                    cp("ACT", sv[:, jl, :, :], psf(bV[jl], 128, 256).rearrange("p (r n) -> p r n", n=128))
                vV = s5wV[l].rearrange("p (j g2 r n) -> p j g2 r n", g2=2, r=2, n=128)[:, q4 * 4:q4 * 4 + 4, g2]
                vY = s5wY[l].rearrange("p (j g2 r n) -> p j g2 r n", g2=2, r=3, n=128)[:, q4 * 4:q4 * 4 + 4, g2]
                dma(vV, sv[:, 0:4], writes=[S.dres(f"s5w{l}")])
                dma(vY, sy[:, 0:4], writes=[S.dres(f"s5w{l}")])

    class TC:
        pass

    def norm_T(tc, gain_row, nsub=None, src=None, n_feat=D):
        P = tc.P
        g = gload(gain_row)
        nsub = tc.nsub if nsub is None else nsub
        src = x_sb if src is None else src
        sss = [ssr.next() for _ in range(nsub)]
        hbs = [hbr.next() for _ in range(nsub)]
        for i in range(nsub):
            act(junk[0:P, :], src[0:P, i, :], AF.Square, accum=sss[i][0:P, 0:1])
        for i in range(nsub):
            act(sss[i][0:P, 1:2], sss[i][0:P, 0:1], AF.Sqrt, bias=eps_c[0:P], scale=1.0 / n_feat)
        for i in range(nsub):
            recip(sss[i][0:P, 2:3], sss[i][0:P, 1:2])
        for i in range(nsub):
            stt(hbs[i][0:P, :], src[0:P, i, :], sss[i][0:P, 2:3], g[0:P, :], ALU.mult, ALU.mult)
        bs = [bank() for _ in range(nsub)]
        for i in range(nsub):
            pb = psb(bs[i])
            for k in range(8):
                tr(pb[:, k * P:(k + 1) * P], hbs[i][0:P, k * 128:(k + 1) * 128], idb[0:P, 0:P])
        for i in range(nsub):
            evac(hT[:, :, i * P:(i + 1) * P], psb(bs[i])[:, 0:8 * P].rearrange("p (k c) -> p k c", c=P))

    def rstd_of(ssum_ap, out_ap, P, n):
        S.op("ACT", lambda: nc.scalar.activation(out=out_ap, in_=ssum_ap, func=AF.Sqrt, bias=eps_c[0:P], scale=1.0 / n),
             [ssum_ap, eps_c[0:P]], [out_ap])
        recip(out_ap, out_ap)

    def postnorm_add(tc, i, pair, g):
        P = tc.P
        pv = psf(pair, P, 1024, nb=2)
        ss = ssr.next()
        act(junk[0:P, :], pv, AF.Square, accum=ss[0:P, 0:1])
        rstd_of(ss[0:P, 0:1], ss[0:P, 1:2], P, D)
        tf = tmpf.next()
        stt(tf[0:P, :], pv, ss[0:P, 1:2], g[0:P, :], ALU.mult, ALU.mult)
        tt("POOL", x_sb[0:P, i, :], x_sb[0:P, i, :], tf[0:P, :], ALU.add)

    def proj_out_tm(tc, l, wname, featT, gain_row):
        P = tc.P
        wv = wload(WB[wname][l].rearrange("(k p) n -> p k n", p=128), [128, 8, D], [f"{wname}{l}"])
        g = gload(gain_row)
        for i in range(tc.nsub):
            pair = bank(2)
            for n in range(2):
                for k in range(8):
                    mm(psf(pair + n, P), featT[:, k, i * P:(i + 1) * P], wv[:, k, n * 512:(n + 1) * 512], k == 0, k == 7)
            postnorm_add(tc, i, pair, g)

    A.reset(phase_mark)
    ucm_raw = A.alloc([128, 4096], BF16, "ucm")
    ucm = ucm_raw[:].rearrange("p (t f) -> p t f", f=512)
    ucmU = ucm_raw[:].rearrange("p (g s h) -> p g s h", s=8, h=16)
    ucmUf = ucm_raw[:].rearrange("p (g n) -> p g n", n=128)
    Uf = A.alloc([128, 32, 64], BF16, "Uf")
    bufA = A.alloc([128, 16, 2, 64], F32, "bufA")
    bufB = A.alloc([128, 16, 2, 64], F32, "bufB")
    bufT = A.alloc([128, 2, 16, 64], F32, "bufT")
    Xpb = A.alloc([128, 16, 2, 64], BF16, "Xpb")
    yfr = Ring([A.alloc([128, 8, 64], BF16, "yf") for _ in range(2)])
    zT = bufT[:].rearrange("p a b c -> p (a b c)").bitcast(BF16)[:, 0:2048].rearrange("p (k s c) -> p k s c", s=8, c=64)
    s5_mark_end = A.mark()
    A.reset(phase_mark)
    q_tm = A.alloc([128, 4, 512], BF16, "q_tm"); k_tm = A.alloc([128, 4, 512], BF16, "k_tm")
    v_tm = A.alloc([128, 4, 512], BF16, "v_tm"); sg_tm = A.alloc([128, 4, 512], BF16, "sg_tm")
    rtm = [A.alloc([128, 256], F32, "rtm") for _ in range(4)]
    qT_sb = A.alloc([128, 4, 128], BF16, "qT_sb"); kT_sb = A.alloc([128, 4, 128], BF16, "kT_sb")
    qdA = A.alloc([128, 4, 128], BF16, "qdA"); qdB = A.alloc([128, 4, 128], BF16, "qdB")
    kdA = A.alloc([128, 4, 128], BF16, "kdA"); kdB = A.alloc([128, 4, 128], BF16, "kdB")
    sc_sb = A.alloc([128, 4, 128], BF16, "sc_sb")
    o_f = A.alloc([128, 4, 128], F32, "o_f"); o_sq = A.alloc([128, 4, 128], F32, "o_sq")
    ret_tmr = Ring([A.alloc([128, 512], BF16, "ret_tm") for _ in range(2)])
    Stmp = A.alloc([128, 512], F32, "Stmp")
    ret_mark_end = A.mark()
    A.reset(phase_mark)
    pTr = Ring([A.alloc([128, 2, 512], BF16, "pT") for _ in range(2)])
    rinv = Ring([A.alloc([128, 512], F32, "rinv") for _ in range(2)])
    kb_x = A.alloc([128, 2, D], BF16, "kb_x")
    kvst = A.alloc([128, 4096], BF16, "kvst")
    kvf = Ring([A.alloc([128, D], F32, "kvf") for _ in range(2)])
    xat_mark_end = A.mark()
    A.reset(phase_mark)
    actT = A.alloc([128, 22, 512], BF16, "actT")
    sgr = Ring([A.alloc([128, 512], BF16, "sgate") for _ in range(2)])
    ffn_mark_end = A.mark()

    def s5_phase(tc, l):
        P, NTt, NC = tc.P, tc.NT, tc.NC
        wv = wload(WB["w_in"][l][:, 0:512].rearrange("(k p) n -> p k n", p=128), [128, 8, 512], [f"w_in{l}"])
        for s in range(8):
            b = bank()
            for k in range(8):
                mm(psf(b, NC), hT[:, k, s:NTt:8], wv[:, k, :], k == 0, k == 7)
            evac(ucmU[0:NC, :, s, :], psf(b, NC).rearrange("p (g h) -> p g h", h=16))
        for g0 in range(0, 32, 8):
            b = bank()
            pb = psb(b)
            for gl in range(8):
                g = g0 + gl
                tr(pb[:, gl * NC:(gl + 1) * NC], ucmUf[0:NC, g, :], idb[0:NC, 0:NC])
            evac(Uf[:, g0:g0 + 8, 0:NC], pb[:, 0:8 * NC].rearrange("p (g c) -> p g c", c=NC))
        vs = bufA
        for q4 in range(4):
            wV = wload(s5wV[l][:, q4 * 2048:(q4 + 1) * 2048].rearrange("p (g r n) -> p g r n", r=2, n=128),
                       [128, 8, 2, 128], [f"s5w{l}"])
            b = bank()
            pv = psf(b, 128, 8 * NC).rearrange("p (j r c) -> p j r c", r=2, c=NC)
            for jl in range(4):
                for ri in range(2):
                    for g2 in range(2):
                        gl = jl * 2 + g2
                        mm(pv[:, jl, ri, :], wV[:, gl, ri, :], Uf[:, q4 * 8 + gl, 0:NC], g2 == 0, g2 == 1)
            evac(vs[:, q4 * 4:q4 * 4 + 4, :, 0:NC], pv)
        rslot = wring.next()
        rt = rslot[:, 0:4096].bitcast(F32).rearrange("p (j r n) -> p j r n", r=2, n=64)
        for (c0, ln, t0) in tc.rot_segs:
            dma(rt[:, :, :, c0:c0 + ln], rotw[l].rearrange("p (j r n) -> p j r n", r=2, n=64)[:, :, :, t0:t0 + ln],
                reads=[S.dres(f"rot{l}")])
        cosv = rt[:, :, 0, 0:NC]; sinv = rt[:, :, 1, 0:NC]
        vre = vs[:, :, 0, 0:NC]; vim = vs[:, :, 1, 0:NC]
        c_ = bufB
        cre = c_[:, :, 0, 0:NC]; cim = c_[:, :, 1, 0:NC]
        T0 = bufT[:, 0, :, 0:NC]; T1 = bufT[:, 1, :, 0:NC]
        tt("DVE", T0, vre, cosv, ALU.mult)
        tt("POOL", T1, vim, sinv, ALU.mult)
        tt("DVE", cre, T0, T1, ALU.add)
        tt("POOL", T0, vim, cosv, ALU.mult)
        tt("DVE", T1, vre, sinv, ALU.mult)
        tt("POOL", cim, T0, T1, ALU.subtract)
        Wb = bufA
        for j in range(16):
            for ri in range(2):
                for (c0, ln, cin_ap) in tc.scan_segs(l, j, ri):
                    S.op("DVE", lambda o=Wb[:, j, ri, c0:c0 + ln], d1=c_[:, j, ri, c0:c0 + ln], ci=cin_ap:
                         nc.vector.tensor_tensor_scan(out=o, data0=Rall[:, l, j:j + 1].to_broadcast([128, ln]), data1=d1,
                                                      initial=ci, op0=ALU.mult, op1=ALU.add),
                         [c_[:, j, ri, c0:c0 + ln], Rall[:, l, j:j + 1], cin_ap], [Wb[:, j, ri, c0:c0 + ln]])
        wre = Wb[:, :, 0, 0:NC]; wim = Wb[:, :, 1, 0:NC]
        Xn = bufB
        xre = Xn[:, :, 0, 0:NC]; xim = Xn[:, :, 1, 0:NC]
        tt("DVE", T0, wre, cosv, ALU.mult)
        tt("POOL", T1, wim, sinv, ALU.mult)
        tt("DVE", xre, T0, T1, ALU.subtract)
        tt("POOL", T0, wre, sinv, ALU.mult)
        tt("DVE", T1, wim, cosv, ALU.mult)
        tt("POOL", xim, T0, T1, ALU.add)
        tc.scan_finish(l, Xn, Xpb)
        for q4 in range(4):
            wY = wload(s5wY[l][:, q4 * 3072:(q4 + 1) * 3072].rearrange("p (g r n) -> p g r n", r=3, n=128),
                       [128, 8, 3, 128], [f"s5w{l}"])
            b = bank()
            pv = psf(b, 128, 8 * NC).rearrange("p (g c) -> p g c", c=NC)
            for gl in range(8):
                g = q4 * 8 + gl
                j = g // 2
                mm(pv[:, gl, :], wY[:, gl, 0, :], Uf[:, g, 0:NC], True, False)
                mm(pv[:, gl, :], wY[:, gl, 1, :], Xpb[:, j, 0, 0:NC], False, False)
                mm(pv[:, gl, :], wY[:, gl, 2, :], Xpb[:, j, 1, 0:NC], False, True)
            yf = yfr.next()
            evac(yf[:, :, 0:NC], pv)
            b2 = bank()
            pb = psb(b2, NC)
            for gl in range(8):
                tr(pb[:, gl * 128:(gl + 1) * 128], yf[:, gl, 0:NC], idb[:, :])
            act(ucm[0:NC, :, q4 * 128:(q4 + 1) * 128].rearrange("c t (g h) -> c t g h", h=16),
                pb[:, 0:1024].rearrange("c (g t h) -> c t g h", t=8, h=16), AF.Gelu_apprx_tanh)
        for kc in range(4):
            b = bank()
            pb = psb(b)
            for s in range(8):
                tr(pb[:, s * NC:(s + 1) * NC], ucm[0:NC, s, kc * 128:(kc + 1) * 128], idb[0:NC, 0:NC])
            evac(zT[:, kc, :, 0:NC], pb[:, 0:8 * NC].rearrange("p (s c) -> p s c", c=NC))
        wg = wload(WB["s5_w_glu"][l].rearrange("(k p) n -> p k n", p=128), [128, 4, 512], [f"s5_w_glu{l}"])
        bgl = gload(W["s5_b_glu"][l:l + 1, :])
        g5 = gload(W["s5_out_norm"][l:l + 1, :])
        s5o_cm = bufA[:].rearrange("p a b c -> p (a b c)").bitcast(BF16)[:, 0:4096].rearrange("p (s n) -> p s n", n=512)
        gtmp = bufB[:].rearrange("p a b c -> p (a b c)")
        gdump = bufT[:].rearrange("p a b c -> p (a b c)")[0:NC, 1024:1536]
        for s0 in (0, 4):
            bs = [bank() for _ in range(4)]
            gas = [gtmp[0:NC, q * 512:(q + 1) * 512] for q in range(4)]
            sss = [ssr.next() for _ in range(4)]
            for q in range(4):
                for kc in range(4):
                    mm(psf(bs[q], NC), zT[:, kc, s0 + q, 0:NC], wg[:, kc, :], kc == 0, kc == 3)
            for q in range(4):
                tt("DVE", gas[q], psf(bs[q], NC), bgl[0:NC, 0:512], ALU.add)
            for q in range(4):
                act(gas[q], gas[q], AF.Sigmoid)
            for q in range(4):
                tt("POOL", gas[q], gas[q], ucm[0:NC, s0 + q, :], ALU.mult)
            for q in range(4):
                stt(gdump, gas[q], 1.0, gas[q], ALU.mult, ALU.mult, accum=sss[q][0:NC, 0:1])
            for q in range(4):
                S.op("ACT", lambda q=q: nc.scalar.activation(out=sss[q][0:NC, 1:2], in_=sss[q][0:NC, 0:1], func=AF.Sqrt,
                                                              bias=eps_c[0:NC], scale=1.0 / 512),
                     [sss[q][0:NC, 0:1], eps_c[0:NC]], [sss[q][0:NC, 1:2]])
            for q in range(4):
                recip(sss[q][0:NC, 1:2], sss[q][0:NC, 1:2])
            for q in range(4):
                stt(s5o_cm[0:NC, s0 + q, :], gas[q], sss[q][0:NC, 1:2], g5[0:NC, 0:512], ALU.mult, ALU.mult)
        for kc in range(4):
            b = bank()
            pb = psb(b)
            for s in range(8):
                tr(pb[:, s * NC:(s + 1) * NC], s5o_cm[0:NC, s, kc * 128:(kc + 1) * 128], idb[0:NC, 0:NC])
            evac(featA[:, kc, 0:NTt].rearrange("p (c s) -> p s c", s=8), pb[:, 0:8 * NC].rearrange("p (s c) -> p s c", c=NC))

    def ret_phase(tc, l):
        P, NTt = tc.P, tc.NT
        gate_g = gload(W["ret_out_norm"][l:l + 1, :])
        for n in (3, 4, 1, 2):
            wv = wload(WB["w_in"][l][:, n * 512:(n + 1) * 512].rearrange("(k p) n -> p k n", p=128), [128, 8, 512], [f"w_in{l}"])
            for i in range(tc.nsub):
                b = bank()
                for k in range(8):
                    mm(psf(b, P), hT[:, k, i * P:(i + 1) * P], wv[:, k, :], k == 0, k == 7)
                pv = psf(b, P)
                if n in (1, 2):
                    dst = (q_tm if n == 1 else k_tm)[0:P, i, :].rearrange("p (h t d) -> p h t d", t=2, d=64)
                    src = pv.rearrange("p (h t d) -> p h t d", t=2, d=64)
                    co = (0 if n == 1 else 2)
                    cosb = rope[0:P, i, co * 64:(co + 1) * 64].unsqueeze(1).to_broadcast([P, 4, 64])
                    sinb = rope[0:P, i, (co + 1) * 64:(co + 2) * 64].unsqueeze(1).to_broadcast([P, 4, 64])
                    r4 = [r[0:P, :].rearrange("p (h d) -> p h d", d=64) for r in rtm]
                    tt("DVE", r4[0], src[:, :, 0, :], cosb, ALU.mult)
                    tt("DVE", r4[1], src[:, :, 1, :], sinb, ALU.mult)
                    tt("DVE", r4[2], src[:, :, 0, :], sinb, ALU.mult)
                    tt("DVE", r4[3], src[:, :, 1, :], cosb, ALU.mult)
                    tt("POOL", dst[:, :, 0, :], r4[0], r4[1], ALU.subtract)
                    tt("POOL", dst[:, :, 1, :], r4[2], r4[3], ALU.add)
                elif n == 3:
                    cp("ACT", v_tm[0:P, i, :], pv)
                else:
                    act(sg_tm[0:P, i, :], pv, AF.Silu)
        dm, kd, qd, gt_ = tc.ret_consts
        pend = [None]

        def ret_final(i, rtm_i):
            b = bank()
            pb = psb(b)
            for h in range(4):
                tr(pb[:, h * P:(h + 1) * P], rtm_i[0:P, h * 128:(h + 1) * 128], idb[0:P, 0:P])
            evac(featA[:, 4:8, i * P:(i + 1) * P], pb[:, 0:4 * P].rearrange("p (h c) -> p h c", c=P))
        for i in range(tc.nsub):
            b = bank()
            pb = psb(b)
            pq = pb[:, 0:4 * P].rearrange("p (h c) -> p h c", c=P)
            pk = pb[:, 4 * P:8 * P].rearrange("p (h c) -> p h c", c=P)
            for h in range(4):
                tr(pq[:, h, :], q_tm[0:P, i, h * 128:(h + 1) * 128], idb[0:P, 0:P])
                tr(pk[:, h, :], k_tm[0:P, i, h * 128:(h + 1) * 128], idb[0:P, 0:P])
            cp("ACT", qT_sb[:, :, 0:P], pq)
            cp("ACT", kT_sb[:, :, 0:P], pk)
            qdv = qd[:, 0:8 * P].rearrange("p (h t c) -> p h t c", t=2, c=P)
            tt("DVE", qdA[:, :, 0:P], pq, qdv[:, :, 0, :], ALU.mult)
            tt("DVE", qdB[:, :, 0:P], pq, qdv[:, :, 1, :], ALU.mult)
            kv4 = k_tm[0:P, i, :].rearrange("p (h d) -> p h d", d=128)
            kdv = kd[0:P, 0:8].rearrange("p (h t) -> p h t", t=2)
            tt("POOL", kdA[0:P], kv4, kdv[:, :, 0:1].to_broadcast([P, 4, 128]), ALU.mult)
            tt("POOL", kdB[0:P], kv4, kdv[:, :, 1:2].to_broadcast([P, 4, 128]), ALU.mult)
            b = bank()
            psc = psf(b, P, 4 * P).rearrange("p (h c) -> p h c", c=P)
            for h in range(4):
                mm(psc[:, h, :], kT_sb[:, h, 0:P], qT_sb[:, h, 0:P])
            tt("DVE", sc_sb[0:P, :, 0:P], psc, dm[0:P, 0:4 * P].rearrange("p (h c) -> p h c", c=P), ALU.mult)
            bA = bank(); bB = bank()
            pkA = psf(bA).rearrange("p (h e) -> p h e", e=128)
            pkB = psf(bB).rearrange("p (h e) -> p h e", e=128)
            for h in range(4):
                mm(pkA[:, h, :], kdA[0:P, h, :], v_tm[0:P, i, h * 128:(h + 1) * 128])
                mm(pkB[:, h, :], kdB[0:P, h, :], v_tm[0:P, i, h * 128:(h + 1) * 128])
            SA_in, SB_in = tc.ret_states(l, i, psf(bA), psf(bB), gt_)
            b = bank()
            po = psf(b, P).rearrange("p (h e) -> p h e", e=128)
            for h in range(4):
                mm(po[:, h, :], sc_sb[0:P, h, 0:P], v_tm[0:P, i, h * 128:(h + 1) * 128], True, False)
                mm(po[:, h, :], qdA[:, h, 0:P], SA_in[:, h * 128:(h + 1) * 128], False, False)
                mm(po[:, h, :], qdB[:, h, 0:P], SB_in[:, h * 128:(h + 1) * 128], False, True)
            cp("ACT", o_f[0:P], po)
            tt("DVE", o_sq[0:P], o_f[0:P], o_f[0:P], ALU.mult)
            ss = ssr.next()
            S.op("DVE", lambda: nc.vector.tensor_reduce(out=ss[0:P, 0:4], in_=o_sq[0:P], axis=AX.X, op=ALU.add),
                 [o_sq[0:P]], [ss[0:P, 0:4]])
            rstd_of(ss[0:P, 0:4], ss[0:P, 4:8], P, 128)
            tt("DVE", o_f[0:P], o_f[0:P], ss[0:P, 4:8].unsqueeze(2).to_broadcast([P, 4, 128]), ALU.mult)
            of2 = o_f[0:P].rearrange("p h e -> p (h e)")
            tt("POOL", of2, of2, gate_g[0:P, 0:512], ALU.mult)
            rtm_i = ret_tmr.next()
            tt("POOL", rtm_i[0:P, :], of2, sg_tm[0:P, i, :], ALU.mult)
            if pend[0] is not None:
                ret_final(*pend[0])
            pend[0] = (i, rtm_i)
        ret_final(*pend[0])

    def mixer(tc, l):
        norm_T(tc, W["mix_norm_pre"][l:l + 1, :])
        s5_phase(tc, l)
        ret_phase(tc, l)
        proj_out_tm(tc, l, "w_out", featA, W["mix_norm_post"][l:l + 1, :])

    def prep_kt(kb, dst):
        for mc in range(2):
            b = bank()
            pb = psb(b)
            for ch in range(8):
                tr(pb[:, ch * 128:(ch + 1) * 128], kb[:, mc, ch * 128:(ch + 1) * 128], idb[:, :])
            evac(dst[:, :, mc * 128:(mc + 1) * 128], pb[:, 0:1024].rearrange("p (k c) -> p k c", c=128))

    def xattn(tc, l):
        P, NTt = tc.P, tc.NT
        norm_T(tc, W["xattn_norm_pre"][l:l + 1, :])
        wv = wload(WB["w_cq"][l].rearrange("(k p) n -> p k n", p=128), [128, 8, D], [f"w_cq{l}"])
        for ch in range(8):
            b = bank()
            for k in range(8):
                mm(psf(b, 128, NTt), wv[:, k, ch * 128:(ch + 1) * 128], hT[:, k, 0:NTt], k == 0, k == 7)
            evac(featB[:, ch, 0:NTt], psf(b, 128, NTt))
        for (c0, N, KT, V) in tc.kv_groups(l):
            pts = {}

            def scores(h, c0=c0, N=N, KT=KT):
                pT = pTr.next()
                for mc in range(2):
                    b = bank()
                    for dc in range(2):
                        mm(psf(b, 128, N), KT[:, 2 * h + dc, mc * 128:(mc + 1) * 128], featB[:, 2 * h + dc, c0:c0 + N], dc == 0, dc == 1)
                    act(pT[:, mc, 0:N], psf(b, 128, N), AF.Exp, scale=1.0 / 16)
                pts[h] = pT

            def pv(h, c0=c0, N=N, V=V):
                pT = pts[h]
                b = bank()
                for mc in range(2):
                    mm(psf(b, 128, N), onesb[:, :], pT[:, mc, 0:N], mc == 0, mc == 1)
                ri = rinv.next()
                recip(ri[:, 0:N], psf(b, 128, N))
                for dc in range(2):
                    b = bank()
                    for mc in range(2):
                        mm(psf(b, 128, N), V[:, mc, (2 * h + dc) * 128:(2 * h + dc + 1) * 128], pT[:, mc, 0:N], mc == 0, mc == 1)
                    tt("DVE", featA[:, 2 * h + dc, c0:c0 + N], psf(b, 128, N), ri[:, 0:N], ALU.mult)

            scores(0)
            for h in range(4):
                if h + 1 < 4:
                    scores(h + 1)
                pv(h)
        proj_out_tm(tc, l, "w_co", featA, W["xattn_norm_post"][l:l + 1, :])

    def ffn(tc, l):
        P, NTt = tc.P, tc.NT
        norm_T(tc, W["ffn_norm_pre"][l:l + 1, :])
        for cb in range(6):
            wcols = 512 if cb < 5 else 256
            wg = wload(WB["w_gate"][l][:, cb * 512:cb * 512 + wcols].rearrange("(k p) n -> p k n", p=128), [128, 8, wcols], [f"w_gate{l}"])
            wu = wload(WB["w_up"][l][:, cb * 512:cb * 512 + wcols].rearrange("(k p) n -> p k n", p=128), [128, 8, wcols], [f"w_up{l}"])
            for f in range(wcols // 128):
                ffc = cb * 4 + f
                bg = bank(); bu = bank()
                for k in range(8):
                    mm(psf(bg, 128, NTt), wg[:, k, f * 128:(f + 1) * 128], hT[:, k, 0:NTt], k == 0, k == 7)
                for k in range(8):
                    mm(psf(bu, 128, NTt), wu[:, k, f * 128:(f + 1) * 128], hT[:, k, 0:NTt], k == 0, k == 7)
                sg = sgr.next()
                act(sg[:, 0:NTt], psf(bg, 128, NTt), AF.Silu)
                tt("DVE", actT[:, ffc, 0:NTt], psf(bu, 128, NTt), sg[:, 0:NTt], ALU.mult)
        g = gload(W["ffn_norm_post"][l:l + 1, :])
        bank_rr[0] = 0
        for fb in range(0, 22, 4):
            nf = min(4, 22 - fb)
            wd = wload(WB["w_down"][l][fb * 128:(fb + nf) * 128, :].rearrange("(f p) n -> p f n", p=128), [128, nf, D], [f"w_down{l}"])
            for fl_ in range(nf):
                ffc = fb + fl_
                for i in range(tc.nsub):
                    for n in range(2):
                        mm(psf(2 * i + n, P), actT[:, ffc, i * P:(i + 1) * P], wd[:, fl_, n * 512:(n + 1) * 512], ffc == 0, ffc == 21)
        for i in range(tc.nsub):
            postnorm_add(tc, i, 2 * i, g)
        bank_rr[0] = 0

    def mem_kv():
        tcm = TC(); tcm.P = 128; tcm.nsub = 2; tcm.NT = 256
        dma(x_sb[:, 0:2, :], mem.rearrange("(i p) d -> p i d", p=128))
        for l in range(nl):
            norm_T(tcm, W["mem_norm"][l:l + 1, :])
            ck(f"mk_norm{l}")
            for (wn, outd, isk) in (("w_ck", o_memk, True), ("w_cv", o_memv, False)):
                wv = wload(WB[wn][l].rearrange("(k p) n -> p k n", p=128), [128, 8, D], [f"{wn}{l}"])
                ck(f"mk_w{l}")
                for i in range(2):
                    pair = bank(2)
                    for n in range(2):
                        for k in range(8):
                            mm(psf(pair + n), hT[:, k, i * 128:(i + 1) * 128], wv[:, k, n * 512:(n + 1) * 512], k == 0, k == 7)
                    ck(f"mk_mm{l}")
                    kf = kvf.next()
                    for n in range(2):
                        pv = psf(pair + n)
                        cp("ACT", kf[:, n * 512:(n + 1) * 512], pv)
                        ck(f"mk_cp{l}")
                        if isk:
                            cp("DVE", kb_x[:, i, n * 512:(n + 1) * 512], pv)
                            ck(f"mk_dcp{l}")
                        else:
                            cp("DVE", kvst[:, 2048 + i * 1024 + n * 512:2048 + i * 1024 + (n + 1) * 512], pv)
                    if stop == f"mk_dma{l}" and n == 1:
                        ck(f"mk_dma{l}")
                    dma(outd[l, i * 128:(i + 1) * 128, :], kf[:])
                    ck(f"mk_dmb{l}")
                ck(f"mk_proj{l}{wn}")
                if isk:
                    prep_kt(kb_x, kvst[:, 0:2048].rearrange("p (k m) -> p k m", m=256))
                    ck(f"mk_kt{l}")
            dma(kvs[l], kvst[:], writes=[S.dres(f"kvs{l}")])

    def prompt_tc(ti):
        tc = TC()
        tc.P = 128; tc.nsub = 4; tc.NT = 512; tc.NC = 64
        tc.rot_segs = [(0, 64, 0)]
        tc.ret_consts = (dmP, kdP, qdP, None)
        gch = [float(np.float32(np.exp(np.float32(64.0) * np.log(np.float32(1.0 - 2.0 ** (-5.0 - h)))))) for h in range(4)]

        def scan_segs(l, j, ri):
            return [(0, 64, car[:, l, ri, j:j + 1])]
        tc.scan_segs = scan_segs

        def scan_finish(l, Xn, Xp):
            cp("ACT", Xp[:, :, :, 0:1], car[:, l].rearrange("p r j -> p j r").unsqueeze(3))
            cp("POOL", Xp[:, :, :, 1:64], Xn[:, :, :, 0:63])
            cp("DVE", car[:, l].rearrange("p r j -> p j r").unsqueeze(3), Xn[:, :, :, 63:64])
        tc.scan_finish = scan_finish

        def ret_states(l, i, pkA, pkB, _):
            sA = Sbf.next()
            cp("ACT", sA[:], Sst[:, l, :])
            gp = gP
            tt("POOL", Stmp[:], Sst[:, l, :], gp[:, :], ALU.mult)
            tt("DVE", Sst[:, l, :], Stmp[:], pkA, ALU.add)
            sB = Sbf.next()
            cp("ACT", sB[:], Sst[:, l, :])
            tt("POOL", Stmp[:], Sst[:, l, :], gp[:, :], ALU.mult)
            tt("DVE", Sst[:, l, :], Stmp[:], pkB, ALU.add)
            return sA, sB
        tc.ret_states = ret_states

        def kv_groups(l):
            slot = wload(kvs[l], [128, 4096], [f"kvs{l}"])
            KT = slot[:, 0:2048].rearrange("p (k m) -> p k m", m=256)
            V = slot[:, 2048:4096].rearrange("p (c n) -> p c n", n=1024)
            return [(0, 512, KT, V)]
        tc.kv_groups = kv_groups
        return tc

    def sample_tc():
        tc = TC()
        tc.P = 32; tc.nsub = 1; tc.NT = 32; tc.NC = 4
        tc.rot_segs = [(0, 2, 0), (2, 2, 0)]
        tc.ret_consts = (dmS, kdS, qdS, None)

        def scan_segs(l, j, ri):
            return [(0, 2, cars[:, 0, ri, j:j + 1]), (2, 2, cars[:, 1, ri, j:j + 1])]
        tc.scan_segs = scan_segs

        def scan_finish(l, Xn, Xp):
            for sq_ in range(2):
                cv = cars[:, sq_].rearrange("p r j -> p j r").unsqueeze(3)
                cp("ACT", Xp[:, :, :, 2 * sq_:2 * sq_ + 1], cv)
                cp("POOL", Xp[:, :, :, 2 * sq_ + 1:2 * sq_ + 2], Xn[:, :, :, 2 * sq_:2 * sq_ + 1])
                for ri, od in ((0, o_s5re_s), (1, o_s5im_s)):
                    dma(od[l, sq_].rearrange("(j q) -> q j", q=128), Xn[:, :, ri, 2 * sq_ + 1], slow=True)
        tc.scan_finish = scan_finish

        def ret_states(l, i, pkA, pkB, _):
            outs = []
            for sq_, pk in ((0, pkA), (1, pkB)):
                sb = Sbf.next()
                cp("ACT", sb[:], Ssm[:, sq_, :])
                tt("POOL", Stmp[:], Ssm[:, sq_, :], gS[:, :], ALU.mult)
                tf = tmpf.next()
                tt("DVE", tf[:, 0:512], Stmp[:], pk, ALU.add)
                dma(o_ret_s[l, sq_].rearrange("h d e -> d h e"), tf[:, 0:512].rearrange("p (h e) -> p h e", e=128))
                outs.append(sb)
            return outs[0], outs[1]
        tc.ret_states = ret_states

        def kv_groups(l):
            res = []
            for sq_ in range(2):
                S.dma("POOL", kb_x[:], cmk[l, sq_].rearrange("(c p) d -> p c d", p=128))
                slot = wring.next()
                KT = slot[:, 0:2048].rearrange("p (k m) -> p k m", m=256)
                V = slot[:, 2048:4096].rearrange("p (c n) -> p c n", n=1024)
                prep_kt(kb_x, KT)
                S.dma("POOL", V, cmv[l, sq_].rearrange("(c p) d -> p c d", p=128))
                res.append((sq_ * 16, 16, KT, V))
            return res
        tc.kv_groups = kv_groups
        return tc

    def program():
        ck("pre")
        for l in range(nl if stop is None or not stop.startswith("mk_") else 0):
            s5_setup(l)
            ck(f"setup{l}")
        A.reset(ffn_mark_end)
        mem_kv()
        ck("memkv")

        def run_tile(tc, load_fn, store_fn, pre_layer=None, tag=""):
            load_fn()
            for l in range(nl):
                if pre_layer is not None:
                    pre_layer(l)
                norm_T(tc, W["mix_norm_pre"][l:l + 1, :])
                ck(f"{tag}norm{l}")
                s5_phase(tc, l)
                ck(f"{tag}s5{l}")
                ret_phase(tc, l)
                ck(f"{tag}ret{l}")
                proj_out_tm(tc, l, "w_out", featA, W["mix_norm_post"][l:l + 1, :])
                ck(f"{tag}mix{l}")
                xattn(tc, l)
                ck(f"{tag}xat{l}")
                ffn(tc, l)
                ck(f"{tag}ffn{l}")
            store_fn()

        for ti in range(n_tiles):
            tc = prompt_tc(ti)
            t0 = ti * NT

            def load_fn(t0=t0):
                dma(x_sb[:], xp[t0:t0 + NT, :].rearrange("(i p) d -> p i d", p=128))
                dma(rope[:], c_ropeP[t0:t0 + NT, :].rearrange("(i p) d -> p i d", p=128))

            def store_fn(t0=t0):
                dma(yp[t0:t0 + NT, :].rearrange("(i p) d -> p i d", p=128), x_sb[:])
            run_tile(tc, load_fn, store_fn, tag=f"t{ti}")
        for l in range(nl):
            dma(o_s5re_p[l].rearrange("(j q) -> q j", q=128), car[:, l, 0, :], slow=True)
            dma(o_s5im_p[l].rearrange("(j q) -> q j", q=128), car[:, l, 1, :], slow=True)
            dma(o_ret_p[l].rearrange("h d e -> d h e"), Sst[:, l, :].rearrange("p (h e) -> p h e", e=128))
        ck("prompt")
        if with_sample:
            tc = sample_tc()

            def load_s():
                dma(x_sb[0:32, 0, :], xs)
                dma(rope[0:32, 0, :], c_ropeS)

            def store_s():
                dma(ys, x_sb[0:32, 0, :])

            def pre_layer(l):
                for sq_ in range(2):
                    dma(Ssm[:, sq_, :].rearrange("p (h e) -> p h e", e=128), ret0[l, sq_].rearrange("h d e -> d h e"))
                    dma(cars[:, sq_, 0, :], s5re0[l, sq_].rearrange("(j g2) p -> (g2 p) j", g2=2), slow=True)
                    dma(cars[:, sq_, 1, :], s5im0[l, sq_].rearrange("(j g2) p -> (g2 p) j", g2=2), slow=True)
            run_tile(tc, load_s, store_s, pre_layer, tag="s")

    try:
        program()
    except _Stop:
        pass
    if stop is not None:
        dma(yp[0:NT, :].rearrange("(i p) d -> p i d", p=128), x_sb[:])
    S.finish()
    return nc, S, A


def _consts(T):
    bf = ml_dtypes.bfloat16
    c = {}
    c["c_idb"] = np.eye(128, dtype=np.float32).astype(bf)
    c["c_idf"] = np.eye(128, dtype=np.float32)
    s_idx = np.arange(128) // 16
    c["c_maskM"] = (s_idx[None, :] >= s_idx[:, None]).astype(np.float32)
    misc = np.zeros((128, 8), np.float32)
    misc[:64, 0] = 1; misc[64:, 1] = 1; misc[:64, 2] = -1; misc[64:, 3] = -1
    misc[:, 4] = EPS; misc[:, 5] = np.pi / 2; misc[:, 6] = 1.0
    c["c_misc"] = misc
    half = 64
    inv = (np.float32(10000.0) ** (-np.arange(half, dtype=np.float32) / np.float32(half))).astype(np.float32)

    def rope_tab(pos):
        ang = pos.astype(np.float32)[:, None] * inv[None, :]
        cs, sn = np.cos(ang).astype(np.float32), np.sin(ang).astype(np.float32)
        qs = np.float32(128.0 ** -0.5)
        return np.concatenate([cs * qs, sn * qs, cs, sn], axis=1).astype(np.float32)
    c["c_ropeP"] = rope_tab(np.arange(T))
    c["c_ropeS"] = rope_tab(np.concatenate([1024 + np.arange(16), 1024 + np.arange(16)]))
    lg = np.log((1.0 - 2.0 ** (-5.0 - np.arange(4, dtype=np.float32))).astype(np.float32)).astype(np.float32)

    def ret_tabs(P, C):
        idx = np.arange(P)
        ch = idx // C
        pos = (idx % C).astype(np.float32)
        dm = np.zeros((P, 4, P), np.float32)
        kd = np.zeros((P, 4, 2), np.float32)
        qd = np.zeros((4, 2, P), np.float32)
        for h in range(4):
            d = np.exp(lg[h] * np.abs(pos[:, None] - pos[None, :])).astype(np.float32)
            dm[:, h, :] = d * (ch[:, None] == ch[None, :])
            kdec = np.exp(lg[h] * (C - 1.0 - pos)).astype(np.float32)
            qdec = np.exp(lg[h] * (pos + 1.0)).astype(np.float32)
            for t in range(2):
                kd[:, h, t] = kdec * (ch == t)
                qd[h, t, :] = qdec * (ch == t)
        g = np.exp(lg * np.float32(C)).astype(np.float32)
        gt = np.repeat(g, 128)[None, :].astype(np.float32)
        return dm.reshape(P, 4 * P), kd.reshape(P, 8), qd.reshape(1, 8 * P), gt
    c["c_dmP"], c["c_kdP"], c["c_qdP"], c["c_gP"] = ret_tabs(128, 64)
    c["c_dmS"], c["c_kdS"], c["c_qdS"], c["c_gS"] = ret_tabs(32, 16)
    return c


_CACHE = {}


def _get(T, with_sample=True, nl=NL, stop=None):
    key = (T, with_sample, nl, stop)
    if key not in _CACHE:
        _CACHE[key] = build(T, with_sample, nl, stop)
    return _CACHE[key]


def run_cores(inputs, T, n_cores=8, with_sample=True, nl=NL, stop=None, trace=False):
    nc, S, A = _get(T, with_sample, nl, stop)
    consts = _consts(T)
    f = lambda a: np.ascontiguousarray(np.asarray(a, dtype=np.float32))
    in_maps = []
    wnames = ["w_in", "s5_w_glu", "w_out", "w_cq", "w_ck", "w_cv", "w_co", "w_gate", "w_up", "w_down",
              "mix_norm_pre", "mix_norm_post", "s5_lambda_re", "s5_lambda_im", "s5_log_step", "s5_b_re", "s5_b_im",
              "s5_c_re", "s5_c_im", "s5_d", "s5_b_glu", "s5_out_norm", "xattn_norm_pre", "xattn_norm_post", "mem_norm",
              "ffn_norm_pre", "ffn_norm_post"]
    shared = {k: f(inputs[k]) for k in wnames}
    shared["ret_out_norm"] = f(inputs["ret_out_norm"]).reshape(NL, 512)
    shared.update(consts)
    for c in range(n_cores):
        b = c % 4
        m = dict(shared)
        m["xp"] = f(inputs["x_prompt"][b, :T])
        m["xs"] = f(inputs["x_sample"][2 * c:2 * c + 2]).reshape(32, D)
        m["mem"] = f(inputs["mem_prompt"][b])
        m["s5re0"] = f(inputs["state_s5_re"][:, 2 * c:2 * c + 2])
        m["s5im0"] = f(inputs["state_s5_im"][:, 2 * c:2 * c + 2])
        m["ret0"] = f(inputs["state_ret"][:, 2 * c:2 * c + 2])
        m["cmk"] = f(inputs["cache_mem_k"][:, 2 * c:2 * c + 2]).reshape(NL, 2, NMEM, D)
        m["cmv"] = f(inputs["cache_mem_v"][:, 2 * c:2 * c + 2]).reshape(NL, 2, NMEM, D)
        in_maps.append(m)
    if trace:
        res = run_bass_kernel_spmd(nc, in_maps, core_ids=list(range(n_cores)), trace=True)
        print("EXEC_NS", res.exec_time_ns, flush=True)
    else:
        res = run_bass_kernel_spmd(nc, in_maps, core_ids=list(range(n_cores)))
    return res.results


def kernel(**inputs):
    T = 8192
    r = run_cores(inputs, T)
    y_prompt = np.stack([r[b]["yp"] for b in range(4)]).astype(np.float32)
    y_sample = np.concatenate([r[c]["ys"].reshape(2, 16, D) for c in range(8)]).astype(np.float32)
    s5re_p = np.stack([r[b]["o_s5re_p"].reshape(NL, 32, 64) for b in range(4)], axis=1)
    s5im_p = np.stack([r[b]["o_s5im_p"].reshape(NL, 32, 64) for b in range(4)], axis=1)
    ret_p = np.stack([r[b]["o_ret_p"] for b in range(4)], axis=1)
    memk = np.stack([r[b]["o_memk"].reshape(NL, NMEM, 4, 256) for b in range(4)], axis=1)
    memv = np.stack([r[b]["o_memv"].reshape(NL, NMEM, 4, 256) for b in range(4)], axis=1)
    s5re_s = np.concatenate([r[c]["o_s5re_s"].reshape(NL, 2, 32, 64) for c in range(8)], axis=1)
    s5im_s = np.concatenate([r[c]["o_s5im_s"].reshape(NL, 2, 32, 64) for c in range(8)], axis=1)
    ret_s = np.concatenate([r[c]["o_ret_s"] for c in range(8)], axis=1)
    return (y_prompt, y_sample, s5re_p.astype(np.float32), s5im_p.astype(np.float32), ret_p.astype(np.float32),
            memk.astype(np.float32), memv.astype(np.float32), s5re_s.astype(np.float32), s5im_s.astype(np.float32),
            ret_s.astype(np.float32))
```

```python
import math
import numpy as np
import ml_dtypes
import concourse.bass as bass
import concourse.mybir as mybir
from concourse.bass_utils import run_bass_kernel_spmd

F32 = mybir.dt.float32
BF16 = mybir.dt.bfloat16
AF = mybir.ActivationFunctionType
ALU = mybir.AluOpType
AX = mybir.AxisListType
DSZ = {F32: 4, BF16: 2}

D = 1024
DIN = 2560
DFF = 2816
NMEM = 256
NL = 4
EPS = 1e-6
SB_BASE = 16640
SB_END = 229376 - 64
ATOM = 256
NDMA = 32
NDMA_SP = 24
SAME_ENGINE_SMALL_ONLY = False
EMBED_WAIT = True


class Sched:
    def __init__(self, nc, sems):
        self.nc = nc
        self.names = ["PE", "ACT", "DVE", "POOL", "SP"] + [f"D{i}" for i in range(NDMA)]
        self.idx = {n: i for i, n in enumerate(self.names)}
        self.eng = {"PE": nc.tensor, "ACT": nc.scalar, "DVE": nc.vector, "POOL": nc.gpsimd, "SP": nc.sync}
        self.sem = sems
        ne = len(self.names)
        self.cnt = np.zeros(ne, np.int64)
        self.mult = np.array([1] * 5 + [16] * NDMA, np.int64)
        self.seen = np.zeros((5, ne), np.int64)
        self.n_sb = (SB_END + ATOM - 1) // ATOM
        self.n_ps = 8 * 2048 // ATOM
        self.n_atoms = self.n_sb + self.n_ps
        self.dram = {}
        cap = self.n_atoms + 256
        self.lw_eng = -np.ones(cap, np.int64)
        self.lw_cnt = np.zeros(cap, np.int64)
        self.rd = np.zeros((cap, ne), np.int64)
        self.dma_rr = 0
        self.dma_rr_sw = 0
        self.ninst = 0

    def dres(self, name):
        if name not in self.dram:
            self.dram[name] = self.n_atoms + len(self.dram)
        a = self.dram[name]
        return (a, a + 1)

    def rng(self, ap):
        if isinstance(ap, tuple):
            return ap
        t = ap.tensor
        cls = type(t).__name__
        if cls.startswith("DRam"):
            return None
        dsz = DSZ[ap.dtype]
        row = 1
        for s in list(t.shape)[1:]:
            row *= int(s)
        col0 = int(ap.offset) % row
        ext = 1
        for st, c in list(ap.ap)[1:]:
            ext += (int(c) - 1) * abs(int(st))
        lo = col0 * dsz
        hi = (col0 + ext) * dsz
        if cls.startswith("SB"):
            base = int(t.manual_sbuf_range[0])
            return ((base + lo) // ATOM, (base + hi + ATOM - 1) // ATOM)
        return (self.n_sb + (lo // 2048) * (2048 // ATOM), self.n_sb + ((hi + 2047) // 2048) * (2048 // ATOM))

    def _deps(self, reads, writes):
        deps = np.zeros(len(self.names), np.int64)
        for r in reads:
            a, b = r
            le = self.lw_eng[a:b]
            m = le >= 0
            if m.any():
                np.maximum.at(deps, le[m], self.lw_cnt[a:b][m])
        for w in writes:
            a, b = w
            le = self.lw_eng[a:b]
            m = le >= 0
            if m.any():
                np.maximum.at(deps, le[m], self.lw_cnt[a:b][m])
            np.maximum(deps, self.rd[a:b].max(axis=0), out=deps)
        return deps

    def _waits(self, e, deps, embed=False):
        ei = self.idx[e]
        need = [o for o in np.nonzero(deps > self.seen[ei])[0] if not (o == ei and e == "PE")]
        last = None
        if embed and EMBED_WAIT and need:
            last = need.pop()
        for o in need:
            self.eng[e].wait_ge(self.sem[o], int(deps[o] * self.mult[o]))
            self.seen[ei, o] = deps[o]
        if last is not None:
            self.seen[ei, last] = deps[last]
            return (self.sem[last], int(deps[last] * self.mult[last]))
        return None

    def _small(self, ap):
        if isinstance(ap, tuple):
            return False
        n = 1
        for st, c in list(ap.ap)[1:]:
            n *= int(c)
        return n * DSZ[ap.dtype] <= 64

    def op(self, e, fn, reads, writes):
        sm_r = [self.rng(r) for r in reads if self._small(r)]
        sm_w = [self.rng(w) for w in writes if self._small(w)]
        reads = [x for x in (self.rng(r) for r in reads) if x is not None]
        writes = [x for x in (self.rng(w) for w in writes) if x is not None]
        deps = self._deps(reads, writes)
        ei = self.idx[e]
        if SAME_ENGINE_SMALL_ONLY and e != "PE":
            dsm = self._deps([x for x in sm_r if x is not None], [x for x in sm_w if x is not None])
            deps[ei] = dsm[ei]
        for a, b in reads:
            if a >= self.n_sb and a < self.n_atoms:
                extra = self.rd[a:b].max(axis=0).copy()
                extra[ei] = 0
                np.maximum(deps, extra, out=deps)
        emb = self._waits(e, deps, embed=True)
        inst = fn()
        if emb is not None:
            inst._wait_ge(emb[0], emb[1])
        self.cnt[ei] += 1
        inst.then_inc(self.sem[ei], 1)
        c = self.cnt[ei]
        for a, b in writes:
            self.lw_eng[a:b] = ei
            self.lw_cnt[a:b] = c
            self.rd[a:b, :] = 0
        for a, b in reads:
            self.rd[a:b, ei] = c
        self.ninst += 1

    def dma(self, q, out, in_, reads=(), writes=(), **kw):
        if q == "POOL":
            di = 5 + NDMA_SP + self.dma_rr_sw
            self.dma_rr_sw = (self.dma_rr_sw + 1) % (NDMA - NDMA_SP)
        else:
            di = 5 + self.dma_rr
            self.dma_rr = (self.dma_rr + 1) % NDMA_SP
        rs = [x for x in (self.rng(r) for r in [in_] + list(reads)) if x is not None]
        ws = [x for x in (self.rng(w) for w in [out] + list(writes)) if x is not None]
        deps = self._deps(rs, ws)
        deps[di] = max(deps[di], self.cnt[di])
        emb = self._waits(q, deps, embed=True)
        inst = self.eng[q].dma_start(out=out, in_=in_, **kw)
        if emb is not None:
            inst._wait_ge(emb[0], emb[1])
        self.cnt[di] += 1
        inst.then_inc(self.sem[di], 16)
        c = self.cnt[di]
        for a, b in ws:
            self.lw_eng[a:b] = di
            self.lw_cnt[a:b] = c
            self.rd[a:b, :] = 0
        for a, b in rs:
            self.rd[a:b, di] = c
        self.ninst += 1

    def finish(self):
        deps = self.cnt.copy()
        self._waits("SP", deps)


class Arena:
    def __init__(self, nc):
        self.nc = nc
        self.off = SB_BASE
        self.n = 0
        self.peak = SB_BASE

    def alloc(self, shape, dtype, name="t"):
        nbytes = DSZ[dtype]
        for s in shape[1:]:
            nbytes *= s
        off = (self.off + 63) // 64 * 64
        assert off + nbytes <= SB_END, f"SBUF overflow allocating {name} {shape}: {off + nbytes}"
        self.n += 1
        t = self.nc.alloc_sbuf_tensor_at(f"{name}_{self.n}", list(shape), dtype, offset=off)
        self.off = off + nbytes
        self.peak = max(self.peak, self.off)
        return t

    def mark(self):
        return self.off

    def reset(self, m):
        self.off = m


class Ring:
    def __init__(self, bufs):
        self.bufs = bufs
        self.i = 0

    def next(self):
        b = self.bufs[self.i]
        self.i = (self.i + 1) % len(self.bufs)
        return b


class _Stop(Exception):
    pass


def build(T, with_sample=True, nl=NL, stop=None):
    def ck(name):
        if stop == name:
            raise _Stop()
    nc = bass.Bass("TRN2", target_bir_lowering=False)
    NT = 512
    assert T % NT == 0
    n_tiles = T // NT

    def din(name, shape, dt=F32):
        return nc.dram_tensor(name, list(shape), dt, kind="ExternalInput").ap()

    def dout(name, shape, dt=F32):
        return nc.dram_tensor(name, list(shape), dt, kind="ExternalOutput").ap()

    def dscr(name, shape, dt=BF16):
        return nc.dram_tensor(name, list(shape), dt).ap()

    xp = din("xp", [T, D]); xs = din("xs", [32, D]); mem = din("mem", [NMEM, D])
    s5re0 = din("s5re0", [NL, 2, 32, 64]); s5im0 = din("s5im0", [NL, 2, 32, 64])
    ret0 = din("ret0", [NL, 2, 4, 128, 128])
    cmk = din("cmk", [NL, 2, NMEM, D]); cmv = din("cmv", [NL, 2, NMEM, D])
    W = {}
    wshapes = dict(w_in=[NL, D, DIN], s5_w_glu=[NL, 512, 512], w_out=[NL, D, D], w_cq=[NL, D, D], w_ck=[NL, D, D],
                   w_cv=[NL, D, D], w_co=[NL, D, D], w_gate=[NL, D, DFF], w_up=[NL, D, DFF], w_down=[NL, DFF, D])
    for k, s in wshapes.items():
        W[k] = din(k, s)
    vshapes = dict(mix_norm_pre=[NL, D], mix_norm_post=[NL, D], s5_lambda_re=[NL, 32, 64], s5_lambda_im=[NL, 32, 64],
                   s5_log_step=[NL, 32], s5_b_re=[NL, 32, 64, 16], s5_b_im=[NL, 32, 64, 16], s5_c_re=[NL, 32, 16, 64],
                   s5_c_im=[NL, 32, 16, 64], s5_d=[NL, 32, 16], s5_b_glu=[NL, 512], s5_out_norm=[NL, 512],
                   ret_out_norm=[NL, 512], xattn_norm_pre=[NL, D], xattn_norm_post=[NL, D], mem_norm=[NL, D],
                   ffn_norm_pre=[NL, D], ffn_norm_post=[NL, D])
    for k, s in vshapes.items():
        W[k] = din(k, s)
    c_idb = din("c_idb", [128, 128], BF16); c_idf = din("c_idf", [128, 128]); c_maskM = din("c_maskM", [128, 128])
    c_misc = din("c_misc", [128, 8])
    c_ropeP = din("c_ropeP", [T, 256]); c_ropeS = din("c_ropeS", [32, 256])
    c_dmP = din("c_dmP", [128, 512]); c_kdP = din("c_kdP", [128, 8]); c_qdP = din("c_qdP", [1, 1024])
    c_dmS = din("c_dmS", [32, 128]); c_kdS = din("c_kdS", [32, 8]); c_qdS = din("c_qdS", [1, 256])
    c_gP = din("c_gP", [1, 512]); c_gS = din("c_gS", [1, 512])

    yp = dout("yp", [T, D]); ys = dout("ys", [32, D])
    o_s5re_p = dout("o_s5re_p", [NL, 32 * 64]); o_s5im_p = dout("o_s5im_p", [NL, 32 * 64])
    o_ret_p = dout("o_ret_p", [NL, 4, 128, 128])
    o_memk = dout("o_memk", [NL, NMEM, D]); o_memv = dout("o_memv", [NL, NMEM, D])
    o_s5re_s = dout("o_s5re_s", [NL, 2, 32 * 64]); o_s5im_s = dout("o_s5im_s", [NL, 2, 32 * 64])
    o_ret_s = dout("o_ret_s", [NL, 2, 4, 128, 128])

    WB = {k: dscr(k + "_b", s) for k, s in wshapes.items()}
    s5wV = dscr("s5wV", [NL, 128, 32 * 2 * 128]); s5wY = dscr("s5wY", [NL, 128, 32 * 3 * 128])
    rotw = dscr("rotw", [NL, 128, 16 * 2 * 64], F32)
    kvs = dscr("kvs", [NL, 128, 4096])

    ps = nc.alloc_psum_tensor("ps", [128, 8, 512], F32)
    sem_cm = [nc.semaphore(f"s{i}") for i in range(5 + NDMA)]
    sems = [s.__enter__() for s in sem_cm]
    S = Sched(nc, sems)
    A = Arena(nc)

    def mm(out, lhsT, rhs, start=True, stop=True):
        S.op("PE", lambda: nc.tensor.matmul(out, lhsT=lhsT, rhs=rhs, start=start, stop=stop),
             [lhsT, rhs] + ([] if start else [out]), [out])

    def tr(out, in_, ident):
        S.op("PE", lambda: nc.tensor.transpose(out, in_, ident), [in_, ident], [out])

    def act(out, in_, func, bias=None, scale=None, accum=None):
        kw = {}
        rd = [in_]
        wr = [out]
        if bias is not None:
            kw["bias"] = bias; rd.append(bias)
        if scale is not None:
            kw["scale"] = scale
            if not isinstance(scale, float):
                rd.append(scale)
        if accum is not None:
            kw["accum_out"] = accum; wr.append(accum)
        S.op("ACT", lambda: nc.scalar.activation(out=out, in_=in_, func=func, **kw), rd, wr)

    def E(e):
        return {"DVE": nc.vector, "POOL": nc.gpsimd, "ACT": nc.scalar}[e]

    def tt(e, out, in0, in1, op):
        S.op(e, lambda: E(e).tensor_tensor(out=out, in0=in0, in1=in1, op=op), [in0, in1], [out])

    def ts(e, out, in0, s1, s2, op0, op1=None):
        rd = [in0] + [s for s in (s1, s2) if s is not None and not isinstance(s, float)]
        if op1 is None:
            S.op(e, lambda: E(e).tensor_scalar(out=out, in0=in0, scalar1=s1, scalar2=None, op0=op0), rd, [out])
        else:
            S.op(e, lambda: E(e).tensor_scalar(out=out, in0=in0, scalar1=s1, scalar2=s2, op0=op0, op1=op1), rd, [out])

    def stt(out, in0, scalar, in1, op0, op1, accum=None):
        rd = [in0, in1] + ([] if isinstance(scalar, float) else [scalar])
        wr = [out] + ([accum] if accum is not None else [])
        kw = {"accum_out": accum} if accum is not None else {}
        S.op("DVE", lambda: nc.vector.scalar_tensor_tensor(out=out, in0=in0, scalar=scalar, in1=in1, op0=op0, op1=op1, **kw),
             rd, wr)

    def cp(e, out, in_):
        if e == "ACT":
            S.op(e, lambda: nc.scalar.copy(out=out, in_=in_), [in_], [out])
        else:
            S.op(e, lambda: E(e).tensor_copy(out=out, in_=in_), [in_], [out])

    def recip(out, in_):
        S.op("DVE", lambda: nc.vector.reciprocal(out=out, in_=in_), [in_], [out])

    def memset(e, ap, v):
        S.op(e, lambda: E(e).memset(ap, v), [], [ap])

    def dma(out, in_, q="SP", reads=(), writes=(), slow=False):
        kw = {"allow_slow_non_contiguous": True} if slow else {}
        S.dma(q, out, in_, reads=reads, writes=writes, **kw)

    bank_rr = [0]

    def bank(n=1):
        b = (bank_rr[0] + n - 1) // n * n
        if b + n > 8:
            b = 0
        bank_rr[0] = (b + n) % 8
        return b

    def psf(b, P=128, n=512, nb=1):
        if nb == 1:
            return ps[0:P, b, 0:n]
        return ps[0:P, b:b + nb, :].rearrange("p b n -> p (b n)")[:, 0:n]

    def psb(b, P=128):
        return ps[0:P, b, :].bitcast(BF16)

    evac_rr = [0]

    def evac(out, in_):
        evac_rr[0] ^= 1
        cp("ACT" if evac_rr[0] else "DVE", out, in_)

    idb = A.alloc([128, 128], BF16, "idb"); idf = A.alloc([128, 128], F32, "idf")
    maskM = A.alloc([128, 128], F32, "maskM"); misc = A.alloc([128, 8], F32, "misc")
    onesb = A.alloc([128, 128], BF16, "onesb")
    dmP = A.alloc([128, 512], F32, "dmP"); kdP = A.alloc([128, 8], F32, "kdP"); qdP = A.alloc([128, 1024], F32, "qdP")
    dmS = A.alloc([128, 128], F32, "dmS"); kdS = A.alloc([128, 8], F32, "kdS"); qdS = A.alloc([128, 256], F32, "qdS")
    gP = A.alloc([128, 512], F32, "gP"); gS = A.alloc([128, 512], F32, "gS")
    rope = A.alloc([128, 4, 256], F32, "rope")
    x_sb = A.alloc([128, 4, D], F32, "x")
    hT = A.alloc([128, 8, NT], BF16, "hT")
    featA = A.alloc([128, 8, NT], BF16, "featA")
    featB = A.alloc([128, 8, NT], BF16, "featB")
    Sst = A.alloc([128, NL, 512], F32, "Sst")
    Ssm = A.alloc([128, 2, 512], F32, "Ssm")
    Sbf = Ring([A.alloc([128, 512], BF16, "Sbf") for _ in range(3)])
    car = A.alloc([128, NL, 2, 16], F32, "car")
    cars = A.alloc([128, 2, 2, 16], F32, "cars")
    Rall = A.alloc([128, NL, 16], F32, "Rall")
    gring = Ring([A.alloc([128, D], F32, "g") for _ in range(2)])
    tmpf = Ring([A.alloc([128, D], F32, "tmpf") for _ in range(2)])
    junk = A.alloc([128, D], BF16, "junk")
    hbr = Ring([A.alloc([128, D], BF16, "hb") for _ in range(4)])
    ssr = Ring([A.alloc([128, 8], F32, "ss") for _ in range(4)])
    ring_mark = A.mark()
    NSLOT = 4
    wring = Ring([A.alloc([128, 8192], BF16, "wr") for _ in range(NSLOT)])
    phase_mark = A.mark()

    pm = lambda g2: misc[:, g2:g2 + 1]
    npm = lambda g2: misc[:, 2 + g2:3 + g2]
    eps_c = misc[:, 4:5]
    hpi_c = misc[:, 5:6]

    dma(idb[:], c_idb); dma(idf[:], c_idf); dma(maskM[:], c_maskM); dma(misc[:], c_misc)
    dma(dmP[:], c_dmP); dma(kdP[:], c_kdP); dma(qdP[:], c_qdP.partition_broadcast(128))
    dma(dmS[0:32, :], c_dmS); dma(kdS[0:32, :], c_kdS); dma(qdS[:], c_qdS.partition_broadcast(128))
    dma(gP[:], c_gP.partition_broadcast(128)); dma(gS[:], c_gS.partition_broadcast(128))
    memset("DVE", onesb[:], 1.0)
    memset("DVE", Sst[:], 0.0)
    memset("DVE", car[:], 0.0)

    for l in range(nl):
        for k in wshapes:
            S.dma("POOL", WB[k][l], W[k][l], writes=[S.dres(f"{k}{l}")])

    def wload(src, shape, res, q="SP"):
        slot = wring.next()
        n = 1
        for s in shape[1:]:
            n *= s
        v = slot[:, 0:n]
        if len(shape) == 3:
            v = v.rearrange("p (a b) -> p a b", b=shape[2])
        elif len(shape) == 4:
            v = v.rearrange("p (a b c) -> p a b c", b=shape[2], c=shape[3])
        dma(v, src, q=q, reads=[S.dres(r) for r in res])
        return v

    def gload(row):
        g = gring.next()
        n = row.shape[-1]
        dma(g[:, 0:n], row.partition_broadcast(128))
        return g

    def s5_setup(l):
        A.reset(ring_mark)
        al = lambda shape, dt=F32, nm="s": A.alloc(shape, dt, nm)
        LR = al([128, 16]); LI = al([128, 16]); LS = al([128, 16])
        lam_v = lambda a: a[l].rearrange("(j g2) p -> (g2 p) j", g2=2)
        dma(LR[:], lam_v(W["s5_lambda_re"]), slow=True)
        dma(LI[:], lam_v(W["s5_lambda_im"]), slow=True)
        lsv = W["s5_log_step"][l].rearrange("(j g2) -> g2 j", g2=2)
        dma(LS[0:64, :], lsv[0:1, :].partition_broadcast(64), slow=True)
        dma(LS[64:128, :], lsv[1:2, :].partition_broadcast(64), slow=True)
        dt_ = al([128, 16]); ar = al([128, 16]); ai = al([128, 16]); mag = al([128, 16])
        act(dt_[:], LS[:], AF.Exp)
        tt("DVE", ar[:], LR[:], dt_[:], ALU.mult)
        tt("DVE", ai[:], LI[:], dt_[:], ALU.mult)
        act(mag[:], ar[:], AF.Exp)
        cs = al([128, 16]); sn = al([128, 16]); t1 = al([128, 16]); t2 = al([128, 16])
        act(sn[:], ai[:], AF.Sin, scale=1.0 / 16)
        act(cs[:], ai[:], AF.Sin, bias=hpi_c, scale=1.0 / 16)
        for _ in range(4):
            tt("DVE", t1[:], cs[:], cs[:], ALU.mult)
            tt("DVE", t2[:], sn[:], sn[:], ALU.mult)
            stt(sn[:], cs[:], 2.0, sn[:], ALU.mult, ALU.mult)
            tt("DVE", cs[:], t1[:], t2[:], ALU.subtract)
        apw = al([128, 9, 2, 16]); aiv = al([128, 9, 2, 16])
        memset("DVE", apw[:, 0, 0, :], 1.0); memset("DVE", apw[:, 0, 1, :], 0.0)
        tt("DVE", apw[:, 1, 0, :], mag[:], cs[:], ALU.mult)
        tt("DVE", apw[:, 1, 1, :], mag[:], sn[:], ALU.mult)

        def cmul(o_re, o_im, a_re, a_im, b_re, b_im, tA, tB):
            tt("DVE", tA, a_re, b_re, ALU.mult)
            tt("DVE", tB, a_im, b_im, ALU.mult)
            tt("DVE", o_re, tA, tB, ALU.subtract)
            tt("DVE", tA, a_re, b_im, ALU.mult)
            tt("DVE", tB, a_im, b_re, ALU.mult)
            tt("DVE", o_im, tA, tB, ALU.add)

        for n in range(2, 9):
            cmul(apw[:, n, 0, :], apw[:, n, 1, :], apw[:, n - 1, 0, :], apw[:, n - 1, 1, :],
                 apw[:, 1, 0, :], apw[:, 1, 1, :], t1[:], t2[:])
        m2 = al([128, 16]); rm2 = al([128, 16])
        tt("DVE", m2[:], mag[:], mag[:], ALU.mult)
        recip(rm2[:], m2[:])
        tt("DVE", aiv[:, 1, 0, :], apw[:, 1, 0, :], rm2[:], ALU.mult)
        stt(aiv[:, 1, 1, :], apw[:, 1, 1, :], -1.0, rm2[:], ALU.mult, ALU.mult)
        for n in range(2, 9):
            cmul(aiv[:, n, 0, :], aiv[:, n, 1, :], aiv[:, n - 1, 0, :], aiv[:, n - 1, 1, :],
                 aiv[:, 1, 0, :], aiv[:, 1, 1, :], t1[:], t2[:])
        nr = al([128, 16]); den = al([128, 16]); fre = al([128, 16]); fim = al([128, 16])
        ts("DVE", nr[:], apw[:, 1, 0, :], -1.0, None, ALU.add)
        tt("DVE", t1[:], LR[:], LR[:], ALU.mult)
        tt("DVE", t2[:], LI[:], LI[:], ALU.mult)
        tt("DVE", den[:], t1[:], t2[:], ALU.add)
        recip(den[:], den[:])
        tt("DVE", t1[:], nr[:], LR[:], ALU.mult)
        tt("DVE", t2[:], apw[:, 1, 1, :], LI[:], ALU.mult)
        tt("DVE", t1[:], t1[:], t2[:], ALU.add)
        tt("DVE", fre[:], t1[:], den[:], ALU.mult)
        tt("DVE", t1[:], apw[:, 1, 1, :], LR[:], ALU.mult)
        tt("DVE", t2[:], nr[:], LI[:], ALU.mult)
        tt("DVE", t1[:], t1[:], t2[:], ALU.subtract)
        tt("DVE", fim[:], t1[:], den[:], ALU.mult)
        R = Rall[:, l, :]
        tt("DVE", t1[:], m2[:], m2[:], ALU.mult)
        tt("DVE", R, t1[:], t1[:], ALU.mult)
        rR = al([128, 16])
        recip(rR[:], R)
        rot = al([128, 16, 2, 64])
        tt("DVE", rot[:, :, 0, 0], apw[:, 8, 0, :], rR[:], ALU.mult)
        tt("DVE", rot[:, :, 1, 0], apw[:, 8, 1, :], rR[:], ALU.mult)
        tr1 = al([128, 16, 32]); tr2 = al([128, 16, 32])
        for k in range(6):
            w = 1 << k
            bre = rot[:, :, 0, w - 1:w].to_broadcast([128, 16, w])
            bim = rot[:, :, 1, w - 1:w].to_broadcast([128, 16, w])
            cmul(rot[:, :, 0, w:2 * w], rot[:, :, 1, w:2 * w], rot[:, :, 0, 0:w], rot[:, :, 1, 0:w], bre, bim,
                 tr1[:, :, 0:w], tr2[:, :, 0:w])
        dma(rotw[l].rearrange("p (j r n) -> p j r n", r=2, n=64), rot[:], writes=[S.dres(f"rot{l}")])
        Bre = al([128, 16, 16]); Bim = al([128, 16, 16])
        bv = lambda a: a[l].rearrange("(j g2) p h -> (g2 p) j h", g2=2)
        dma(Bre[:], bv(W["s5_b_re"]), slow=True)
        dma(Bim[:], bv(W["s5_b_im"]), slow=True)
        Cre = al([128, 16, 16]); Cim = al([128, 16, 16])
        cin = al([128, 128])
        for (src, dst) in ((W["s5_c_re"], Cre), (W["s5_c_im"], Cim)):
            for half in range(2):
                cv = src[l].rearrange("(j g2) h p -> j h g2 p", g2=2)[half * 8:(half + 1) * 8]
                for jl in range(8):
                    dma(cin[jl * 16:(jl + 1) * 16, :].rearrange("h (g2 p) -> h g2 p", p=64), cv[jl], slow=True)
                b = bank()
                tr(psf(b, 128, 128), cin[:], idf[:])
                cp("DVE", dst[:, half * 8:(half + 1) * 8, :], psf(b, 128, 128).rearrange("p (j h) -> p j h", h=16))
        Bbr = al([128, 16, 16]); Bbi = al([128, 16, 16]); u1 = al([128, 16, 16]); u2 = al([128, 16, 16])
        bc = lambda a: a.unsqueeze(2).to_broadcast([128, 16, 16])
        cmul(Bbr[:], Bbi[:], bc(fre[:]), bc(fim[:]), Bre[:], Bim[:], u1[:], u2[:])
        CAr = al([128, 16, 8, 16]); CAi = al([128, 16, 8, 16])
        ABr = al([128, 16, 8, 16]); ABi = al([128, 16, 8, 16])
        Vr = al([128, 16, 8, 16]); Vi = al([128, 16, 8, 16])
        for t in range(8):
            cmul(CAr[:, :, t, :], CAi[:, :, t, :], bc(apw[:, t + 1, 0, :]), bc(apw[:, t + 1, 1, :]), Cre[:], Cim[:], u1[:], u2[:])
            cmul(ABr[:, :, t, :], ABi[:, :, t, :], bc(aiv[:, t + 1, 0, :]), bc(aiv[:, t + 1, 1, :]), Bbr[:], Bbi[:], u1[:], u2[:])
            cmul(Vr[:, :, t, :], Vi[:, :, t, :], bc(apw[:, 7 - t, 0, :]), bc(apw[:, 7 - t, 1, :]), Bbr[:], Bbi[:], u1[:], u2[:])
        dcol = al([128, 32])
        for s in range(8):
            dma(dcol[s * 16:(s + 1) * 16, :], W["s5_d"][l].rearrange("g h -> h g"), slow=True)
        fl = lambda a: a[:].rearrange("p j t h -> p j (t h)")
        SV = Ring([al([128, 8, 2, 128], BF16) for _ in range(2)])
        SY = Ring([al([128, 8, 3, 128], BF16) for _ in range(2)])
        mtmp = Ring([al([128, 128]) for _ in range(20)])
        for g2 in range(2):
            for q4 in range(4):
                sv = SV.next(); sy = SY.next()
                js = [q4 * 4 + jl for jl in range(4)]
                m0 = [mtmp.next() for _ in range(4)]; m1 = [mtmp.next() for _ in range(4)]
                mt = [mtmp.next() for _ in range(4)]; m2_ = [mtmp.next() for _ in range(4)]; m3_ = [mtmp.next() for _ in range(4)]
                for jl, j in enumerate(js):
                    act(m0[jl][:], fl(ABr)[:, j, :], AF.Copy, scale=pm(g2))
                    act(m1[jl][:], fl(ABi)[:, j, :], AF.Copy, scale=npm(g2))
                for jl, j in enumerate(js):
                    ts("DVE", m2_[jl][:], fl(Vr)[:, j, :], pm(g2), None, ALU.mult)
                    ts("DVE", m3_[jl][:], fl(Vi)[:, j, :], pm(g2), None, ALU.mult)
                bM = [bank() for _ in range(4)]
                for jl, j in enumerate(js):
                    mm(psf(bM[jl], 128, 128), m0[jl][:], fl(CAr)[:, j, :], True, False)
                    mm(psf(bM[jl], 128, 128), m1[jl][:], fl(CAi)[:, j, :], False, True)
                bV = [bank() for _ in range(4)]
                for jl, j in enumerate(js):
                    tr(psf(bV[jl], 128, 128), m2_[jl][:], idf[:])
                    tr(psf(bV[jl], 128, 256)[:, 128:256], m3_[jl][:], idf[:])
                for jl, j in enumerate(js):
                    g = 2 * j + g2
                    tt("DVE", mt[jl][:], psf(bM[jl], 128, 128), maskM[:], ALU.mult)
                    stt(sy[:, jl, 0, :], idf[:], dcol[:, g:g + 1], mt[jl][:], ALU.mult, ALU.add)
                for jl, j in enumerate(js):
                    act(sy[:, jl, 1, :], fl(CAr)[:, j, :], AF.Copy, scale=pm(g2))
                    act(sy[:, jl, 2, :], fl(CAi)[:, j, :], AF.Copy, scale=npm(g2))
                for jl, j in enumerate(js):
                    cp("ACT", sv[:, jl, :, :], psf(bV[jl], 128, 256).rearrange("p (r n) -> p r n", n=128))
                vV = s5wV[l].rearrange("p (j g2 r n) -> p j g2 r n", g2=2, r=2, n=128)[:, q4 * 4:q4 * 4 + 4, g2]
                vY = s5wY[l].rearrange("p (j g2 r n) -> p j g2 r n", g2=2, r=3, n=128)[:, q4 * 4:q4 * 4 + 4, g2]
                dma(vV, sv[:, 0:4], writes=[S.dres(f"s5w{l}")])
                dma(vY, sy[:, 0:4], writes=[S.dres(f"s5w{l}")])

    class TC:
        pass

    def norm_T(tc, gain_row, nsub=None, src=None, n_feat=D):
        P = tc.P
        g = gload(gain_row)
        nsub = tc.nsub if nsub is None else nsub
        src = x_sb if src is None else src
        sss = [ssr.next() for _ in range(nsub)]
        hbs = [hbr.next() for _ in range(nsub)]
        for i in range(nsub):
            act(junk[0:P, :], src[0:P, i, :], AF.Square, accum=sss[i][0:P, 0:1])
        for i in range(nsub):
            act(sss[i][0:P, 1:2], sss[i][0:P, 0:1], AF.Sqrt, bias=eps_c[0:P], scale=1.0 / n_feat)
        for i in range(nsub):
            recip(sss[i][0:P, 2:3], sss[i][0:P, 1:2])
        for i in range(nsub):
            stt(hbs[i][0:P, :], src[0:P, i, :], sss[i][0:P, 2:3], g[0:P, :], ALU.mult, ALU.mult)
        bs = [bank() for _ in range(nsub)]
        for i in range(nsub):
            pb = psb(bs[i])
            for k in range(8):
                tr(pb[:, k * P:(k + 1) * P], hbs[i][0:P, k * 128:(k + 1) * 128], idb[0:P, 0:P])
        for i in range(nsub):
            evac(hT[:, :, i * P:(i + 1) * P], psb(bs[i])[:, 0:8 * P].rearrange("p (k c) -> p k c", c=P))

    def rstd_of(ssum_ap, out_ap, P, n):
        S.op("ACT", lambda: nc.scalar.activation(out=out_ap, in_=ssum_ap, func=AF.Sqrt, bias=eps_c[0:P], scale=1.0 / n),
             [ssum_ap, eps_c[0:P]], [out_ap])
        recip(out_ap, out_ap)

    def postnorm_add(tc, i, pair, g):
        P = tc.P
        pv = psf(pair, P, 1024, nb=2)
        ss = ssr.next()
        act(junk[0:P, :], pv, AF.Square, accum=ss[0:P, 0:1])
        rstd_of(ss[0:P, 0:1], ss[0:P, 1:2], P, D)
        tf = tmpf.next()
        stt(tf[0:P, :], pv, ss[0:P, 1:2], g[0:P, :], ALU.mult, ALU.mult)
        tt("DVE", x_sb[0:P, i, :], x_sb[0:P, i, :], tf[0:P, :], ALU.add)

    def proj_out_tm(tc, l, wname, featT, gain_row):
        P = tc.P
        wv = wload(WB[wname][l].rearrange("(k p) n -> p k n", p=128), [128, 8, D], [f"{wname}{l}"])
        g = gload(gain_row)
        for i in range(tc.nsub):
            pair = bank(2)
            for n in range(2):
                for k in range(8):
                    mm(psf(pair + n, P), featT[:, k, i * P:(i + 1) * P], wv[:, k, n * 512:(n + 1) * 512], k == 0, k == 7)
            postnorm_add(tc, i, pair, g)

    A.reset(phase_mark)
    ucm_raw = A.alloc([128, 4096], BF16, "ucm")
    ucm = ucm_raw[:].rearrange("p (t f) -> p t f", f=512)
    ucmU = ucm_raw[:].rearrange("p (g s h) -> p g s h", s=8, h=16)
    ucmUf = ucm_raw[:].rearrange("p (g n) -> p g n", n=128)
    Uf = A.alloc([128, 32, 64], BF16, "Uf")
    bufA = A.alloc([128, 16, 2, 64], F32, "bufA")
    bufB = A.alloc([128, 16, 2, 64], F32, "bufB")
    bufT = A.alloc([128, 2, 16, 64], F32, "bufT")
    Xpb = A.alloc([128, 16, 2, 64], BF16, "Xpb")
    yfr = Ring([A.alloc([128, 8, 64], BF16, "yf") for _ in range(2)])
    zT = bufT[:].rearrange("p a b c -> p (a b c)").bitcast(BF16)[:, 0:2048].rearrange("p (k s c) -> p k s c", s=8, c=64)
    s5_mark_end = A.mark()
    A.reset(phase_mark)
    q_tm = A.alloc([128, 4, 512], BF16, "q_tm"); k_tm = A.alloc([128, 4, 512], BF16, "k_tm")
    v_tm = A.alloc([128, 4, 512], BF16, "v_tm"); sg_tm = A.alloc([128, 4, 512], BF16, "sg_tm")
    rtm = [A.alloc([128, 256], F32, "rtm") for _ in range(4)]
    qT_sb = A.alloc([128, 4, 128], BF16, "qT_sb"); kT_sb = A.alloc([128, 4, 128], BF16, "kT_sb")
    qdA = A.alloc([128, 4, 128], BF16, "qdA"); qdB = A.alloc([128, 4, 128], BF16, "qdB")
    kdA = A.alloc([128, 4, 128], BF16, "kdA"); kdB = A.alloc([128, 4, 128], BF16, "kdB")
    sc_sb = A.alloc([128, 4, 128], BF16, "sc_sb")
    o_f = A.alloc([128, 4, 128], F32, "o_f"); o_sq = A.alloc([128, 4, 128], F32, "o_sq")
    ret_tmr = Ring([A.alloc([128, 512], BF16, "ret_tm") for _ in range(2)])
    Stmp = A.alloc([128, 512], F32, "Stmp")
    ret_mark_end = A.mark()
    A.reset(phase_mark)
    pTr = Ring([A.alloc([128, 2, 512], BF16, "pT") for _ in range(2)])
    rinv = Ring([A.alloc([128, 512], F32, "rinv") for _ in range(2)])
    kb_x = A.alloc([128, 2, D], BF16, "kb_x")
    kvst = A.alloc([128, 4096], BF16, "kvst")
    kvf = Ring([A.alloc([128, D], F32, "kvf") for _ in range(2)])
    xat_mark_end = A.mark()
    A.reset(phase_mark)
    actT = A.alloc([128, 22, 512], BF16, "actT")
    sgr = Ring([A.alloc([128, 512], BF16, "sgate") for _ in range(2)])
    ffn_mark_end = A.mark()

    def s5_phase(tc, l):
        P, NTt, NC = tc.P, tc.NT, tc.NC
        wv = wload(WB["w_in"][l][:, 0:512].rearrange("(k p) n -> p k n", p=128), [128, 8, 512], [f"w_in{l}"])
        for s in range(8):
            b = bank()
            for k in range(8):
                mm(psf(b, NC), hT[:, k, s:NTt:8], wv[:, k, :], k == 0, k == 7)
            evac(ucmU[0:NC, :, s, :], psf(b, NC).rearrange("p (g h) -> p g h", h=16))
        for g0 in range(0, 32, 8):
            b = bank()
            pb = psb(b)
            for gl in range(8):
                g = g0 + gl
                tr(pb[:, gl * NC:(gl + 1) * NC], ucmUf[0:NC, g, :], idb[0:NC, 0:NC])
            evac(Uf[:, g0:g0 + 8, 0:NC], pb[:, 0:8 * NC].rearrange("p (g c) -> p g c", c=NC))
        vs = bufA
        for q4 in range(4):
            wV = wload(s5wV[l][:, q4 * 2048:(q4 + 1) * 2048].rearrange("p (g r n) -> p g r n", r=2, n=128),
                       [128, 8, 2, 128], [f"s5w{l}"])
            b = bank()
            pv = psf(b, 128, 8 * NC).rearrange("p (j r c) -> p j r c", r=2, c=NC)
            for jl in range(4):
                for ri in range(2):
                    for g2 in range(2):
                        gl = jl * 2 + g2
                        mm(pv[:, jl, ri, :], wV[:, gl, ri, :], Uf[:, q4 * 8 + gl, 0:NC], g2 == 0, g2 == 1)
            evac(vs[:, q4 * 4:q4 * 4 + 4, :, 0:NC], pv)
        rslot = wring.next()
        rt = rslot[:, 0:4096].bitcast(F32).rearrange("p (j r n) -> p j r n", r=2, n=64)
        for (c0, ln, t0) in tc.rot_segs:
            dma(rt[:, :, :, c0:c0 + ln], rotw[l].rearrange("p (j r n) -> p j r n", r=2, n=64)[:, :, :, t0:t0 + ln],
                reads=[S.dres(f"rot{l}")])
        cosv = rt[:, :, 0, 0:NC]; sinv = rt[:, :, 1, 0:NC]
        vre = vs[:, :, 0, 0:NC]; vim = vs[:, :, 1, 0:NC]
        c_ = bufB
        cre = c_[:, :, 0, 0:NC]; cim = c_[:, :, 1, 0:NC]
        T0 = bufT[:, 0, :, 0:NC]; T1 = bufT[:, 1, :, 0:NC]
        tt("DVE", T0, vre, cosv, ALU.mult)
        tt("DVE", T1, vim, sinv, ALU.mult)
        tt("DVE", cre, T0, T1, ALU.add)
        tt("DVE", T0, vim, cosv, ALU.mult)
        tt("DVE", T1, vre, sinv, ALU.mult)
        tt("DVE", cim, T0, T1, ALU.subtract)
        Wb = bufA
        for j in range(16):
            for ri in range(2):
                for (c0, ln, cin_ap) in tc.scan_segs(l, j, ri):
                    S.op("DVE", lambda o=Wb[:, j, ri, c0:c0 + ln], d1=c_[:, j, ri, c0:c0 + ln], ci=cin_ap:
                         nc.vector.tensor_tensor_scan(out=o, data0=Rall[:, l, j:j + 1].to_broadcast([128, ln]), data1=d1,
                                                      initial=ci, op0=ALU.mult, op1=ALU.add),
                         [c_[:, j, ri, c0:c0 + ln], Rall[:, l, j:j + 1], cin_ap], [Wb[:, j, ri, c0:c0 + ln]])
        wre = Wb[:, :, 0, 0:NC]; wim = Wb[:, :, 1, 0:NC]
        Xn = bufB
        xre = Xn[:, :, 0, 0:NC]; xim = Xn[:, :, 1, 0:NC]
        tt("DVE", T0, wre, cosv, ALU.mult)
        tt("DVE", T1, wim, sinv, ALU.mult)
        tt("DVE", xre, T0, T1, ALU.subtract)
        tt("DVE", T0, wre, sinv, ALU.mult)
        tt("DVE", T1, wim, cosv, ALU.mult)
        tt("DVE", xim, T0, T1, ALU.add)
        tc.scan_finish(l, Xn, Xpb)
        for q4 in range(4):
            wY = wload(s5wY[l][:, q4 * 3072:(q4 + 1) * 3072].rearrange("p (g r n) -> p g r n", r=3, n=128),
                       [128, 8, 3, 128], [f"s5w{l}"])
            b = bank()
            pv = psf(b, 128, 8 * NC).rearrange("p (g c) -> p g c", c=NC)
            for gl in range(8):
                g = q4 * 8 + gl
                j = g // 2
                mm(pv[:, gl, :], wY[:, gl, 0, :], Uf[:, g, 0:NC], True, False)
                mm(pv[:, gl, :], wY[:, gl, 1, :], Xpb[:, j, 0, 0:NC], False, False)
                mm(pv[:, gl, :], wY[:, gl, 2, :], Xpb[:, j, 1, 0:NC], False, True)
            yf = yfr.next()
            evac(yf[:, :, 0:NC], pv)
            b2 = bank()
            pb = psb(b2, NC)
            for gl in range(8):
                tr(pb[:, gl * 128:(gl + 1) * 128], yf[:, gl, 0:NC], idb[:, :])
            act(ucm[0:NC, :, q4 * 128:(q4 + 1) * 128].rearrange("c t (g h) -> c t g h", h=16),
                pb[:, 0:1024].rearrange("c (g t h) -> c t g h", t=8, h=16), AF.Gelu_apprx_tanh)
        for kc in range(4):
            b = bank()
            pb = psb(b)
            for s in range(8):
                tr(pb[:, s * NC:(s + 1) * NC], ucm[0:NC, s, kc * 128:(kc + 1) * 128], idb[0:NC, 0:NC])
            evac(zT[:, kc, :, 0:NC], pb[:, 0:8 * NC].rearrange("p (s c) -> p s c", c=NC))
        wg = wload(WB["s5_w_glu"][l].rearrange("(k p) n -> p k n", p=128), [128, 4, 512], [f"s5_w_glu{l}"])
        bgl = gload(W["s5_b_glu"][l:l + 1, :])
        g5 = gload(W["s5_out_norm"][l:l + 1, :])
        s5o_cm = bufA[:].rearrange("p a b c -> p (a b c)").bitcast(BF16)[:, 0:4096].rearrange("p (s n) -> p s n", n=512)
        gtmp = bufB[:].rearrange("p a b c -> p (a b c)")
        gdump = bufT[:].rearrange("p a b c -> p (a b c)")[0:NC, 1024:1536]
        for s0 in (0, 4):
            bs = [bank() for _ in range(4)]
            gas = [gtmp[0:NC, q * 512:(q + 1) * 512] for q in range(4)]
            sss = [ssr.next() for _ in range(4)]
            for q in range(4):
                for kc in range(4):
                    mm(psf(bs[q], NC), zT[:, kc, s0 + q, 0:NC], wg[:, kc, :], kc == 0, kc == 3)
            for q in range(4):
                tt("DVE", gas[q], psf(bs[q], NC), bgl[0:NC, 0:512], ALU.add)
            for q in range(4):
                act(gas[q], gas[q], AF.Sigmoid)
            for q in range(4):
                tt("DVE", gas[q], gas[q], ucm[0:NC, s0 + q, :], ALU.mult)
            for q in range(4):
                stt(gdump, gas[q], 1.0, gas[q], ALU.mult, ALU.mult, accum=sss[q][0:NC, 0:1])
            for q in range(4):
                S.op("ACT", lambda q=q: nc.scalar.activation(out=sss[q][0:NC, 1:2], in_=sss[q][0:NC, 0:1], func=AF.Sqrt,
                                                              bias=eps_c[0:NC], scale=1.0 / 512),
                     [sss[q][0:NC, 0:1], eps_c[0:NC]], [sss[q][0:NC, 1:2]])
            for q in range(4):
                recip(sss[q][0:NC, 1:2], sss[q][0:NC, 1:2])
            for q in range(4):
                stt(s5o_cm[0:NC, s0 + q, :], gas[q], sss[q][0:NC, 1:2], g5[0:NC, 0:512], ALU.mult, ALU.mult)
        for kc in range(4):
            b = bank()
            pb = psb(b)
            for s in range(8):
                tr(pb[:, s * NC:(s + 1) * NC], s5o_cm[0:NC, s, kc * 128:(kc + 1) * 128], idb[0:NC, 0:NC])
            evac(featA[:, kc, 0:NTt].rearrange("p (c s) -> p s c", s=8), pb[:, 0:8 * NC].rearrange("p (s c) -> p s c", c=NC))

    def ret_phase(tc, l):
        P, NTt = tc.P, tc.NT
        gate_g = gload(W["ret_out_norm"][l:l + 1, :])
        for n in (3, 4, 1, 2):
            wv = wload(WB["w_in"][l][:, n * 512:(n + 1) * 512].rearrange("(k p) n -> p k n", p=128), [128, 8, 512], [f"w_in{l}"])
            for i in range(tc.nsub):
                b = bank()
                for k in range(8):
                    mm(psf(b, P), hT[:, k, i * P:(i + 1) * P], wv[:, k, :], k == 0, k == 7)
                pv = psf(b, P)
                if n in (1, 2):
                    dst = (q_tm if n == 1 else k_tm)[0:P, i, :].rearrange("p (h t d) -> p h t d", t=2, d=64)
                    src = pv.rearrange("p (h t d) -> p h t d", t=2, d=64)
                    co = (0 if n == 1 else 2)
                    cosb = rope[0:P, i, co * 64:(co + 1) * 64].unsqueeze(1).to_broadcast([P, 4, 64])
                    sinb = rope[0:P, i, (co + 1) * 64:(co + 2) * 64].unsqueeze(1).to_broadcast([P, 4, 64])
                    r4 = [r[0:P, :].rearrange("p (h d) -> p h d", d=64) for r in rtm]
                    tt("DVE", r4[0], src[:, :, 0, :], cosb, ALU.mult)
                    tt("DVE", r4[1], src[:, :, 1, :], sinb, ALU.mult)
                    tt("DVE", r4[2], src[:, :, 0, :], sinb, ALU.mult)
                    tt("DVE", r4[3], src[:, :, 1, :], cosb, ALU.mult)
                    tt("DVE", dst[:, :, 0, :], r4[0], r4[1], ALU.subtract)
                    tt("DVE", dst[:, :, 1, :], r4[2], r4[3], ALU.add)
                elif n == 3:
                    cp("ACT", v_tm[0:P, i, :], pv)
                else:
                    act(sg_tm[0:P, i, :], pv, AF.Silu)
        dm, kd, qd, gt_ = tc.ret_consts
        pend = [None]

        def ret_final(i, rtm_i):
            b = bank()
            pb = psb(b)
            for h in range(4):
                tr(pb[:, h * P:(h + 1) * P], rtm_i[0:P, h * 128:(h + 1) * 128], idb[0:P, 0:P])
            evac(featA[:, 4:8, i * P:(i + 1) * P], pb[:, 0:4 * P].rearrange("p (h c) -> p h c", c=P))
        for i in range(tc.nsub):
            b = bank()
            pb = psb(b)
            pq = pb[:, 0:4 * P].rearrange("p (h c) -> p h c", c=P)
            pk = pb[:, 4 * P:8 * P].rearrange("p (h c) -> p h c", c=P)
            for h in range(4):
                tr(pq[:, h, :], q_tm[0:P, i, h * 128:(h + 1) * 128], idb[0:P, 0:P])
                tr(pk[:, h, :], k_tm[0:P, i, h * 128:(h + 1) * 128], idb[0:P, 0:P])
            cp("ACT", qT_sb[:, :, 0:P], pq)
            cp("ACT", kT_sb[:, :, 0:P], pk)
            qdv = qd[:, 0:8 * P].rearrange("p (h t c) -> p h t c", t=2, c=P)
            tt("DVE", qdA[:, :, 0:P], pq, qdv[:, :, 0, :], ALU.mult)
            tt("DVE", qdB[:, :, 0:P], pq, qdv[:, :, 1, :], ALU.mult)
            kv4 = k_tm[0:P, i, :].rearrange("p (h d) -> p h d", d=128)
            kdv = kd[0:P, 0:8].rearrange("p (h t) -> p h t", t=2)
            tt("DVE", kdA[0:P], kv4, kdv[:, :, 0:1].to_broadcast([P, 4, 128]), ALU.mult)
            tt("DVE", kdB[0:P], kv4, kdv[:, :, 1:2].to_broadcast([P, 4, 128]), ALU.mult)
            b = bank()
            psc = psf(b, P, 4 * P).rearrange("p (h c) -> p h c", c=P)
            for h in range(4):
                mm(psc[:, h, :], kT_sb[:, h, 0:P], qT_sb[:, h, 0:P])
            tt("DVE", sc_sb[0:P, :, 0:P], psc, dm[0:P, 0:4 * P].rearrange("p (h c) -> p h c", c=P), ALU.mult)
            bA = bank(); bB = bank()
            pkA = psf(bA).rearrange("p (h e) -> p h e", e=128)
            pkB = psf(bB).rearrange("p (h e) -> p h e", e=128)
            for h in range(4):
                mm(pkA[:, h, :], kdA[0:P, h, :], v_tm[0:P, i, h * 128:(h + 1) * 128])
                mm(pkB[:, h, :], kdB[0:P, h, :], v_tm[0:P, i, h * 128:(h + 1) * 128])
            SA_in, SB_in = tc.ret_states(l, i, psf(bA), psf(bB), gt_)
            b = bank()
            po = psf(b, P).rearrange("p (h e) -> p h e", e=128)
            for h in range(4):
                mm(po[:, h, :], sc_sb[0:P, h, 0:P], v_tm[0:P, i, h * 128:(h + 1) * 128], True, False)
                mm(po[:, h, :], qdA[:, h, 0:P], SA_in[:, h * 128:(h + 1) * 128], False, False)
                mm(po[:, h, :], qdB[:, h, 0:P], SB_in[:, h * 128:(h + 1) * 128], False, True)
            cp("ACT", o_f[0:P], po)
            tt("DVE", o_sq[0:P], o_f[0:P], o_f[0:P], ALU.mult)
            ss = ssr.next()
            S.op("DVE", lambda: nc.vector.tensor_reduce(out=ss[0:P, 0:4], in_=o_sq[0:P], axis=AX.X, op=ALU.add),
                 [o_sq[0:P]], [ss[0:P, 0:4]])
            rstd_of(ss[0:P, 0:4], ss[0:P, 4:8], P, 128)
            tt("DVE", o_f[0:P], o_f[0:P], ss[0:P, 4:8].unsqueeze(2).to_broadcast([P, 4, 128]), ALU.mult)
            of2 = o_f[0:P].rearrange("p h e -> p (h e)")
            tt("DVE", of2, of2, gate_g[0:P, 0:512], ALU.mult)
            rtm_i = ret_tmr.next()
            tt("DVE", rtm_i[0:P, :], of2, sg_tm[0:P, i, :], ALU.mult)
            if pend[0] is not None:
                ret_final(*pend[0])
            pend[0] = (i, rtm_i)
        ret_final(*pend[0])

    def mixer(tc, l):
        norm_T(tc, W["mix_norm_pre"][l:l + 1, :])
        s5_phase(tc, l)
        ret_phase(tc, l)
        proj_out_tm(tc, l, "w_out", featA, W["mix_norm_post"][l:l + 1, :])

    def prep_kt(kb, dst):
        for mc in range(2):
            b = bank()
            pb = psb(b)
            for ch in range(8):
                tr(pb[:, ch * 128:(ch + 1) * 128], kb[:, mc, ch * 128:(ch + 1) * 128], idb[:, :])
            evac(dst[:, :, mc * 128:(mc + 1) * 128], pb[:, 0:1024].rearrange("p (k c) -> p k c", c=128))

    def xattn(tc, l):
        P, NTt = tc.P, tc.NT
        norm_T(tc, W["xattn_norm_pre"][l:l + 1, :])
        wv = wload(WB["w_cq"][l].rearrange("(k p) n -> p k n", p=128), [128, 8, D], [f"w_cq{l}"])
        for ch in range(8):
            b = bank()
            for k in range(8):
                mm(psf(b, 128, NTt), wv[:, k, ch * 128:(ch + 1) * 128], hT[:, k, 0:NTt], k == 0, k == 7)
            evac(featB[:, ch, 0:NTt], psf(b, 128, NTt))
        for (c0, N, KT, V) in tc.kv_groups(l):
            pts = {}

            def scores(h, c0=c0, N=N, KT=KT):
                pT = pTr.next()
                for mc in range(2):
                    b = bank()
                    for dc in range(2):
                        mm(psf(b, 128, N), KT[:, 2 * h + dc, mc * 128:(mc + 1) * 128], featB[:, 2 * h + dc, c0:c0 + N], dc == 0, dc == 1)
                    act(pT[:, mc, 0:N], psf(b, 128, N), AF.Exp, scale=1.0 / 16)
                pts[h] = pT

            def pv(h, c0=c0, N=N, V=V):
                pT = pts[h]
                b = bank()
                for mc in range(2):
                    mm(psf(b, 128, N), onesb[:, :], pT[:, mc, 0:N], mc == 0, mc == 1)
                ri = rinv.next()
                recip(ri[:, 0:N], psf(b, 128, N))
                for dc in range(2):
                    b = bank()
                    for mc in range(2):
                        mm(psf(b, 128, N), V[:, mc, (2 * h + dc) * 128:(2 * h + dc + 1) * 128], pT[:, mc, 0:N], mc == 0, mc == 1)
                    tt("DVE", featA[:, 2 * h + dc, c0:c0 + N], psf(b, 128, N), ri[:, 0:N], ALU.mult)

            scores(0)
            for h in range(4):
                if h + 1 < 4:
                    scores(h + 1)
                pv(h)
        proj_out_tm(tc, l, "w_co", featA, W["xattn_norm_post"][l:l + 1, :])

    def ffn(tc, l):
        P, NTt = tc.P, tc.NT
        norm_T(tc, W["ffn_norm_pre"][l:l + 1, :])
        for cb in range(6):
            wcols = 512 if cb < 5 else 256
            wg = wload(WB["w_gate"][l][:, cb * 512:cb * 512 + wcols].rearrange("(k p) n -> p k n", p=128), [128, 8, wcols], [f"w_gate{l}"])
            wu = wload(WB["w_up"][l][:, cb * 512:cb * 512 + wcols].rearrange("(k p) n -> p k n", p=128), [128, 8, wcols], [f"w_up{l}"])
            for f in range(wcols // 128):
                ffc = cb * 4 + f
                bg = bank(); bu = bank()
                for k in range(8):
                    mm(psf(bg, 128, NTt), wg[:, k, f * 128:(f + 1) * 128], hT[:, k, 0:NTt], k == 0, k == 7)
                for k in range(8):
                    mm(psf(bu, 128, NTt), wu[:, k, f * 128:(f + 1) * 128], hT[:, k, 0:NTt], k == 0, k == 7)
                sg = sgr.next()
                act(sg[:, 0:NTt], psf(bg, 128, NTt), AF.Silu)
                tt("DVE", actT[:, ffc, 0:NTt], psf(bu, 128, NTt), sg[:, 0:NTt], ALU.mult)
        g = gload(W["ffn_norm_post"][l:l + 1, :])
        bank_rr[0] = 0
        for fb in range(0, 22, 4):
            nf = min(4, 22 - fb)
            wd = wload(WB["w_down"][l][fb * 128:(fb + nf) * 128, :].rearrange("(f p) n -> p f n", p=128), [128, nf, D], [f"w_down{l}"])
            for fl_ in range(nf):
                ffc = fb + fl_
                for i in range(tc.nsub):
                    for n in range(2):
                        mm(psf(2 * i + n, P), actT[:, ffc, i * P:(i + 1) * P], wd[:, fl_, n * 512:(n + 1) * 512], ffc == 0, ffc == 21)
        for i in range(tc.nsub):
            postnorm_add(tc, i, 2 * i, g)
        bank_rr[0] = 0

    def mem_kv():
        tcm = TC(); tcm.P = 128; tcm.nsub = 2; tcm.NT = 256
        dma(x_sb[:, 0:2, :], mem.rearrange("(i p) d -> p i d", p=128))
        for l in range(nl):
            norm_T(tcm, W["mem_norm"][l:l + 1, :])
            ck(f"mk_norm{l}")
            for (wn, outd, isk) in (("w_ck", o_memk, True), ("w_cv", o_memv, False)):
                wv = wload(WB[wn][l].rearrange("(k p) n -> p k n", p=128), [128, 8, D], [f"{wn}{l}"])
                ck(f"mk_w{l}")
                for i in range(2):
                    pair = bank(2)
                    for n in range(2):
                        for k in range(8):
                            mm(psf(pair + n), hT[:, k, i * 128:(i + 1) * 128], wv[:, k, n * 512:(n + 1) * 512], k == 0, k == 7)
                    ck(f"mk_mm{l}")
                    kf = kvf.next()
                    for n in range(2):
                        pv = psf(pair + n)
                        cp("ACT", kf[:, n * 512:(n + 1) * 512], pv)
                        ck(f"mk_cp{l}")
                        if isk:
                            cp("DVE", kb_x[:, i, n * 512:(n + 1) * 512], pv)
                            ck(f"mk_dcp{l}")
                        else:
                            cp("DVE", kvst[:, 2048 + i * 1024 + n * 512:2048 + i * 1024 + (n + 1) * 512], pv)
                    if stop == f"mk_dma{l}" and n == 1:
                        ck(f"mk_dma{l}")
                    dma(outd[l, i * 128:(i + 1) * 128, :], kf[:])
                    ck(f"mk_dmb{l}")
                ck(f"mk_proj{l}{wn}")
                if isk:
                    prep_kt(kb_x, kvst[:, 0:2048].rearrange("p (k m) -> p k m", m=256))
                    ck(f"mk_kt{l}")
            dma(kvs[l], kvst[:], writes=[S.dres(f"kvs{l}")])

    def prompt_tc(ti):
        tc = TC()
        tc.P = 128; tc.nsub = 4; tc.NT = 512; tc.NC = 64
        tc.rot_segs = [(0, 64, 0)]
        tc.ret_consts = (dmP, kdP, qdP, None)
        gch = [float(np.float32(np.exp(np.float32(64.0) * np.log(np.float32(1.0 - 2.0 ** (-5.0 - h)))))) for h in range(4)]

        def scan_segs(l, j, ri):
            return [(0, 64, car[:, l, ri, j:j + 1])]
        tc.scan_segs = scan_segs

        def scan_finish(l, Xn, Xp):
            cp("ACT", Xp[:, :, :, 0:1], car[:, l].rearrange("p r j -> p j r").unsqueeze(3))
            cp("ACT", Xp[:, :, :, 1:64], Xn[:, :, :, 0:63])
            cp("DVE", car[:, l].rearrange("p r j -> p j r").unsqueeze(3), Xn[:, :, :, 63:64])
        tc.scan_finish = scan_finish

        def ret_states(l, i, pkA, pkB, _):
            sA = Sbf.next()
            cp("ACT", sA[:], Sst[:, l, :])
            gp = gP
            tt("DVE", Stmp[:], Sst[:, l, :], gp[:, :], ALU.mult)
            tt("DVE", Sst[:, l, :], Stmp[:], pkA, ALU.add)
            sB = Sbf.next()
            cp("ACT", sB[:], Sst[:, l, :])
            tt("DVE", Stmp[:], Sst[:, l, :], gp[:, :], ALU.mult)
            tt("DVE", Sst[:, l, :], Stmp[:], pkB, ALU.add)
            return sA, sB
        tc.ret_states = ret_states

        def kv_groups(l):
            slot = wload(kvs[l], [128, 4096], [f"kvs{l}"])
            KT = slot[:, 0:2048].rearrange("p (k m) -> p k m", m=256)
            V = slot[:, 2048:4096].rearrange("p (c n) -> p c n", n=1024)
            return [(0, 512, KT, V)]
        tc.kv_groups = kv_groups
        return tc

    def sample_tc():
        tc = TC()
        tc.P = 32; tc.nsub = 1; tc.NT = 32; tc.NC = 4
        tc.rot_segs = [(0, 2, 0), (2, 2, 0)]
        tc.ret_consts = (dmS, kdS, qdS, None)

        def scan_segs(l, j, ri):
            return [(0, 2, cars[:, 0, ri, j:j + 1]), (2, 2, cars[:, 1, ri, j:j + 1])]
        tc.scan_segs = scan_segs

        def scan_finish(l, Xn, Xp):
            for sq_ in range(2):
                cv = cars[:, sq_].rearrange("p r j -> p j r").unsqueeze(3)
                cp("ACT", Xp[:, :, :, 2 * sq_:2 * sq_ + 1], cv)
                cp("ACT", Xp[:, :, :, 2 * sq_ + 1:2 * sq_ + 2], Xn[:, :, :, 2 * sq_:2 * sq_ + 1])
                for ri, od in ((0, o_s5re_s), (1, o_s5im_s)):
                    dma(od[l, sq_].rearrange("(j q) -> q j", q=128), Xn[:, :, ri, 2 * sq_ + 1], slow=True)
        tc.scan_finish = scan_finish

        def ret_states(l, i, pkA, pkB, _):
            outs = []
            for sq_, pk in ((0, pkA), (1, pkB)):
                sb = Sbf.next()
                cp("ACT", sb[:], Ssm[:, sq_, :])
                tt("DVE", Stmp[:], Ssm[:, sq_, :], gS[:, :], ALU.mult)
                tf = tmpf.next()
                tt("DVE", tf[:, 0:512], Stmp[:], pk, ALU.add)
                dma(o_ret_s[l, sq_].rearrange("h d e -> d h e"), tf[:, 0:512].rearrange("p (h e) -> p h e", e=128))
                outs.append(sb)
            return outs[0], outs[1]
        tc.ret_states = ret_states

        def kv_groups(l):
            res = []
            for sq_ in range(2):
                S.dma("POOL", kb_x[:], cmk[l, sq_].rearrange("(c p) d -> p c d", p=128))
                slot = wring.next()
                KT = slot[:, 0:2048].rearrange("p (k m) -> p k m", m=256)
                V = slot[:, 2048:4096].rearrange("p (c n) -> p c n", n=1024)
                prep_kt(kb_x, KT)
                S.dma("POOL", V, cmv[l, sq_].rearrange("(c p) d -> p c d", p=128))
                res.append((sq_ * 16, 16, KT, V))
            return res
        tc.kv_groups = kv_groups
        return tc

    def program():
        ck("pre")
        for l in range(nl if stop is None or not stop.startswith("mk_") else 0):
            s5_setup(l)
            ck(f"setup{l}")
        A.reset(ffn_mark_end)
        mem_kv()
        ck("memkv")

        def run_tile(tc, load_fn, store_fn, pre_layer=None, tag=""):
            load_fn()
            for l in range(nl):
                if pre_layer is not None:
                    pre_layer(l)
                norm_T(tc, W["mix_norm_pre"][l:l + 1, :])
                ck(f"{tag}norm{l}")
                s5_phase(tc, l)
                ck(f"{tag}s5{l}")
                ret_phase(tc, l)
                ck(f"{tag}ret{l}")
                proj_out_tm(tc, l, "w_out", featA, W["mix_norm_post"][l:l + 1, :])
                ck(f"{tag}mix{l}")
                xattn(tc, l)
                ck(f"{tag}xat{l}")
                ffn(tc, l)
                ck(f"{tag}ffn{l}")
            store_fn()

        for ti in range(n_tiles):
            tc = prompt_tc(ti)
            t0 = ti * NT

            def load_fn(t0=t0):
                dma(x_sb[:], xp[t0:t0 + NT, :].rearrange("(i p) d -> p i d", p=128))
                dma(rope[:], c_ropeP[t0:t0 + NT, :].rearrange("(i p) d -> p i d", p=128))

            def store_fn(t0=t0):
                dma(yp[t0:t0 + NT, :].rearrange("(i p) d -> p i d", p=128), x_sb[:])
            run_tile(tc, load_fn, store_fn, tag=f"t{ti}")
        for l in range(nl):
            dma(o_s5re_p[l].rearrange("(j q) -> q j", q=128), car[:, l, 0, :], slow=True)
            dma(o_s5im_p[l].rearrange("(j q) -> q j", q=128), car[:, l, 1, :], slow=True)
            dma(o_ret_p[l].rearrange("h d e -> d h e"), Sst[:, l, :].rearrange("p (h e) -> p h e", e=128))
        ck("prompt")
        if with_sample:
            tc = sample_tc()

            def load_s():
                dma(x_sb[0:32, 0, :], xs)
                dma(rope[0:32, 0, :], c_ropeS)

            def store_s():
                dma(ys, x_sb[0:32, 0, :])

            def pre_layer(l):
                for sq_ in range(2):
                    dma(Ssm[:, sq_, :].rearrange("p (h e) -> p h e", e=128), ret0[l, sq_].rearrange("h d e -> d h e"))
                    dma(cars[:, sq_, 0, :], s5re0[l, sq_].rearrange("(j g2) p -> (g2 p) j", g2=2), slow=True)
                    dma(cars[:, sq_, 1, :], s5im0[l, sq_].rearrange("(j g2) p -> (g2 p) j", g2=2), slow=True)
            run_tile(tc, load_s, store_s, pre_layer, tag="s")

    try:
        program()
    except _Stop:
        pass
    if stop is not None:
        dma(yp[0:NT, :].rearrange("(i p) d -> p i d", p=128), x_sb[:])
    S.finish()
    return nc, S, A


def _consts(T):
    bf = ml_dtypes.bfloat16
    c = {}
    c["c_idb"] = np.eye(128, dtype=np.float32).astype(bf)
    c["c_idf"] = np.eye(128, dtype=np.float32)
    s_idx = np.arange(128) // 16
    c["c_maskM"] = (s_idx[None, :] >= s_idx[:, None]).astype(np.float32)
    misc = np.zeros((128, 8), np.float32)
    misc[:64, 0] = 1; misc[64:, 1] = 1; misc[:64, 2] = -1; misc[64:, 3] = -1
    misc[:, 4] = EPS; misc[:, 5] = np.pi / 2; misc[:, 6] = 1.0
    c["c_misc"] = misc
    half = 64
    inv = (np.float32(10000.0) ** (-np.arange(half, dtype=np.float32) / np.float32(half))).astype(np.float32)

    def rope_tab(pos):
        ang = pos.astype(np.float32)[:, None] * inv[None, :]
        cs, sn = np.cos(ang).astype(np.float32), np.sin(ang).astype(np.float32)
        qs = np.float32(128.0 ** -0.5)
        return np.concatenate([cs * qs, sn * qs, cs, sn], axis=1).astype(np.float32)
    c["c_ropeP"] = rope_tab(np.arange(T))
    c["c_ropeS"] = rope_tab(np.concatenate([1024 + np.arange(16), 1024 + np.arange(16)]))
    lg = np.log((1.0 - 2.0 ** (-5.0 - np.arange(4, dtype=np.float32))).astype(np.float32)).astype(np.float32)

    def ret_tabs(P, C):
        idx = np.arange(P)
        ch = idx // C
        pos = (idx % C).astype(np.float32)
        dm = np.zeros((P, 4, P), np.float32)
        kd = np.zeros((P, 4, 2), np.float32)
        qd = np.zeros((4, 2, P), np.float32)
        for h in range(4):
            d = np.exp(lg[h] * np.abs(pos[:, None] - pos[None, :])).astype(np.float32)
            dm[:, h, :] = d * (ch[:, None] == ch[None, :])
            kdec = np.exp(lg[h] * (C - 1.0 - pos)).astype(np.float32)
            qdec = np.exp(lg[h] * (pos + 1.0)).astype(np.float32)
            for t in range(2):
                kd[:, h, t] = kdec * (ch == t)
                qd[h, t, :] = qdec * (ch == t)
        g = np.exp(lg * np.float32(C)).astype(np.float32)
        gt = np.repeat(g, 128)[None, :].astype(np.float32)
        return dm.reshape(P, 4 * P), kd.reshape(P, 8), qd.reshape(1, 8 * P), gt
    c["c_dmP"], c["c_kdP"], c["c_qdP"], c["c_gP"] = ret_tabs(128, 64)
    c["c_dmS"], c["c_kdS"], c["c_qdS"], c["c_gS"] = ret_tabs(32, 16)
    return c


_CACHE = {}


def _get(T, with_sample=True, nl=NL, stop=None):
    key = (T, with_sample, nl, stop)
    if key not in _CACHE:
        _CACHE[key] = build(T, with_sample, nl, stop)
    return _CACHE[key]


def run_cores(inputs, T, n_cores=8, with_sample=True, nl=NL, stop=None, trace=False):
    nc, S, A = _get(T, with_sample, nl, stop)
    consts = _consts(T)
    f = lambda a: np.ascontiguousarray(np.asarray(a, dtype=np.float32))
    in_maps = []
    wnames = ["w_in", "s5_w_glu", "w_out", "w_cq", "w_ck", "w_cv", "w_co", "w_gate", "w_up", "w_down",
              "mix_norm_pre", "mix_norm_post", "s5_lambda_re", "s5_lambda_im", "s5_log_step", "s5_b_re", "s5_b_im",
              "s5_c_re", "s5_c_im", "s5_d", "s5_b_glu", "s5_out_norm", "xattn_norm_pre", "xattn_norm_post", "mem_norm",
              "ffn_norm_pre", "ffn_norm_post"]
    shared = {k: f(inputs[k]) for k in wnames}
    shared["ret_out_norm"] = f(inputs["ret_out_norm"]).reshape(NL, 512)
    shared.update(consts)
    for c in range(n_cores):
        b = c % 4
        m = dict(shared)
        m["xp"] = f(inputs["x_prompt"][b, :T])
        m["xs"] = f(inputs["x_sample"][2 * c:2 * c + 2]).reshape(32, D)
        m["mem"] = f(inputs["mem_prompt"][b])
        m["s5re0"] = f(inputs["state_s5_re"][:, 2 * c:2 * c + 2])
        m["s5im0"] = f(inputs["state_s5_im"][:, 2 * c:2 * c + 2])
        m["ret0"] = f(inputs["state_ret"][:, 2 * c:2 * c + 2])
        m["cmk"] = f(inputs["cache_mem_k"][:, 2 * c:2 * c + 2]).reshape(NL, 2, NMEM, D)
        m["cmv"] = f(inputs["cache_mem_v"][:, 2 * c:2 * c + 2]).reshape(NL, 2, NMEM, D)
        in_maps.append(m)
    if trace:
        res = run_bass_kernel_spmd(nc, in_maps, core_ids=list(range(n_cores)), trace=True)
        print("EXEC_NS", res.exec_time_ns, flush=True)
    else:
        res = run_bass_kernel_spmd(nc, in_maps, core_ids=list(range(n_cores)))
    return res.results


def kernel(**inputs):
    T = 8192
    r = run_cores(inputs, T)
    y_prompt = np.stack([r[b]["yp"] for b in range(4)]).astype(np.float32)
    y_sample = np.concatenate([r[c]["ys"].reshape(2, 16, D) for c in range(8)]).astype(np.float32)
    s5re_p = np.stack([r[b]["o_s5re_p"].reshape(NL, 32, 64) for b in range(4)], axis=1)
    s5im_p = np.stack([r[b]["o_s5im_p"].reshape(NL, 32, 64) for b in range(4)], axis=1)
    ret_p = np.stack([r[b]["o_ret_p"] for b in range(4)], axis=1)
    memk = np.stack([r[b]["o_memk"].reshape(NL, NMEM, 4, 256) for b in range(4)], axis=1)
    memv = np.stack([r[b]["o_memv"].reshape(NL, NMEM, 4, 256) for b in range(4)], axis=1)
    s5re_s = np.concatenate([r[c]["o_s5re_s"].reshape(NL, 2, 32, 64) for c in range(8)], axis=1)
    s5im_s = np.concatenate([r[c]["o_s5im_s"].reshape(NL, 2, 32, 64) for c in range(8)], axis=1)
    ret_s = np.concatenate([r[c]["o_ret_s"] for c in range(8)], axis=1)
    return (y_prompt, y_sample, s5re_p.astype(np.float32), s5im_p.astype(np.float32), ret_p.astype(np.float32),
            memk.astype(np.float32), memv.astype(np.float32), s5re_s.astype(np.float32), s5im_s.astype(np.float32),
            ret_s.astype(np.float32))
```

```python
import math
import numpy as np
import ml_dtypes
import concourse.bass as bass
import concourse.mybir as mybir
from concourse.bass_utils import run_bass_kernel_spmd

F32 = mybir.dt.float32
BF16 = mybir.dt.bfloat16
AF = mybir.ActivationFunctionType
ALU = mybir.AluOpType
AX = mybir.AxisListType
DSZ = {F32: 4, BF16: 2}

D = 1024
DIN = 2560
DFF = 2816
NMEM = 256
NL = 4
EPS = 1e-6
SB_BASE = 16640
SB_END = 229376 - 64
ATOM = 256
NDMA = 32
NDMA_SP = 24
SAME_ENGINE_SMALL_ONLY = False
EMBED_WAIT = True


class Sched:
    def __init__(self, nc, sems):
        self.nc = nc
        self.names = ["PE", "ACT", "DVE", "POOL", "SP"] + [f"D{i}" for i in range(NDMA)]
        self.idx = {n: i for i, n in enumerate(self.names)}
        self.eng = {"PE": nc.tensor, "ACT": nc.scalar, "DVE": nc.vector, "POOL": nc.gpsimd, "SP": nc.sync}
        self.sem = sems
        ne = len(self.names)
        self.cnt = np.zeros(ne, np.int64)
        self.mult = np.array([1] * 5 + [16] * NDMA, np.int64)
        self.seen = np.zeros((5, ne), np.int64)
        self.n_sb = (SB_END + ATOM - 1) // ATOM
        self.n_ps = 8 * 2048 // ATOM
        self.n_atoms = self.n_sb + self.n_ps
        self.dram = {}
        cap = self.n_atoms + 256
        self.lw_eng = -np.ones(cap, np.int64)
        self.lw_cnt = np.zeros(cap, np.int64)
        self.rd = np.zeros((cap, ne), np.int64)
        self.dma_rr = 0
        self.dma_rr_sw = 0
        self.ninst = 0

    def dres(self, name):
        if name not in self.dram:
            self.dram[name] = self.n_atoms + len(self.dram)
        a = self.dram[name]
        return (a, a + 1)

    def rng(self, ap):
        if isinstance(ap, tuple):
            return ap
        t = ap.tensor
        cls = type(t).__name__
        if cls.startswith("DRam"):
            return None
        dsz = DSZ[ap.dtype]
        row = 1
        for s in list(t.shape)[1:]:
            row *= int(s)
        col0 = int(ap.offset) % row
        ext = 1
        for st, c in list(ap.ap)[1:]:
            ext += (int(c) - 1) * abs(int(st))
        lo = col0 * dsz
        hi = (col0 + ext) * dsz
        if cls.startswith("SB"):
            base = int(t.manual_sbuf_range[0])
            return ((base + lo) // ATOM, (base + hi + ATOM - 1) // ATOM)
        return (self.n_sb + (lo // 2048) * (2048 // ATOM), self.n_sb + ((hi + 2047) // 2048) * (2048 // ATOM))

    def _deps(self, reads, writes):
        deps = np.zeros(len(self.names), np.int64)
        for r in reads:
            a, b = r
            le = self.lw_eng[a:b]
            m = le >= 0
            if m.any():
                np.maximum.at(deps, le[m], self.lw_cnt[a:b][m])
        for w in writes:
            a, b = w
            le = self.lw_eng[a:b]
            m = le >= 0
            if m.any():
                np.maximum.at(deps, le[m], self.lw_cnt[a:b][m])
            np.maximum(deps, self.rd[a:b].max(axis=0), out=deps)
        return deps

    def _waits(self, e, deps, embed=False):
        ei = self.idx[e]
        need = [o for o in np.nonzero(deps > self.seen[ei])[0] if not (o == ei and e == "PE")]
        last = None
        if embed and EMBED_WAIT and need:
            last = need.pop()
        for o in need:
            self.eng[e].wait_ge(self.sem[o], int(deps[o] * self.mult[o]))
            self.seen[ei, o] = deps[o]
        if last is not None:
            self.seen[ei, last] = deps[last]
            return (self.sem[last], int(deps[last] * self.mult[last]))
        return None

    def _small(self, ap):
        if isinstance(ap, tuple):
            return False
        n = 1
        for st, c in list(ap.ap)[1:]:
            n *= int(c)
        return n * DSZ[ap.dtype] <= 64

    def op(self, e, fn, reads, writes):
        sm_r = [self.rng(r) for r in reads if self._small(r)]
        sm_w = [self.rng(w) for w in writes if self._small(w)]
        reads = [x for x in (self.rng(r) for r in reads) if x is not None]
        writes = [x for x in (self.rng(w) for w in writes) if x is not None]
        deps = self._deps(reads, writes)
        ei = self.idx[e]
        if SAME_ENGINE_SMALL_ONLY and e != "PE":
            dsm = self._deps([x for x in sm_r if x is not None], [x for x in sm_w if x is not None])
            deps[ei] = dsm[ei]
        for a, b in reads:
            if a >= self.n_sb and a < self.n_atoms:
                extra = self.rd[a:b].max(axis=0).copy()
                extra[ei] = 0
                np.maximum(deps, extra, out=deps)
        emb = self._waits(e, deps, embed=True)
        inst = fn()
        if emb is not None:
            inst._wait_ge(emb[0], emb[1])
        self.cnt[ei] += 1
        inst.then_inc(self.sem[ei], 1)
        c = self.cnt[ei]
        for a, b in writes:
            self.lw_eng[a:b] = ei
            self.lw_cnt[a:b] = c
            self.rd[a:b, :] = 0
        for a, b in reads:
            self.rd[a:b, ei] = c
        self.ninst += 1

    def dma(self, q, out, in_, reads=(), writes=(), **kw):
        if q == "POOL":
            di = 5 + NDMA_SP + self.dma_rr_sw
            self.dma_rr_sw = (self.dma_rr_sw + 1) % (NDMA - NDMA_SP)
        else:
            di = 5 + self.dma_rr
            self.dma_rr = (self.dma_rr + 1) % NDMA_SP
        rs = [x for x in (self.rng(r) for r in [in_] + list(reads)) if x is not None]
        ws = [x for x in (self.rng(w) for w in [out] + list(writes)) if x is not None]
        deps = self._deps(rs, ws)
        deps[di] = max(deps[di], self.cnt[di])
        emb = self._waits(q, deps, embed=True)
        inst = self.eng[q].dma_start(out=out, in_=in_, **kw)
        if emb is not None:
            inst._wait_ge(emb[0], emb[1])
        self.cnt[di] += 1
        inst.then_inc(self.sem[di], 16)
        c = self.cnt[di]
        for a, b in ws:
            self.lw_eng[a:b] = di
            self.lw_cnt[a:b] = c
            self.rd[a:b, :] = 0
        for a, b in rs:
            self.rd[a:b, di] = c
        self.ninst += 1

    def finish(self):
        deps = self.cnt.copy()
        self._waits("SP", deps)


class Arena:
    def __init__(self, nc):
        self.nc = nc
        self.off = SB_BASE
        self.n = 0
        self.peak = SB_BASE

    def alloc(self, shape, dtype, name="t"):
        nbytes = DSZ[dtype]
        for s in shape[1:]:
            nbytes *= s
        off = (self.off + 63) // 64 * 64
        assert off + nbytes <= SB_END, f"SBUF overflow allocating {name} {shape}: {off + nbytes}"
        self.n += 1
        t = self.nc.alloc_sbuf_tensor_at(f"{name}_{self.n}", list(shape), dtype, offset=off)
        self.off = off + nbytes
        self.peak = max(self.peak, self.off)
        return t

    def mark(self):
        return self.off

    def reset(self, m):
        self.off = m


class Ring:
    def __init__(self, bufs):
        self.bufs = bufs
        self.i = 0

    def next(self):
        b = self.bufs[self.i]
        self.i = (self.i + 1) % len(self.bufs)
        return b


class _Stop(Exception):
    pass


def build(T, with_sample=True, nl=NL, stop=None):
    def ck(name):
        if stop == name:
            raise _Stop()
    nc = bass.Bass("TRN2", target_bir_lowering=False)
    NT = 512
    assert T % NT == 0
    n_tiles = T // NT

    def din(name, shape, dt=F32):
        return nc.dram_tensor(name, list(shape), dt, kind="ExternalInput").ap()

    def dout(name, shape, dt=F32):
        return nc.dram_tensor(name, list(shape), dt, kind="ExternalOutput").ap()

    def dscr(name, shape, dt=BF16):
        return nc.dram_tensor(name, list(shape), dt).ap()

    xp = din("xp", [T, D]); xs = din("xs", [32, D]); mem = din("mem", [NMEM, D])
    s5re0 = din("s5re0", [NL, 2, 32, 64]); s5im0 = din("s5im0", [NL, 2, 32, 64])
    ret0 = din("ret0", [NL, 2, 4, 128, 128])
    cmk = din("cmk", [NL, 2, NMEM, D]); cmv = din("cmv", [NL, 2, NMEM, D])
    W = {}
    wshapes = dict(w_in=[NL, D, DIN], s5_w_glu=[NL, 512, 512], w_out=[NL, D, D], w_cq=[NL, D, D], w_ck=[NL, D, D],
                   w_cv=[NL, D, D], w_co=[NL, D, D], w_gate=[NL, D, DFF], w_up=[NL, D, DFF], w_down=[NL, DFF, D])
    for k, s in wshapes.items():
        W[k] = din(k, s)
    vshapes = dict(mix_norm_pre=[NL, D], mix_norm_post=[NL, D], s5_lambda_re=[NL, 32, 64], s5_lambda_im=[NL, 32, 64],
                   s5_log_step=[NL, 32], s5_b_re=[NL, 32, 64, 16], s5_b_im=[NL, 32, 64, 16], s5_c_re=[NL, 32, 16, 64],
                   s5_c_im=[NL, 32, 16, 64], s5_d=[NL, 32, 16], s5_b_glu=[NL, 512], s5_out_norm=[NL, 512],
                   ret_out_norm=[NL, 512], xattn_norm_pre=[NL, D], xattn_norm_post=[NL, D], mem_norm=[NL, D],
                   ffn_norm_pre=[NL, D], ffn_norm_post=[NL, D])
    for k, s in vshapes.items():
        W[k] = din(k, s)
    c_idb = din("c_idb", [128, 128], BF16); c_idf = din("c_idf", [128, 128]); c_maskM = din("c_maskM", [128, 128])
    c_misc = din("c_misc", [128, 8])
    c_ropeP = din("c_ropeP", [T, 256]); c_ropeS = din("c_ropeS", [32, 256])
    c_dmP = din("c_dmP", [128, 512]); c_kdP = din("c_kdP", [128, 8]); c_qdP = din("c_qdP", [1, 1024])
    c_dmS = din("c_dmS", [32, 128]); c_kdS = din("c_kdS", [32, 8]); c_qdS = din("c_qdS", [1, 256])
    c_gP = din("c_gP", [1, 512]); c_gS = din("c_gS", [1, 512])

    yp = dout("yp", [T, D]); ys = dout("ys", [32, D])
    o_s5re_p = dout("o_s5re_p", [NL, 32 * 64]); o_s5im_p = dout("o_s5im_p", [NL, 32 * 64])
    o_ret_p = dout("o_ret_p", [NL, 4, 128, 128])
    o_memk = dout("o_memk", [NL, NMEM, D]); o_memv = dout("o_memv", [NL, NMEM, D])
    o_s5re_s = dout("o_s5re_s", [NL, 2, 32 * 64]); o_s5im_s = dout("o_s5im_s", [NL, 2, 32 * 64])
    o_ret_s = dout("o_ret_s", [NL, 2, 4, 128, 128])

    WB = {k: dscr(k + "_b", s) for k, s in wshapes.items()}
    SLABW = {"w_in": 5, "w_gate": 6, "w_up": 6}
    for k_, ns_ in SLABW.items():
        WB[k_] = dscr(k_ + "_sb", [NL, ns_, 128, 4096])

    for k_ in ("w_out", "w_cq", "w_ck", "w_cv", "w_co"):
        WB[k_] = dscr(k_ + "_sb", [NL, 128, 8192])
    WB["w_down"] = dscr("w_down_sb", [NL, 6, 128, 4096])
    WB["s5_w_glu"] = dscr("s5_w_glu_sb", [NL, 128, 2048])

    def sq_src(name, l):
        return WB[name][l].rearrange("p (k n) -> p k n", n=1024)

    def down_src(l, fbi, nf):
        return WB["w_down"][l, fbi][:, 0:nf * 1024].rearrange("p (f n) -> p f n", n=1024)

    def glu_src(l):
        return WB["s5_w_glu"][l].rearrange("p (k n) -> p k n", n=512)

    def slab_src(name, l, cb, wc=512):
        return WB[name][l, cb][:, 0:8 * wc].rearrange("p (k n) -> p k n", n=wc)
    s5wV = dscr("s5wV", [NL, 128, 32 * 2 * 128]); s5wY = dscr("s5wY", [NL, 128, 32 * 3 * 128])
    rotw = dscr("rotw", [NL, 128, 16 * 2 * 64], F32)
    kvs = dscr("kvs", [NL, 128, 4096])

    ps = nc.alloc_psum_tensor("ps", [128, 8, 512], F32)
    sem_cm = [nc.semaphore(f"s{i}") for i in range(5 + NDMA)]
    sems = [s.__enter__() for s in sem_cm]
    S = Sched(nc, sems)
    A = Arena(nc)

    def mm(out, lhsT, rhs, start=True, stop=True):
        S.op("PE", lambda: nc.tensor.matmul(out, lhsT=lhsT, rhs=rhs, start=start, stop=stop),
             [lhsT, rhs] + ([] if start else [out]), [out])

    def tr(out, in_, ident):
        S.op("PE", lambda: nc.tensor.transpose(out, in_, ident), [in_, ident], [out])

    def act(out, in_, func, bias=None, scale=None, accum=None):
        kw = {}
        rd = [in_]
        wr = [out]
        if bias is not None:
            kw["bias"] = bias; rd.append(bias)
        if scale is not None:
            kw["scale"] = scale
            if not isinstance(scale, float):
                rd.append(scale)
        if accum is not None:
            kw["accum_out"] = accum; wr.append(accum)
        S.op("ACT", lambda: nc.scalar.activation(out=out, in_=in_, func=func, **kw), rd, wr)

    def E(e):
        return {"DVE": nc.vector, "POOL": nc.gpsimd, "ACT": nc.scalar}[e]

    def tt(e, out, in0, in1, op):
        S.op(e, lambda: E(e).tensor_tensor(out=out, in0=in0, in1=in1, op=op), [in0, in1], [out])

    def ts(e, out, in0, s1, s2, op0, op1=None):
        rd = [in0] + [s for s in (s1, s2) if s is not None and not isinstance(s, float)]
        if op1 is None:
            S.op(e, lambda: E(e).tensor_scalar(out=out, in0=in0, scalar1=s1, scalar2=None, op0=op0), rd, [out])
        else:
            S.op(e, lambda: E(e).tensor_scalar(out=out, in0=in0, scalar1=s1, scalar2=s2, op0=op0, op1=op1), rd, [out])

    def stt(out, in0, scalar, in1, op0, op1, accum=None):
        rd = [in0, in1] + ([] if isinstance(scalar, float) else [scalar])
        wr = [out] + ([accum] if accum is not None else [])
        kw = {"accum_out": accum} if accum is not None else {}
        S.op("DVE", lambda: nc.vector.scalar_tensor_tensor(out=out, in0=in0, scalar=scalar, in1=in1, op0=op0, op1=op1, **kw),
             rd, wr)

    def cp(e, out, in_):
        if e == "ACT":
            S.op(e, lambda: nc.scalar.copy(out=out, in_=in_), [in_], [out])
        else:
            S.op(e, lambda: E(e).tensor_copy(out=out, in_=in_), [in_], [out])

    def recip(out, in_):
        S.op("DVE", lambda: nc.vector.reciprocal(out=out, in_=in_), [in_], [out])

    def memset(e, ap, v):
        S.op(e, lambda: E(e).memset(ap, v), [], [ap])

    def dma(out, in_, q="SP", reads=(), writes=(), slow=False):
        kw = {"allow_slow_non_contiguous": True} if slow else {}
        S.dma(q, out, in_, reads=reads, writes=writes, **kw)

    bank_rr = [0]

    def bank(n=1):
        b = (bank_rr[0] + n - 1) // n * n
        if b + n > 8:
            b = 0
        bank_rr[0] = (b + n) % 8
        return b

    def psf(b, P=128, n=512, nb=1):
        if nb == 1:
            return ps[0:P, b, 0:n]
        return ps[0:P, b:b + nb, :].rearrange("p b n -> p (b n)")[:, 0:n]

    def psb(b, P=128):
        return ps[0:P, b, :].bitcast(BF16)

    evac_rr = [0]

    def evac(out, in_):
        evac_rr[0] ^= 1
        cp("ACT" if evac_rr[0] else "DVE", out, in_)

    idb = A.alloc([128, 128], BF16, "idb"); idf = A.alloc([128, 128], F32, "idf")
    maskM = A.alloc([128, 128], F32, "maskM"); misc = A.alloc([128, 8], F32, "misc")
    onesb = A.alloc([128, 128], BF16, "onesb")
    dmP = A.alloc([128, 512], F32, "dmP"); kdP = A.alloc([128, 8], F32, "kdP"); qdP = A.alloc([128, 1024], F32, "qdP")
    dmS = A.alloc([128, 128], F32, "dmS"); kdS = A.alloc([128, 8], F32, "kdS"); qdS = A.alloc([128, 256], F32, "qdS")
    gP = A.alloc([128, 512], F32, "gP"); gS = A.alloc([128, 512], F32, "gS")
    rope = A.alloc([128, 4, 256], F32, "rope")
    x_sb = A.alloc([128, 4, D], F32, "x")
    hT = A.alloc([128, 8, NT], BF16, "hT")
    featA = A.alloc([128, 8, NT], BF16, "featA")
    featB = A.alloc([128, 8, NT], BF16, "featB")
    Sst = A.alloc([128, NL, 512], F32, "Sst")
    Ssm = A.alloc([128, 2, 512], F32, "Ssm")
    Sbf = Ring([A.alloc([128, 512], BF16, "Sbf") for _ in range(3)])
    car = A.alloc([128, NL, 2, 16], F32, "car")
    cars = A.alloc([128, 2, 2, 16], F32, "cars")
    Rall = A.alloc([128, NL, 16], F32, "Rall")
    gring = Ring([A.alloc([128, D], F32, "g") for _ in range(2)])
    tmpf = Ring([A.alloc([128, D], F32, "tmpf") for _ in range(2)])
    junk = A.alloc([128, D], BF16, "junk")
    hbr = Ring([A.alloc([128, D], BF16, "hb") for _ in range(4)])
    ssr = Ring([A.alloc([128, 8], F32, "ss") for _ in range(4)])
    ring_mark = A.mark()
    NSLOT = 4
    wring = Ring([A.alloc([128, 8192], BF16, "wr") for _ in range(NSLOT)])
    phase_mark = A.mark()

    pm = lambda g2: misc[:, g2:g2 + 1]
    npm = lambda g2: misc[:, 2 + g2:3 + g2]
    eps_c = misc[:, 4:5]
    hpi_c = misc[:, 5:6]

    dma(idb[:], c_idb); dma(idf[:], c_idf); dma(maskM[:], c_maskM); dma(misc[:], c_misc)
    dma(dmP[:], c_dmP); dma(kdP[:], c_kdP); dma(qdP[:], c_qdP.partition_broadcast(128))
    dma(dmS[0:32, :], c_dmS); dma(kdS[0:32, :], c_kdS); dma(qdS[:], c_qdS.partition_broadcast(128))
    dma(gP[:], c_gP.partition_broadcast(128)); dma(gS[:], c_gS.partition_broadcast(128))
    memset("DVE", onesb[:], 1.0)
    memset("DVE", Sst[:], 0.0)
    memset("DVE", car[:], 0.0)

    for l in range(nl):
        for k in wshapes:
            if k in SLABW:
                ncols = wshapes[k][2]
                for cb in range(SLABW[k]):
                    wc = min(512, ncols - cb * 512)
                    S.dma("POOL", slab_src(k, l, cb, wc), W[k][l][:, cb * 512:cb * 512 + wc].rearrange("(k p) n -> p k n", p=128),
                          writes=[S.dres(f"{k}{l}")])
            elif k == "w_down":
                for fbi, fb in enumerate(range(0, 22, 4)):
                    nf = min(4, 22 - fb)
                    S.dma("POOL", down_src(l, fbi, nf), W[k][l][fb * 128:(fb + nf) * 128, :].rearrange("(f p) n -> p f n", p=128),
                          writes=[S.dres(f"{k}{l}")])
            elif k == "s5_w_glu":
                S.dma("POOL", glu_src(l), W[k][l].rearrange("(k p) n -> p k n", p=128), writes=[S.dres(f"{k}{l}")])
            else:
                S.dma("POOL", sq_src(k, l), W[k][l].rearrange("(k p) n -> p k n", p=128), writes=[S.dres(f"{k}{l}")])

    def wload(src, shape, res, q="SP"):
        slot = wring.next()
        n = 1
        for s in shape[1:]:
            n *= s
        v = slot[:, 0:n]
        if len(shape) == 3:
            v = v.rearrange("p (a b) -> p a b", b=shape[2])
        elif len(shape) == 4:
            v = v.rearrange("p (a b c) -> p a b c", b=shape[2], c=shape[3])
        dma(v, src, q=q, reads=[S.dres(r) for r in res])
        return v

    def gload(row):
        g = gring.next()
        n = row.shape[-1]
        dma(g[:, 0:n], row.partition_broadcast(128))
        return g

    def s5_setup(l):
        A.reset(ring_mark)
        al = lambda shape, dt=F32, nm="s": A.alloc(shape, dt, nm)
        LR = al([128, 16]); LI = al([128, 16]); LS = al([128, 16])
        lam_v = lambda a: a[l].rearrange("(j g2) p -> (g2 p) j", g2=2)
        dma(LR[:], lam_v(W["s5_lambda_re"]), slow=True)
        dma(LI[:], lam_v(W["s5_lambda_im"]), slow=True)
        lsv = W["s5_log_step"][l].rearrange("(j g2) -> g2 j", g2=2)
        dma(LS[0:64, :], lsv[0:1, :].partition_broadcast(64), slow=True)
        dma(LS[64:128, :], lsv[1:2, :].partition_broadcast(64), slow=True)
        dt_ = al([128, 16]); ar = al([128, 16]); ai = al([128, 16]); mag = al([128, 16])
        act(dt_[:], LS[:], AF.Exp)
        tt("DVE", ar[:], LR[:], dt_[:], ALU.mult)
        tt("DVE", ai[:], LI[:], dt_[:], ALU.mult)
        act(mag[:], ar[:], AF.Exp)
        cs = al([128, 16]); sn = al([128, 16]); t1 = al([128, 16]); t2 = al([128, 16])
        act(sn[:], ai[:], AF.Sin, scale=1.0 / 16)
        act(cs[:], ai[:], AF.Sin, bias=hpi_c, scale=1.0 / 16)
        for _ in range(4):
            tt("DVE", t1[:], cs[:], cs[:], ALU.mult)
            tt("DVE", t2[:], sn[:], sn[:], ALU.mult)
            stt(sn[:], cs[:], 2.0, sn[:], ALU.mult, ALU.mult)
            tt("DVE", cs[:], t1[:], t2[:], ALU.subtract)
        apw = al([128, 9, 2, 16]); aiv = al([128, 9, 2, 16])
        memset("DVE", apw[:, 0, 0, :], 1.0); memset("DVE", apw[:, 0, 1, :], 0.0)
        tt("DVE", apw[:, 1, 0, :], mag[:], cs[:], ALU.mult)
        tt("DVE", apw[:, 1, 1, :], mag[:], sn[:], ALU.mult)

        def cmul(o_re, o_im, a_re, a_im, b_re, b_im, tA, tB):
            tt("DVE", tA, a_re, b_re, ALU.mult)
            tt("DVE", tB, a_im, b_im, ALU.mult)
            tt("DVE", o_re, tA, tB, ALU.subtract)
            tt("DVE", tA, a_re, b_im, ALU.mult)
            tt("DVE", tB, a_im, b_re, ALU.mult)
            tt("DVE", o_im, tA, tB, ALU.add)

        for n in range(2, 9):
            cmul(apw[:, n, 0, :], apw[:, n, 1, :], apw[:, n - 1, 0, :], apw[:, n - 1, 1, :],
                 apw[:, 1, 0, :], apw[:, 1, 1, :], t1[:], t2[:])
        m2 = al([128, 16]); rm2 = al([128, 16])
        tt("DVE", m2[:], mag[:], mag[:], ALU.mult)
        recip(rm2[:], m2[:])
        tt("DVE", aiv[:, 1, 0, :], apw[:, 1, 0, :], rm2[:], ALU.mult)
        stt(aiv[:, 1, 1, :], apw[:, 1, 1, :], -1.0, rm2[:], ALU.mult, ALU.mult)
        for n in range(2, 9):
            cmul(aiv[:, n, 0, :], aiv[:, n, 1, :], aiv[:, n - 1, 0, :], aiv[:, n - 1, 1, :],
                 aiv[:, 1, 0, :], aiv[:, 1, 1, :], t1[:], t2[:])
        nr = al([128, 16]); den = al([128, 16]); fre = al([128, 16]); fim = al([128, 16])
        ts("DVE", nr[:], apw[:, 1, 0, :], -1.0, None, ALU.add)
        tt("DVE", t1[:], LR[:], LR[:], ALU.mult)
        tt("DVE", t2[:], LI[:], LI[:], ALU.mult)
        tt("DVE", den[:], t1[:], t2[:], ALU.add)
        recip(den[:], den[:])
        tt("DVE", t1[:], nr[:], LR[:], ALU.mult)
        tt("DVE", t2[:], apw[:, 1, 1, :], LI[:], ALU.mult)
        tt("DVE", t1[:], t1[:], t2[:], ALU.add)
        tt("DVE", fre[:], t1[:], den[:], ALU.mult)
        tt("DVE", t1[:], apw[:, 1, 1, :], LR[:], ALU.mult)
        tt("DVE", t2[:], nr[:], LI[:], ALU.mult)
        tt("DVE", t1[:], t1[:], t2[:], ALU.subtract)
        tt("DVE", fim[:], t1[:], den[:], ALU.mult)
        R = Rall[:, l, :]
        tt("DVE", t1[:], m2[:], m2[:], ALU.mult)
        tt("DVE", R, t1[:], t1[:], ALU.mult)
        rR = al([128, 16])
        recip(rR[:], R)
        rot = al([128, 16, 2, 64])
        tt("DVE", rot[:, :, 0, 0], apw[:, 8, 0, :], rR[:], ALU.mult)
        tt("DVE", rot[:, :, 1, 0], apw[:, 8, 1, :], rR[:], ALU.mult)
        tr1 = al([128, 16, 32]); tr2 = al([128, 16, 32])
        for k in range(6):
            w = 1 << k
            bre = rot[:, :, 0, w - 1:w].to_broadcast([128, 16, w])
            bim = rot[:, :, 1, w - 1:w].to_broadcast([128, 16, w])
            cmul(rot[:, :, 0, w:2 * w], rot[:, :, 1, w:2 * w], rot[:, :, 0, 0:w], rot[:, :, 1, 0:w], bre, bim,
                 tr1[:, :, 0:w], tr2[:, :, 0:w])
        dma(rotw[l].rearrange("p (j r n) -> p j r n", r=2, n=64), rot[:], writes=[S.dres(f"rot{l}")])
        Bre = al([128, 16, 16]); Bim = al([128, 16, 16])
        bv = lambda a: a[l].rearrange("(j g2) p h -> (g2 p) j h", g2=2)
        dma(Bre[:], bv(W["s5_b_re"]), slow=True)
        dma(Bim[:], bv(W["s5_b_im"]), slow=True)
        Cre = al([128, 16, 16]); Cim = al([128, 16, 16])
        cin = al([128, 128])
        for (src, dst) in ((W["s5_c_re"], Cre), (W["s5_c_im"], Cim)):
            for half in range(2):
                cv = src[l].rearrange("(j g2) h p -> j h g2 p", g2=2)[half * 8:(half + 1) * 8]
                for jl in range(8):
                    dma(cin[jl * 16:(jl + 1) * 16, :].rearrange("h (g2 p) -> h g2 p", p=64), cv[jl], slow=True)
                b = bank()
                tr(psf(b, 128, 128), cin[:], idf[:])
                cp("DVE", dst[:, half * 8:(half + 1) * 8, :], psf(b, 128, 128).rearrange("p (j h) -> p j h", h=16))
        Bbr = al([128, 16, 16]); Bbi = al([128, 16, 16]); u1 = al([128, 16, 16]); u2 = al([128, 16, 16])
        bc = lambda a: a.unsqueeze(2).to_broadcast([128, 16, 16])
        cmul(Bbr[:], Bbi[:], bc(fre[:]), bc(fim[:]), Bre[:], Bim[:], u1[:], u2[:])
        CAr = al([128, 16, 8, 16]); CAi = al([128, 16, 8, 16])
        ABr = al([128, 16, 8, 16]); ABi = al([128, 16, 8, 16])
        Vr = al([128, 16, 8, 16]); Vi = al([128, 16, 8, 16])
        for t in range(8):
            cmul(CAr[:, :, t, :], CAi[:, :, t, :], bc(apw[:, t + 1, 0, :]), bc(apw[:, t + 1, 1, :]), Cre[:], Cim[:], u1[:], u2[:])
            cmul(ABr[:, :, t, :], ABi[:, :, t, :], bc(aiv[:, t + 1, 0, :]), bc(aiv[:, t + 1, 1, :]), Bbr[:], Bbi[:], u1[:], u2[:])
            cmul(Vr[:, :, t, :], Vi[:, :, t, :], bc(apw[:, 7 - t, 0, :]), bc(apw[:, 7 - t, 1, :]), Bbr[:], Bbi[:], u1[:], u2[:])
        dcol = al([128, 32])
        for s in range(8):
            dma(dcol[s * 16:(s + 1) * 16, :], W["s5_d"][l].rearrange("g h -> h g"), slow=True)
        fl = lambda a: a[:].rearrange("p j t h -> p j (t h)")
        SV = Ring([al([128, 8, 2, 128], BF16) for _ in range(2)])
        SY = Ring([al([128, 8, 3, 128], BF16) for _ in range(2)])
        mtmp = Ring([al([128, 128]) for _ in range(20)])
        for g2 in range(2):
            for q4 in range(4):
                sv = SV.next(); sy = SY.next()
                js = [q4 * 4 + jl for jl in range(4)]
                m0 = [mtmp.next() for _ in range(4)]; m1 = [mtmp.next() for _ in range(4)]
                mt = [mtmp.next() for _ in range(4)]; m2_ = [mtmp.next() for _ in range(4)]; m3_ = [mtmp.next() for _ in range(4)]
                for jl, j in enumerate(js):
                    act(m0[jl][:], fl(ABr)[:, j, :], AF.Copy, scale=pm(g2))
                    act(m1[jl][:], fl(ABi)[:, j, :], AF.Copy, scale=npm(g2))
                for jl, j in enumerate(js):
                    ts("DVE", m2_[jl][:], fl(Vr)[:, j, :], pm(g2), None, ALU.mult)
                    ts("DVE", m3_[jl][:], fl(Vi)[:, j, :], pm(g2), None, ALU.mult)
                bM = [bank() for _ in range(4)]
                for jl, j in enumerate(js):
                    mm(psf(bM[jl], 128, 128), m0[jl][:], fl(CAr)[:, j, :], True, False)
                    mm(psf(bM[jl], 128, 128), m1[jl][:], fl(CAi)[:, j, :], False, True)
                bV = [bank() for _ in range(4)]
                for jl, j in enumerate(js):
                    tr(psf(bV[jl], 128, 128), m2_[jl][:], idf[:])
                    tr(psf(bV[jl], 128, 256)[:, 128:256], m3_[jl][:], idf[:])
                for jl, j in enumerate(js):
                    g = 2 * j + g2
                    tt("DVE", mt[jl][:], psf(bM[jl], 128, 128), maskM[:], ALU.mult)
                    stt(sy[:, jl, 0, :], idf[:], dcol[:, g:g + 1], mt[jl][:], ALU.mult, ALU.add)
                for jl, j in enumerate(js):
                    act(sy[:, jl, 1, :], fl(CAr)[:, j, :], AF.Copy, scale=pm(g2))
                    act(sy[:, jl, 2, :], fl(CAi)[:, j, :], AF.Copy, scale=npm(g2))
                for jl, j in enumerate(js):
                    cp("ACT", sv[:, jl, :, :], psf(bV[jl], 128, 256).rearrange("p (r n) -> p r n", n=128))
                vV = s5wV[l].rearrange("p (j g2 r n) -> p j g2 r n", g2=2, r=2, n=128)[:, q4 * 4:q4 * 4 + 4, g2]
                vY = s5wY[l].rearrange("p (j g2 r n) -> p j g2 r n", g2=2, r=3, n=128)[:, q4 * 4:q4 * 4 + 4, g2]
                dma(vV, sv[:, 0:4], writes=[S.dres(f"s5w{l}")])
                dma(vY, sy[:, 0:4], writes=[S.dres(f"s5w{l}")])

    class TC:
        pass

    def norm_T(tc, gain_row, nsub=None, src=None, n_feat=D):
        P = tc.P
        g = gload(gain_row)
        nsub = tc.nsub if nsub is None else nsub
        src = x_sb if src is None else src
        sss = [ssr.next() for _ in range(nsub)]
        hbs = [hbr.next() for _ in range(nsub)]
        for i in range(nsub):
            act(junk[0:P, :], src[0:P, i, :], AF.Square, accum=sss[i][0:P, 0:1])
        for i in range(nsub):
            act(sss[i][0:P, 1:2], sss[i][0:P, 0:1], AF.Sqrt, bias=eps_c[0:P], scale=1.0 / n_feat)
        for i in range(nsub):
            recip(sss[i][0:P, 2:3], sss[i][0:P, 1:2])
        for i in range(nsub):
            stt(hbs[i][0:P, :], src[0:P, i, :], sss[i][0:P, 2:3], g[0:P, :], ALU.mult, ALU.mult)
        bs = [bank() for _ in range(nsub)]
        for i in range(nsub):
            pb = psb(bs[i])
            for k in range(8):
                tr(pb[:, k * P:(k + 1) * P], hbs[i][0:P, k * 128:(k + 1) * 128], idb[0:P, 0:P])
        for i in range(nsub):
            evac(hT[:, :, i * P:(i + 1) * P], psb(bs[i])[:, 0:8 * P].rearrange("p (k c) -> p k c", c=P))

    def rstd_of(ssum_ap, out_ap, P, n):
        S.op("ACT", lambda: nc.scalar.activation(out=out_ap, in_=ssum_ap, func=AF.Sqrt, bias=eps_c[0:P], scale=1.0 / n),
             [ssum_ap, eps_c[0:P]], [out_ap])
        recip(out_ap, out_ap)

    def postnorm_add(tc, i, pair, g):
        P = tc.P
        pv = psf(pair, P, 1024, nb=2)
        ss = ssr.next()
        act(junk[0:P, :], pv, AF.Square, accum=ss[0:P, 0:1])
        rstd_of(ss[0:P, 0:1], ss[0:P, 1:2], P, D)
        tf = tmpf.next()
        stt(tf[0:P, :], pv, ss[0:P, 1:2], g[0:P, :], ALU.mult, ALU.mult)
        tt("DVE", x_sb[0:P, i, :], x_sb[0:P, i, :], tf[0:P, :], ALU.add)

    def proj_out_tm(tc, l, wname, featT, gain_row):
        P = tc.P
        wv = wload(sq_src(wname, l), [128, 8, D], [f"{wname}{l}"])
        g = gload(gain_row)
        for i in range(tc.nsub):
            pair = bank(2)
            for n in range(2):
                for k in range(8):
                    mm(psf(pair + n, P), featT[:, k, i * P:(i + 1) * P], wv[:, k, n * 512:(n + 1) * 512], k == 0, k == 7)
            postnorm_add(tc, i, pair, g)

    A.reset(phase_mark)
    ucm_raw = A.alloc([128, 4096], BF16, "ucm")
    ucm = ucm_raw[:].rearrange("p (t f) -> p t f", f=512)
    ucmU = ucm_raw[:].rearrange("p (g s h) -> p g s h", s=8, h=16)
    ucmUf = ucm_raw[:].rearrange("p (g n) -> p g n", n=128)
    Uf = A.alloc([128, 32, 64], BF16, "Uf")
    bufA = A.alloc([128, 16, 2, 64], F32, "bufA")
    bufB = A.alloc([128, 16, 2, 64], F32, "bufB")
    bufT = A.alloc([128, 2, 16, 64], F32, "bufT")
    Xpb = A.alloc([128, 16, 2, 64], BF16, "Xpb")
    yfr = Ring([A.alloc([128, 8, 64], BF16, "yf") for _ in range(2)])
    zT = bufT[:].rearrange("p a b c -> p (a b c)").bitcast(BF16)[:, 0:2048].rearrange("p (k s c) -> p k s c", s=8, c=64)
    s5_mark_end = A.mark()
    A.reset(phase_mark)
    q_tm = A.alloc([128, 4, 512], BF16, "q_tm"); k_tm = A.alloc([128, 4, 512], BF16, "k_tm")
    v_tm = A.alloc([128, 4, 512], BF16, "v_tm"); sg_tm = A.alloc([128, 4, 512], BF16, "sg_tm")
    rtm = [A.alloc([128, 256], F32, "rtm") for _ in range(4)]
    qT_sb = A.alloc([128, 4, 128], BF16, "qT_sb"); kT_sb = A.alloc([128, 4, 128], BF16, "kT_sb")
    qdA = A.alloc([128, 4, 128], BF16, "qdA"); qdB = A.alloc([128, 4, 128], BF16, "qdB")
    kdA = A.alloc([128, 4, 128], BF16, "kdA"); kdB = A.alloc([128, 4, 128], BF16, "kdB")
    sc_sb = A.alloc([128, 4, 128], BF16, "sc_sb")
    o_f = A.alloc([128, 4, 128], F32, "o_f"); o_sq = A.alloc([128, 4, 128], F32, "o_sq")
    ret_tmr = Ring([A.alloc([128, 512], BF16, "ret_tm") for _ in range(2)])
    Stmp = A.alloc([128, 512], F32, "Stmp")
    ret_mark_end = A.mark()
    A.reset(phase_mark)
    pTr = Ring([A.alloc([128, 2, 512], BF16, "pT") for _ in range(2)])
    rinv = Ring([A.alloc([128, 512], F32, "rinv") for _ in range(2)])
    kb_x = A.alloc([128, 2, D], BF16, "kb_x")
    kvst = A.alloc([128, 4096], BF16, "kvst")
    kvf = Ring([A.alloc([128, D], F32, "kvf") for _ in range(2)])
    xat_mark_end = A.mark()
    A.reset(phase_mark)
    actT = A.alloc([128, 22, 512], BF16, "actT")
    sgr = Ring([A.alloc([128, 512], BF16, "sgate") for _ in range(2)])
    ffn_mark_end = A.mark()

    def s5_phase(tc, l):
        P, NTt, NC = tc.P, tc.NT, tc.NC
        wv = wload(slab_src("w_in", l, 0), [128, 8, 512], [f"w_in{l}"])
        for s in range(8):
            b = bank()
            for k in range(8):
                mm(psf(b, NC), hT[:, k, s:NTt:8], wv[:, k, :], k == 0, k == 7)
            evac(ucmU[0:NC, :, s, :], psf(b, NC).rearrange("p (g h) -> p g h", h=16))
        for g0 in range(0, 32, 8):
            b = bank()
            pb = psb(b)
            for gl in range(8):
                g = g0 + gl
                tr(pb[:, gl * NC:(gl + 1) * NC], ucmUf[0:NC, g, :], idb[0:NC, 0:NC])
            evac(Uf[:, g0:g0 + 8, 0:NC], pb[:, 0:8 * NC].rearrange("p (g c) -> p g c", c=NC))
        vs = bufA
        for q4 in range(4):
            wV = wload(s5wV[l][:, q4 * 2048:(q4 + 1) * 2048].rearrange("p (g r n) -> p g r n", r=2, n=128),
                       [128, 8, 2, 128], [f"s5w{l}"])
            b = bank()
            pv = psf(b, 128, 8 * NC).rearrange("p (j r c) -> p j r c", r=2, c=NC)
            for jl in range(4):
                for ri in range(2):
                    for g2 in range(2):
                        gl = jl * 2 + g2
                        mm(pv[:, jl, ri, :], wV[:, gl, ri, :], Uf[:, q4 * 8 + gl, 0:NC], g2 == 0, g2 == 1)
            evac(vs[:, q4 * 4:q4 * 4 + 4, :, 0:NC], pv)
        rslot = wring.next()
        rt = rslot[:, 0:4096].bitcast(F32).rearrange("p (j r n) -> p j r n", r=2, n=64)
        for (c0, ln, t0) in tc.rot_segs:
            dma(rt[:, :, :, c0:c0 + ln], rotw[l].rearrange("p (j r n) -> p j r n", r=2, n=64)[:, :, :, t0:t0 + ln],
                reads=[S.dres(f"rot{l}")])
        cosv = rt[:, :, 0, 0:NC]; sinv = rt[:, :, 1, 0:NC]
        vre = vs[:, :, 0, 0:NC]; vim = vs[:, :, 1, 0:NC]
        c_ = bufB
        cre = c_[:, :, 0, 0:NC]; cim = c_[:, :, 1, 0:NC]
        T0 = bufT[:, 0, :, 0:NC]; T1 = bufT[:, 1, :, 0:NC]
        tt("DVE", T0, vre, cosv, ALU.mult)
        tt("DVE", T1, vim, sinv, ALU.mult)
        tt("DVE", cre, T0, T1, ALU.add)
        tt("DVE", T0, vim, cosv, ALU.mult)
        tt("DVE", T1, vre, sinv, ALU.mult)
        tt("DVE", cim, T0, T1, ALU.subtract)
        Wb = bufA
        for j in range(16):
            for ri in range(2):
                for (c0, ln, cin_ap) in tc.scan_segs(l, j, ri):
                    S.op("DVE", lambda o=Wb[:, j, ri, c0:c0 + ln], d1=c_[:, j, ri, c0:c0 + ln], ci=cin_ap:
                         nc.vector.tensor_tensor_scan(out=o, data0=Rall[:, l, j:j + 1].to_broadcast([128, ln]), data1=d1,
                                                      initial=ci, op0=ALU.mult, op1=ALU.add),
                         [c_[:, j, ri, c0:c0 + ln], Rall[:, l, j:j + 1], cin_ap], [Wb[:, j, ri, c0:c0 + ln]])
        wre = Wb[:, :, 0, 0:NC]; wim = Wb[:, :, 1, 0:NC]
        Xn = bufB
        xre = Xn[:, :, 0, 0:NC]; xim = Xn[:, :, 1, 0:NC]
        tt("DVE", T0, wre, cosv, ALU.mult)
        tt("DVE", T1, wim, sinv, ALU.mult)
        tt("DVE", xre, T0, T1, ALU.subtract)
        tt("DVE", T0, wre, sinv, ALU.mult)
        tt("DVE", T1, wim, cosv, ALU.mult)
        tt("DVE", xim, T0, T1, ALU.add)
        tc.scan_finish(l, Xn, Xpb)
        for q4 in range(4):
            wY = wload(s5wY[l][:, q4 * 3072:(q4 + 1) * 3072].rearrange("p (g r n) -> p g r n", r=3, n=128),
                       [128, 8, 3, 128], [f"s5w{l}"])
            b = bank()
            pv = psf(b, 128, 8 * NC).rearrange("p (g c) -> p g c", c=NC)
            for gl in range(8):
                g = q4 * 8 + gl
                j = g // 2
                mm(pv[:, gl, :], wY[:, gl, 0, :], Uf[:, g, 0:NC], True, False)
                mm(pv[:, gl, :], wY[:, gl, 1, :], Xpb[:, j, 0, 0:NC], False, False)
                mm(pv[:, gl, :], wY[:, gl, 2, :], Xpb[:, j, 1, 0:NC], False, True)
            yf = yfr.next()
            evac(yf[:, :, 0:NC], pv)
            b2 = bank()
            pb = psb(b2, NC)
            for gl in range(8):
                tr(pb[:, gl * 128:(gl + 1) * 128], yf[:, gl, 0:NC], idb[:, :])
            act(ucm[0:NC, :, q4 * 128:(q4 + 1) * 128].rearrange("c t (g h) -> c t g h", h=16),
                pb[:, 0:1024].rearrange("c (g t h) -> c t g h", t=8, h=16), AF.Gelu_apprx_tanh)
        for kc in range(4):
            b = bank()
            pb = psb(b)
            for s in range(8):
                tr(pb[:, s * NC:(s + 1) * NC], ucm[0:NC, s, kc * 128:(kc + 1) * 128], idb[0:NC, 0:NC])
            evac(zT[:, kc, :, 0:NC], pb[:, 0:8 * NC].rearrange("p (s c) -> p s c", c=NC))
        wg = wload(glu_src(l), [128, 4, 512], [f"s5_w_glu{l}"])
        bgl = gload(W["s5_b_glu"][l:l + 1, :])
        g5 = gload(W["s5_out_norm"][l:l + 1, :])
        s5o_cm = bufA[:].rearrange("p a b c -> p (a b c)").bitcast(BF16)[:, 0:4096].rearrange("p (s n) -> p s n", n=512)
        gtmp = bufB[:].rearrange("p a b c -> p (a b c)")
        gdump = bufT[:].rearrange("p a b c -> p (a b c)")[0:NC, 1024:1536]
        for s0 in (0, 4):
            bs = [bank() for _ in range(4)]
            gas = [gtmp[0:NC, q * 512:(q + 1) * 512] for q in range(4)]
            sss = [ssr.next() for _ in range(4)]
            for q in range(4):
                for kc in range(4):
                    mm(psf(bs[q], NC), zT[:, kc, s0 + q, 0:NC], wg[:, kc, :], kc == 0, kc == 3)
            for q in range(4):
                tt("DVE", gas[q], psf(bs[q], NC), bgl[0:NC, 0:512], ALU.add)
            for q in range(4):
                act(gas[q], gas[q], AF.Sigmoid)
            for q in range(4):
                tt("DVE", gas[q], gas[q], ucm[0:NC, s0 + q, :], ALU.mult)
            for q in range(4):
                stt(gdump, gas[q], 1.0, gas[q], ALU.mult, ALU.mult, accum=sss[q][0:NC, 0:1])
            for q in range(4):
                S.op("ACT", lambda q=q: nc.scalar.activation(out=sss[q][0:NC, 1:2], in_=sss[q][0:NC, 0:1], func=AF.Sqrt,
                                                              bias=eps_c[0:NC], scale=1.0 / 512),
                     [sss[q][0:NC, 0:1], eps_c[0:NC]], [sss[q][0:NC, 1:2]])
            for q in range(4):
                recip(sss[q][0:NC, 1:2], sss[q][0:NC, 1:2])
            for q in range(4):
                stt(s5o_cm[0:NC, s0 + q, :], gas[q], sss[q][0:NC, 1:2], g5[0:NC, 0:512], ALU.mult, ALU.mult)
        for kc in range(4):
            b = bank()
            pb = psb(b)
            for s in range(8):
                tr(pb[:, s * NC:(s + 1) * NC], s5o_cm[0:NC, s, kc * 128:(kc + 1) * 128], idb[0:NC, 0:NC])
            evac(featA[:, kc, 0:NTt].rearrange("p (c s) -> p s c", s=8), pb[:, 0:8 * NC].rearrange("p (s c) -> p s c", c=NC))

    def ret_phase(tc, l):
        P, NTt = tc.P, tc.NT
        gate_g = gload(W["ret_out_norm"][l:l + 1, :])
        for n in (3, 4, 1, 2):
            wv = wload(slab_src("w_in", l, n), [128, 8, 512], [f"w_in{l}"])
            for i in range(tc.nsub):
                b = bank()
                for k in range(8):
                    mm(psf(b, P), hT[:, k, i * P:(i + 1) * P], wv[:, k, :], k == 0, k == 7)
                pv = psf(b, P)
                if n in (1, 2):
                    dst = (q_tm if n == 1 else k_tm)[0:P, i, :].rearrange("p (h t d) -> p h t d", t=2, d=64)
                    src = pv.rearrange("p (h t d) -> p h t d", t=2, d=64)
                    co = (0 if n == 1 else 2)
                    cosb = rope[0:P, i, co * 64:(co + 1) * 64].unsqueeze(1).to_broadcast([P, 4, 64])
                    sinb = rope[0:P, i, (co + 1) * 64:(co + 2) * 64].unsqueeze(1).to_broadcast([P, 4, 64])
                    r4 = [r[0:P, :].rearrange("p (h d) -> p h d", d=64) for r in rtm]
                    tt("DVE", r4[0], src[:, :, 0, :], cosb, ALU.mult)
                    tt("DVE", r4[1], src[:, :, 1, :], sinb, ALU.mult)
                    tt("DVE", r4[2], src[:, :, 0, :], sinb, ALU.mult)
                    tt("DVE", r4[3], src[:, :, 1, :], cosb, ALU.mult)
                    tt("DVE", dst[:, :, 0, :], r4[0], r4[1], ALU.subtract)
                    tt("DVE", dst[:, :, 1, :], r4[2], r4[3], ALU.add)
                elif n == 3:
                    cp("ACT", v_tm[0:P, i, :], pv)
                else:
                    act(sg_tm[0:P, i, :], pv, AF.Silu)
        dm, kd, qd, gt_ = tc.ret_consts
        pend = [None]

        def ret_final(i, rtm_i):
            b = bank()
            pb = psb(b)
            for h in range(4):
                tr(pb[:, h * P:(h + 1) * P], rtm_i[0:P, h * 128:(h + 1) * 128], idb[0:P, 0:P])
            evac(featA[:, 4:8, i * P:(i + 1) * P], pb[:, 0:4 * P].rearrange("p (h c) -> p h c", c=P))
        for i in range(tc.nsub):
            b = bank()
            pb = psb(b)
            pq = pb[:, 0:4 * P].rearrange("p (h c) -> p h c", c=P)
            pk = pb[:, 4 * P:8 * P].rearrange("p (h c) -> p h c", c=P)
            for h in range(4):
                tr(pq[:, h, :], q_tm[0:P, i, h * 128:(h + 1) * 128], idb[0:P, 0:P])
                tr(pk[:, h, :], k_tm[0:P, i, h * 128:(h + 1) * 128], idb[0:P, 0:P])
            cp("ACT", qT_sb[:, :, 0:P], pq)
            cp("ACT", kT_sb[:, :, 0:P], pk)
            qdv = qd[:, 0:8 * P].rearrange("p (h t c) -> p h t c", t=2, c=P)
            tt("DVE", qdA[:, :, 0:P], pq, qdv[:, :, 0, :], ALU.mult)
            tt("DVE", qdB[:, :, 0:P], pq, qdv[:, :, 1, :], ALU.mult)
            kv4 = k_tm[0:P, i, :].rearrange("p (h d) -> p h d", d=128)
            kdv = kd[0:P, 0:8].rearrange("p (h t) -> p h t", t=2)
            tt("DVE", kdA[0:P], kv4, kdv[:, :, 0:1].to_broadcast([P, 4, 128]), ALU.mult)
            tt("DVE", kdB[0:P], kv4, kdv[:, :, 1:2].to_broadcast([P, 4, 128]), ALU.mult)
            b = bank()
            psc = psf(b, P, 4 * P).rearrange("p (h c) -> p h c", c=P)
            for h in range(4):
                mm(psc[:, h, :], kT_sb[:, h, 0:P], qT_sb[:, h, 0:P])
            tt("DVE", sc_sb[0:P, :, 0:P], psc, dm[0:P, 0:4 * P].rearrange("p (h c) -> p h c", c=P), ALU.mult)
            bA = bank(); bB = bank()
            pkA = psf(bA).rearrange("p (h e) -> p h e", e=128)
            pkB = psf(bB).rearrange("p (h e) -> p h e", e=128)
            for h in range(4):
                mm(pkA[:, h, :], kdA[0:P, h, :], v_tm[0:P, i, h * 128:(h + 1) * 128])
                mm(pkB[:, h, :], kdB[0:P, h, :], v_tm[0:P, i, h * 128:(h + 1) * 128])
            SA_in, SB_in = tc.ret_states(l, i, psf(bA), psf(bB), gt_)
            b = bank()
            po = psf(b, P).rearrange("p (h e) -> p h e", e=128)
            for h in range(4):
                mm(po[:, h, :], sc_sb[0:P, h, 0:P], v_tm[0:P, i, h * 128:(h + 1) * 128], True, False)
                mm(po[:, h, :], qdA[:, h, 0:P], SA_in[:, h * 128:(h + 1) * 128], False, False)
                mm(po[:, h, :], qdB[:, h, 0:P], SB_in[:, h * 128:(h + 1) * 128], False, True)
            cp("ACT", o_f[0:P], po)
            tt("DVE", o_sq[0:P], o_f[0:P], o_f[0:P], ALU.mult)
            ss = ssr.next()
            S.op("DVE", lambda: nc.vector.tensor_reduce(out=ss[0:P, 0:4], in_=o_sq[0:P], axis=AX.X, op=ALU.add),
                 [o_sq[0:P]], [ss[0:P, 0:4]])
            rstd_of(ss[0:P, 0:4], ss[0:P, 4:8], P, 128)
            tt("DVE", o_f[0:P], o_f[0:P], ss[0:P, 4:8].unsqueeze(2).to_broadcast([P, 4, 128]), ALU.mult)
            of2 = o_f[0:P].rearrange("p h e -> p (h e)")
            tt("DVE", of2, of2, gate_g[0:P, 0:512], ALU.mult)
            rtm_i = ret_tmr.next()
            tt("DVE", rtm_i[0:P, :], of2, sg_tm[0:P, i, :], ALU.mult)
            if pend[0] is not None:
                ret_final(*pend[0])
            pend[0] = (i, rtm_i)
        ret_final(*pend[0])

    def mixer(tc, l):
        norm_T(tc, W["mix_norm_pre"][l:l + 1, :])
        s5_phase(tc, l)
        ret_phase(tc, l)
        proj_out_tm(tc, l, "w_out", featA, W["mix_norm_post"][l:l + 1, :])

    def prep_kt(kb, dst):
        for mc in range(2):
            b = bank()
            pb = psb(b)
            for ch in range(8):
                tr(pb[:, ch * 128:(ch + 1) * 128], kb[:, mc, ch * 128:(ch + 1) * 128], idb[:, :])
            evac(dst[:, :, mc * 128:(mc + 1) * 128], pb[:, 0:1024].rearrange("p (k c) -> p k c", c=128))

    def xattn(tc, l):
        P, NTt = tc.P, tc.NT
        norm_T(tc, W["xattn_norm_pre"][l:l + 1, :])
        wv = wload(sq_src("w_cq", l), [128, 8, D], [f"w_cq{l}"])
        for ch in range(8):
            b = bank()
            for k in range(8):
                mm(psf(b, 128, NTt), wv[:, k, ch * 128:(ch + 1) * 128], hT[:, k, 0:NTt], k == 0, k == 7)
            evac(featB[:, ch, 0:NTt], psf(b, 128, NTt))
        for (c0, N, KT, V) in tc.kv_groups(l):
            pts = {}

            def scores(h, c0=c0, N=N, KT=KT):
                pT = pTr.next()
                for mc in range(2):
                    b = bank()
                    for dc in range(2):
                        mm(psf(b, 128, N), KT[:, 2 * h + dc, mc * 128:(mc + 1) * 128], featB[:, 2 * h + dc, c0:c0 + N], dc == 0, dc == 1)
                    act(pT[:, mc, 0:N], psf(b, 128, N), AF.Exp, scale=1.0 / 16)
                pts[h] = pT

            def pv(h, c0=c0, N=N, V=V):
                pT = pts[h]
                b = bank()
                for mc in range(2):
                    mm(psf(b, 128, N), onesb[:, :], pT[:, mc, 0:N], mc == 0, mc == 1)
                ri = rinv.next()
                recip(ri[:, 0:N], psf(b, 128, N))
                for dc in range(2):
                    b = bank()
                    for mc in range(2):
                        mm(psf(b, 128, N), V[:, mc, (2 * h + dc) * 128:(2 * h + dc + 1) * 128], pT[:, mc, 0:N], mc == 0, mc == 1)
                    tt("DVE", featA[:, 2 * h + dc, c0:c0 + N], psf(b, 128, N), ri[:, 0:N], ALU.mult)

            scores(0)
            for h in range(4):
                if h + 1 < 4:
                    scores(h + 1)
                pv(h)
        proj_out_tm(tc, l, "w_co", featA, W["xattn_norm_post"][l:l + 1, :])

    def ffn(tc, l):
        P, NTt = tc.P, tc.NT
        norm_T(tc, W["ffn_norm_pre"][l:l + 1, :])
        for cb in range(6):
            wcols = 512 if cb < 5 else 256
            wg = wload(slab_src("w_gate", l, cb, wcols), [128, 8, wcols], [f"w_gate{l}"])
            wu = wload(slab_src("w_up", l, cb, wcols), [128, 8, wcols], [f"w_up{l}"])
            for f in range(wcols // 128):
                ffc = cb * 4 + f
                bg = bank(); bu = bank()
                for k in range(8):
                    mm(psf(bg, 128, NTt), wg[:, k, f * 128:(f + 1) * 128], hT[:, k, 0:NTt], k == 0, k == 7)
                for k in range(8):
                    mm(psf(bu, 128, NTt), wu[:, k, f * 128:(f + 1) * 128], hT[:, k, 0:NTt], k == 0, k == 7)
                sg = sgr.next()
                act(sg[:, 0:NTt], psf(bg, 128, NTt), AF.Silu)
                tt("DVE", actT[:, ffc, 0:NTt], psf(bu, 128, NTt), sg[:, 0:NTt], ALU.mult)
        g = gload(W["ffn_norm_post"][l:l + 1, :])
        bank_rr[0] = 0
        for fb in range(0, 22, 4):
            nf = min(4, 22 - fb)
            wd = wload(down_src(l, fb // 4, nf), [128, nf, D], [f"w_down{l}"])
            for fl_ in range(nf):
                ffc = fb + fl_
                for i in range(tc.nsub):
                    for n in range(2):
                        mm(psf(2 * i + n, P), actT[:, ffc, i * P:(i + 1) * P], wd[:, fl_, n * 512:(n + 1) * 512], ffc == 0, ffc == 21)
        for i in range(tc.nsub):
            postnorm_add(tc, i, 2 * i, g)
        bank_rr[0] = 0

    def mem_kv():
        tcm = TC(); tcm.P = 128; tcm.nsub = 2; tcm.NT = 256
        dma(x_sb[:, 0:2, :], mem.rearrange("(i p) d -> p i d", p=128))
        for l in range(nl):
            norm_T(tcm, W["mem_norm"][l:l + 1, :])
            ck(f"mk_norm{l}")
            for (wn, outd, isk) in (("w_ck", o_memk, True), ("w_cv", o_memv, False)):
                wv = wload(sq_src(wn, l), [128, 8, D], [f"{wn}{l}"])
                ck(f"mk_w{l}")
                for i in range(2):
                    pair = bank(2)
                    for n in range(2):
                        for k in range(8):
                            mm(psf(pair + n), hT[:, k, i * 128:(i + 1) * 128], wv[:, k, n * 512:(n + 1) * 512], k == 0, k == 7)
                    ck(f"mk_mm{l}")
                    kf = kvf.next()
                    for n in range(2):
                        pv = psf(pair + n)
                        cp("ACT", kf[:, n * 512:(n + 1) * 512], pv)
                        ck(f"mk_cp{l}")
                        if isk:
                            cp("DVE", kb_x[:, i, n * 512:(n + 1) * 512], pv)
                            ck(f"mk_dcp{l}")
                        else:
                            cp("DVE", kvst[:, 2048 + i * 1024 + n * 512:2048 + i * 1024 + (n + 1) * 512], pv)
                    if stop == f"mk_dma{l}" and n == 1:
                        ck(f"mk_dma{l}")
                    dma(outd[l, i * 128:(i + 1) * 128, :], kf[:])
                    ck(f"mk_dmb{l}")
                ck(f"mk_proj{l}{wn}")
                if isk:
                    prep_kt(kb_x, kvst[:, 0:2048].rearrange("p (k m) -> p k m", m=256))
                    ck(f"mk_kt{l}")
            dma(kvs[l], kvst[:], writes=[S.dres(f"kvs{l}")])

    def prompt_tc(ti):
        tc = TC()
        tc.P = 128; tc.nsub = 4; tc.NT = 512; tc.NC = 64
        tc.rot_segs = [(0, 64, 0)]
        tc.ret_consts = (dmP, kdP, qdP, None)
        gch = [float(np.float32(np.exp(np.float32(64.0) * np.log(np.float32(1.0 - 2.0 ** (-5.0 - h)))))) for h in range(4)]

        def scan_segs(l, j, ri):
            return [(0, 64, car[:, l, ri, j:j + 1])]
        tc.scan_segs = scan_segs

        def scan_finish(l, Xn, Xp):
            cp("ACT", Xp[:, :, :, 0:1], car[:, l].rearrange("p r j -> p j r").unsqueeze(3))
            cp("ACT", Xp[:, :, :, 1:64], Xn[:, :, :, 0:63])
            cp("DVE", car[:, l].rearrange("p r j -> p j r").unsqueeze(3), Xn[:, :, :, 63:64])
        tc.scan_finish = scan_finish

        def ret_states(l, i, pkA, pkB, _):
            sA = Sbf.next()
            cp("ACT", sA[:], Sst[:, l, :])
            gp = gP
            tt("DVE", Stmp[:], Sst[:, l, :], gp[:, :], ALU.mult)
            tt("DVE", Sst[:, l, :], Stmp[:], pkA, ALU.add)
            sB = Sbf.next()
            cp("ACT", sB[:], Sst[:, l, :])
            tt("DVE", Stmp[:], Sst[:, l, :], gp[:, :], ALU.mult)
            tt("DVE", Sst[:, l, :], Stmp[:], pkB, ALU.add)
            return sA, sB
        tc.ret_states = ret_states

        def kv_groups(l):
            slot = wload(kvs[l], [128, 4096], [f"kvs{l}"])
            KT = slot[:, 0:2048].rearrange("p (k m) -> p k m", m=256)
            V = slot[:, 2048:4096].rearrange("p (c n) -> p c n", n=1024)
            return [(0, 512, KT, V)]
        tc.kv_groups = kv_groups
        return tc

    def sample_tc():
        tc = TC()
        tc.P = 32; tc.nsub = 1; tc.NT = 32; tc.NC = 4
        tc.rot_segs = [(0, 2, 0), (2, 2, 0)]
        tc.ret_consts = (dmS, kdS, qdS, None)

        def scan_segs(l, j, ri):
            return [(0, 2, cars[:, 0, ri, j:j + 1]), (2, 2, cars[:, 1, ri, j:j + 1])]
        tc.scan_segs = scan_segs

        def scan_finish(l, Xn, Xp):
            for sq_ in range(2):
                cv = cars[:, sq_].rearrange("p r j -> p j r").unsqueeze(3)
                cp("ACT", Xp[:, :, :, 2 * sq_:2 * sq_ + 1], cv)
                cp("ACT", Xp[:, :, :, 2 * sq_ + 1:2 * sq_ + 2], Xn[:, :, :, 2 * sq_:2 * sq_ + 1])
                for ri, od in ((0, o_s5re_s), (1, o_s5im_s)):
                    dma(od[l, sq_].rearrange("(j q) -> q j", q=128), Xn[:, :, ri, 2 * sq_ + 1], slow=True)
        tc.scan_finish = scan_finish

        def ret_states(l, i, pkA, pkB, _):
            outs = []
            for sq_, pk in ((0, pkA), (1, pkB)):
                sb = Sbf.next()
                cp("ACT", sb[:], Ssm[:, sq_, :])
                tt("DVE", Stmp[:], Ssm[:, sq_, :], gS[:, :], ALU.mult)
                tf = tmpf.next()
                tt("DVE", tf[:, 0:512], Stmp[:], pk, ALU.add)
                dma(o_ret_s[l, sq_].rearrange("h d e -> d h e"), tf[:, 0:512].rearrange("p (h e) -> p h e", e=128))
                outs.append(sb)
            return outs[0], outs[1]
        tc.ret_states = ret_states

        def kv_groups(l):
            res = []
            for sq_ in range(2):
                S.dma("POOL", kb_x[:], cmk[l, sq_].rearrange("(c p) d -> p c d", p=128))
                slot = wring.next()
                KT = slot[:, 0:2048].rearrange("p (k m) -> p k m", m=256)
                V = slot[:, 2048:4096].rearrange("p (c n) -> p c n", n=1024)
                prep_kt(kb_x, KT)
                S.dma("POOL", V, cmv[l, sq_].rearrange("(c p) d -> p c d", p=128))
                res.append((sq_ * 16, 16, KT, V))
            return res
        tc.kv_groups = kv_groups
        return tc

    def program():
        ck("pre")
        for l in range(nl if stop is None or not stop.startswith("mk_") else 0):
            s5_setup(l)
            ck(f"setup{l}")
        A.reset(ffn_mark_end)
        mem_kv()
        ck("memkv")

        def run_tile(tc, load_fn, store_fn, pre_layer=None, tag=""):
            load_fn()
            for l in range(nl):
                if pre_layer is not None:
                    pre_layer(l)
                norm_T(tc, W["mix_norm_pre"][l:l + 1, :])
                ck(f"{tag}norm{l}")
                s5_phase(tc, l)
                ck(f"{tag}s5{l}")
                ret_phase(tc, l)
                ck(f"{tag}ret{l}")
                proj_out_tm(tc, l, "w_out", featA, W["mix_norm_post"][l:l + 1, :])
                ck(f"{tag}mix{l}")
                xattn(tc, l)
                ck(f"{tag}xat{l}")
                ffn(tc, l)
                ck(f"{tag}ffn{l}")
            store_fn()

        for ti in range(n_tiles):
            tc = prompt_tc(ti)
            t0 = ti * NT

            def load_fn(t0=t0):
                dma(x_sb[:], xp[t0:t0 + NT, :].rearrange("(i p) d -> p i d", p=128))
                dma(rope[:], c_ropeP[t0:t0 + NT, :].rearrange("(i p) d -> p i d", p=128))

            def store_fn(t0=t0):
                dma(yp[t0:t0 + NT, :].rearrange("(i p) d -> p i d", p=128), x_sb[:])
            run_tile(tc, load_fn, store_fn, tag=f"t{ti}")
        for l in range(nl):
            dma(o_s5re_p[l].rearrange("(j q) -> q j", q=128), car[:, l, 0, :], slow=True)
            dma(o_s5im_p[l].rearrange("(j q) -> q j", q=128), car[:, l, 1, :], slow=True)
            dma(o_ret_p[l].rearrange("h d e -> d h e"), Sst[:, l, :].rearrange("p (h e) -> p h e", e=128))
        ck("prompt")
        if with_sample:
            tc = sample_tc()

            def load_s():
                dma(x_sb[0:32, 0, :], xs)
                dma(rope[0:32, 0, :], c_ropeS)

            def store_s():
                dma(ys, x_sb[0:32, 0, :])

            def pre_layer(l):
                for sq_ in range(2):
                    dma(Ssm[:, sq_, :].rearrange("p (h e) -> p h e", e=128), ret0[l, sq_].rearrange("h d e -> d h e"))
                    dma(cars[:, sq_, 0, :], s5re0[l, sq_].rearrange("(j g2) p -> (g2 p) j", g2=2), slow=True)
                    dma(cars[:, sq_, 1, :], s5im0[l, sq_].rearrange("(j g2) p -> (g2 p) j", g2=2), slow=True)
            run_tile(tc, load_s, store_s, pre_layer, tag="s")

    try:
        program()
    except _Stop:
        pass
    if stop is not None:
        dma(yp[0:NT, :].rearrange("(i p) d -> p i d", p=128), x_sb[:])
    S.finish()
    return nc, S, A


def _consts(T):
    bf = ml_dtypes.bfloat16
    c = {}
    c["c_idb"] = np.eye(128, dtype=np.float32).astype(bf)
    c["c_idf"] = np.eye(128, dtype=np.float32)
    s_idx = np.arange(128) // 16
    c["c_maskM"] = (s_idx[None, :] >= s_idx[:, None]).astype(np.float32)
    misc = np.zeros((128, 8), np.float32)
    misc[:64, 0] = 1; misc[64:, 1] = 1; misc[:64, 2] = -1; misc[64:, 3] = -1
    misc[:, 4] = EPS; misc[:, 5] = np.pi / 2; misc[:, 6] = 1.0
    c["c_misc"] = misc
    half = 64
    inv = (np.float32(10000.0) ** (-np.arange(half, dtype=np.float32) / np.float32(half))).astype(np.float32)

    def rope_tab(pos):
        ang = pos.astype(np.float32)[:, None] * inv[None, :]
        cs, sn = np.cos(ang).astype(np.float32), np.sin(ang).astype(np.float32)
        qs = np.float32(128.0 ** -0.5)
        return np.concatenate([cs * qs, sn * qs, cs, sn], axis=1).astype(np.float32)
    c["c_ropeP"] = rope_tab(np.arange(T))
    c["c_ropeS"] = rope_tab(np.concatenate([1024 + np.arange(16), 1024 + np.arange(16)]))
    lg = np.log((1.0 - 2.0 ** (-5.0 - np.arange(4, dtype=np.float32))).astype(np.float32)).astype(np.float32)

    def ret_tabs(P, C):
        idx = np.arange(P)
        ch = idx // C
        pos = (idx % C).astype(np.float32)
        dm = np.zeros((P, 4, P), np.float32)
        kd = np.zeros((P, 4, 2), np.float32)
        qd = np.zeros((4, 2, P), np.float32)
        for h in range(4):
            d = np.exp(lg[h] * np.abs(pos[:, None] - pos[None, :])).astype(np.float32)
            dm[:, h, :] = d * (ch[:, None] == ch[None, :])
            kdec = np.exp(lg[h] * (C - 1.0 - pos)).astype(np.float32)
            qdec = np.exp(lg[h] * (pos + 1.0)).astype(np.float32)
            for t in range(2):
                kd[:, h, t] = kdec * (ch == t)
                qd[h, t, :] = qdec * (ch == t)
        g = np.exp(lg * np.float32(C)).astype(np.float32)
        gt = np.repeat(g, 128)[None, :].astype(np.float32)
        return dm.reshape(P, 4 * P), kd.reshape(P, 8), qd.reshape(1, 8 * P), gt
    c["c_dmP"], c["c_kdP"], c["c_qdP"], c["c_gP"] = ret_tabs(128, 64)
    c["c_dmS"], c["c_kdS"], c["c_qdS"], c["c_gS"] = ret_tabs(32, 16)
    return c


_CACHE = {}


def _get(T, with_sample=True, nl=NL, stop=None):
    key = (T, with_sample, nl, stop)
    if key not in _CACHE:
        _CACHE[key] = build(T, with_sample, nl, stop)
    return _CACHE[key]


def run_cores(inputs, T, n_cores=8, with_sample=True, nl=NL, stop=None, trace=False):
    nc, S, A = _get(T, with_sample, nl, stop)
    consts = _consts(T)
    f = lambda a: np.ascontiguousarray(np.asarray(a, dtype=np.float32))
    in_maps = []
    wnames = ["w_in", "s5_w_glu", "w_out", "w_cq", "w_ck", "w_cv", "w_co", "w_gate", "w_up", "w_down",
              "mix_norm_pre", "mix_norm_post", "s5_lambda_re", "s5_lambda_im", "s5_log_step", "s5_b_re", "s5_b_im",
              "s5_c_re", "s5_c_im", "s5_d", "s5_b_glu", "s5_out_norm", "xattn_norm_pre", "xattn_norm_post", "mem_norm",
              "ffn_norm_pre", "ffn_norm_post"]
    shared = {k: f(inputs[k]) for k in wnames}
    shared["ret_out_norm"] = f(inputs["ret_out_norm"]).reshape(NL, 512)
    shared.update(consts)
    for c in range(n_cores):
        b = c % 4
        m = dict(shared)
        m["xp"] = f(inputs["x_prompt"][b, :T])
        m["xs"] = f(inputs["x_sample"][2 * c:2 * c + 2]).reshape(32, D)
        m["mem"] = f(inputs["mem_prompt"][b])
        m["s5re0"] = f(inputs["state_s5_re"][:, 2 * c:2 * c + 2])
        m["s5im0"] = f(inputs["state_s5_im"][:, 2 * c:2 * c + 2])
        m["ret0"] = f(inputs["state_ret"][:, 2 * c:2 * c + 2])
        m["cmk"] = f(inputs["cache_mem_k"][:, 2 * c:2 * c + 2]).reshape(NL, 2, NMEM, D)
        m["cmv"] = f(inputs["cache_mem_v"][:, 2 * c:2 * c + 2]).reshape(NL, 2, NMEM, D)
        in_maps.append(m)
    if trace:
        res = run_bass_kernel_spmd(nc, in_maps, core_ids=list(range(n_cores)), trace=True)
        print("EXEC_NS", res.exec_time_ns, flush=True)
    else:
        res = run_bass_kernel_spmd(nc, in_maps, core_ids=list(range(n_cores)))
    return res.results


def kernel(**inputs):
    T = 8192
    r = run_cores(inputs, T)
    y_prompt = np.stack([r[b]["yp"] for b in range(4)]).astype(np.float32)
    y_sample = np.concatenate([r[c]["ys"].reshape(2, 16, D) for c in range(8)]).astype(np.float32)
    s5re_p = np.stack([r[b]["o_s5re_p"].reshape(NL, 32, 64) for b in range(4)], axis=1)
    s5im_p = np.stack([r[b]["o_s5im_p"].reshape(NL, 32, 64) for b in range(4)], axis=1)
    ret_p = np.stack([r[b]["o_ret_p"] for b in range(4)], axis=1)
    memk = np.stack([r[b]["o_memk"].reshape(NL, NMEM, 4, 256) for b in range(4)], axis=1)
    memv = np.stack([r[b]["o_memv"].reshape(NL, NMEM, 4, 256) for b in range(4)], axis=1)
    s5re_s = np.concatenate([r[c]["o_s5re_s"].reshape(NL, 2, 32, 64) for c in range(8)], axis=1)
    s5im_s = np.concatenate([r[c]["o_s5im_s"].reshape(NL, 2, 32, 64) for c in range(8)], axis=1)
    ret_s = np.concatenate([r[c]["o_ret_s"] for c in range(8)], axis=1)
    return (y_prompt, y_sample, s5re_p.astype(np.float32), s5im_p.astype(np.float32), ret_p.astype(np.float32),
            memk.astype(np.float32), memv.astype(np.float32), s5re_s.astype(np.float32), s5im_s.astype(np.float32),
            ret_s.astype(np.float32))
```
